# Optimizing a Trainium2 kernel written in Bass

```python
import math
import jax
import jax.numpy as jnp
from jax import lax
import numpy as np

D_MODEL = 2048
BATCH = 4
SEQ = 2048
DEPTH = 2

NORM_EPS = 1e-6
MLSTM_HEADS = 8
MLSTM_DQK = D_MODEL // 16
MLSTM_DV = D_MODEL // 8
MLSTM_CHUNK = 64
SSD_D_INNER = D_MODEL
SSD_HEADDIM = 64
SSD_HEADS = SSD_D_INNER // SSD_HEADDIM
SSD_STATE = 128
SSD_GROUPS = 8
SSD_CONV = 4
SSD_CHUNK = 128
ATTN_HEADS = 16
ATTN_HEAD_DIM = D_MODEL // ATTN_HEADS
MOBA_BLOCK = 256
MOBA_TOPK = 3
MOBA_Q_CHUNK = 16
ROPE_THETA = 10000.0
D_FF = (8 * D_MODEL + 3 * 256 - 1) // (3 * 256) * 256

MLSTM_QK_W = MLSTM_HEADS * MLSTM_DQK
MLSTM_V_W = MLSTM_HEADS * MLSTM_DV
SSD_BC_W = SSD_GROUPS * SSD_STATE
SSD_XBC_W = SSD_D_INNER + 2 * SSD_BC_W
IN_SPLITS = (MLSTM_QK_W, MLSTM_QK_W, MLSTM_V_W, MLSTM_V_W, MLSTM_HEADS, MLSTM_HEADS,
             SSD_D_INNER, SSD_XBC_W, SSD_HEADS)
IN_COLS = sum(IN_SPLITS)
MIX_W = MLSTM_V_W + SSD_D_INNER

kernel_name = 'hybrid_mlstm_ssd_moba_block'


def rms_norm(x, g):
    xf = x.astype(jnp.float32)
    y = xf * lax.rsqrt(jnp.mean(xf * xf, axis=-1, keepdims=True) + NORM_EPS)
    return (y * g.astype(jnp.float32)).astype(x.dtype)


def swiglu(x, w_gate, w_up, w_down):
    return (jax.nn.silu(x @ w_gate) * (x @ w_up)) @ w_down


def rope(x, pos):
    half = x.shape[-1] // 2
    inv = ROPE_THETA ** (-jnp.arange(half, dtype=jnp.float32) / half)
    ang = pos.astype(jnp.float32)[:, None] * inv[None, :]
    cos, sin = jnp.cos(ang), jnp.sin(ang)
    x1, x2 = x[..., :half], x[..., half:]
    return jnp.concatenate([x1 * cos - x2 * sin, x2 * cos + x1 * sin], axis=-1)


def mlstm_chunkwise(q, k, v, i_pre, f_pre):
    b, h, s, dk = q.shape
    dv = v.shape[-1]
    L = MLSTM_CHUNK
    nc = s // L
    qc = (q * dk ** -0.5).reshape(b, h, nc, L, dk)
    kc = k.reshape(b, h, nc, L, dk)
    vc = v.reshape(b, h, nc, L, dv)
    ic = i_pre.reshape(b, h, nc, L)
    logf = jax.nn.log_sigmoid(f_pre).reshape(b, h, nc, L)
    bcum = jnp.cumsum(logf, axis=-1)
    causal = jnp.tril(jnp.ones((L, L), dtype=bool))
    log_d = jnp.where(causal, bcum[..., :, None] - bcum[..., None, :] + ic[..., None, :], -jnp.inf)
    log_w_end = bcum[..., -1:] - bcum + ic

    def step(carry, xs):
        C, n, m = carry
        k_c, v_c, lw, btot = xs
        m_new = jnp.maximum(btot + m, jnp.max(lw, axis=-1))
        decay = jnp.exp(btot + m - m_new)
        w = jnp.exp(lw - m_new[..., None])
        C_new = decay[..., None, None] * C + jnp.einsum('bhl,bhld,bhle->bhde', w, k_c, v_c)
        n_new = decay[..., None] * n + jnp.einsum('bhl,bhld->bhd', w, k_c)
        return (C_new, n_new, m_new), (C, n, m)

    init = (jnp.zeros((b, h, dk, dv), jnp.float32), jnp.zeros((b, h, dk), jnp.float32),
            jnp.zeros((b, h), jnp.float32))
    xs = (jnp.moveaxis(kc, 2, 0), jnp.moveaxis(vc, 2, 0), jnp.moveaxis(log_w_end, 2, 0),
          jnp.moveaxis(bcum[..., -1], 2, 0))
    _, (C_prev, n_prev, m_prev) = lax.scan(step, init, xs)
    C_prev = jnp.moveaxis(C_prev, 0, 2)
    n_prev = jnp.moveaxis(n_prev, 0, 2)
    m_prev = jnp.moveaxis(m_prev, 0, 2)

    log_inter = bcum + m_prev[..., None]
    m_row = jnp.maximum(log_inter, jnp.max(log_d, axis=-1))
    dmat = jnp.exp(log_d - m_row[..., None])
    scores = jnp.einsum('bhcld,bhcsd->bhcls', qc, kc) * dmat
    inter = jnp.exp(log_inter - m_row)
    num = (jnp.einsum('bhcls,bhcse->bhcle', scores, vc)
           + inter[..., None] * jnp.einsum('bhcld,bhcde->bhcle', qc, C_prev))
    nq = jnp.sum(scores, axis=-1) + inter * jnp.einsum('bhcld,bhcd->bhcl', qc, n_prev)
    denom = jnp.maximum(jnp.abs(nq), jnp.exp(-m_row))
    return (num / denom[..., None]).reshape(b, h, s, dv)


def segsum(a):
    L = a.shape[-1]
    cs = jnp.cumsum(a, axis=-1)
    mask = jnp.tril(jnp.ones((L, L), dtype=bool))
    return jnp.where(mask, cs[..., :, None] - cs[..., None, :], -jnp.inf)


def ssd_chunked(x, dt, a, bm, cm):
    b, s, h, p = x.shape
    g, n = bm.shape[2], bm.shape[3]
    r = h // g
    L = SSD_CHUNK
    nc = s // L
    xd = (x * dt[..., None]).reshape(b, nc, L, g, r, p)
    da = (dt * a).reshape(b, nc, L, g, r).transpose(0, 3, 4, 1, 2)
    bc = bm.reshape(b, nc, L, g, n)
    cc = cm.reshape(b, nc, L, g, n)
    da_cs = jnp.cumsum(da, axis=-1)
    lmat = jnp.exp(segsum(da))
    cb = jnp.einsum('bclgn,bcsgn->bcgls', cc, bc)
    y_diag = jnp.einsum('bcgls,bgrcls,bcsgrp->bclgrp', cb, lmat, xd)
    decay_end = jnp.exp(da_cs[..., -1:] - da_cs)
    chunk_states = jnp.einsum('bcsgn,bgrcs,bcsgrp->bcgrpn', bc, decay_end, xd)
    chunk_decay = jnp.exp(da_cs[..., -1])

    def step(state, xs):
        st_c, dec_c = xs
        return dec_c[..., None, None] * state + st_c, state

    init = jnp.zeros((b, g, r, p, n), jnp.float32)
    _, prev = lax.scan(step, init, (jnp.moveaxis(chunk_states, 1, 0), jnp.moveaxis(chunk_decay, 3, 0)))
    prev = jnp.moveaxis(prev, 0, 1)
    y_off = jnp.einsum('bclgn,bcgrpn,bgrcl->bclgrp', cc, prev, jnp.exp(da_cs))
    return (y_diag + y_off).reshape(b, s, h, p)


def causal_depthwise_conv(x, w, bias):
    kw = w.astype(x.dtype).reshape(SSD_CONV, 1, -1)
    y = lax.conv_general_dilated(x, kw, window_strides=(1,), padding=[(SSD_CONV - 1, 0)],
                                 dimension_numbers=('NWC', 'WIO', 'NWC'),
                                 feature_group_count=x.shape[-1])
    return y + bias.astype(x.dtype)


def mixer_mlstm_ssd(xn, w_in, mlstm_gate_bias, mlstm_norm, ssd_conv_w, ssd_conv_b,
                    ssd_dt_bias, ssd_a_log, ssd_d, ssd_norm, w_out):
    b, s, _ = xn.shape
    f32 = jnp.float32
    proj = (xn @ w_in).astype(f32)
    offs = np.cumsum(IN_SPLITS)[:-1].tolist()
    q, k, v, o, ig, fg, z, xbc, dt = jnp.split(proj, offs, axis=-1)

    def to_heads(t, d):
        return t.reshape(b, s, MLSTM_HEADS, d).transpose(0, 2, 1, 3)
    gb = mlstm_gate_bias.astype(f32)
    i_pre = (ig + gb[:MLSTM_HEADS]).transpose(0, 2, 1)
    f_pre = (fg + gb[MLSTM_HEADS:]).transpose(0, 2, 1)
    hm = mlstm_chunkwise(to_heads(q, MLSTM_DQK), to_heads(k, MLSTM_DQK), to_heads(v, MLSTM_DV), i_pre, f_pre)
    hm = hm.transpose(0, 2, 1, 3)
    hm = rms_norm(hm, mlstm_norm.reshape(MLSTM_HEADS, MLSTM_DV)) * jax.nn.sigmoid(o).reshape(b, s, MLSTM_HEADS, MLSTM_DV)
    hm = hm.reshape(b, s, MLSTM_V_W)

    xbc = jax.nn.silu(causal_depthwise_conv(xbc, ssd_conv_w, ssd_conv_b))
    xs, bm, cm = jnp.split(xbc, [SSD_D_INNER, SSD_D_INNER + SSD_BC_W], axis=-1)
    xs = xs.reshape(b, s, SSD_HEADS, SSD_HEADDIM)
    bm = bm.reshape(b, s, SSD_GROUPS, SSD_STATE)
    cm = cm.reshape(b, s, SSD_GROUPS, SSD_STATE)
    dt = jax.nn.softplus(dt + ssd_dt_bias.astype(f32))
    a = -jnp.exp(ssd_a_log.astype(f32))
    y = ssd_chunked(xs, dt, a, bm, cm) + ssd_d.astype(f32)[:, None] * xs
    y = y.reshape(b, s, SSD_D_INNER) * jax.nn.silu(z)
    y = rms_norm(y.reshape(b, s, SSD_GROUPS, SSD_D_INNER // SSD_GROUPS),
                 ssd_norm.reshape(SSD_GROUPS, SSD_D_INNER // SSD_GROUPS)).reshape(b, s, SSD_D_INNER)

    mixed = jnp.concatenate([hm, y], axis=-1).astype(xn.dtype)
    return mixed @ w_out


def moba_attention(q, k, v):
    b, h, s, hd = q.shape
    nb = -(-s // MOBA_BLOCK)
    pad = nb * MOBA_BLOCK - s
    k_pad = jnp.pad(k, ((0, 0), (0, 0), (0, pad), (0, 0)))
    v_pad = jnp.pad(v, ((0, 0), (0, 0), (0, pad), (0, 0)))
    k_blocks = k_pad.reshape(b, h, nb, MOBA_BLOCK, hd)
    v_blocks = v_pad.reshape(b, h, nb, MOBA_BLOCK, hd)
    scale = hd ** -0.5
    k_mean = jnp.mean(k_blocks, axis=3)
    q_blk = jnp.arange(s) // MOBA_BLOCK
    gate = jnp.einsum('bhsd,bhnd->bhsn', q, k_mean)
    past = jnp.arange(nb)[None, :] < q_blk[:, None]
    gate = jnp.where(past, gate, -jnp.inf)
    topk = min(MOBA_TOPK, nb)
    _, sel = lax.top_k(gate, topk)
    sel_valid = jnp.arange(topk)[None, :] < q_blk[:, None]
    b_idx = jnp.arange(b)[:, None, None, None]
    h_idx = jnp.arange(h)[None, :, None, None]
    QC = MOBA_Q_CHUNK

    def chunk(c):
        start = c * QC
        q_c = lax.dynamic_slice_in_dim(q, start, QC, axis=2)
        sel_c = lax.dynamic_slice_in_dim(sel, start, QC, axis=2)
        valid_c = lax.dynamic_slice_in_dim(sel_valid, start, QC, axis=0)
        own = start // MOBA_BLOCK
        k_own = lax.dynamic_slice_in_dim(k_pad, own * MOBA_BLOCK, MOBA_BLOCK, axis=2)
        v_own = lax.dynamic_slice_in_dim(v_pad, own * MOBA_BLOCK, MOBA_BLOCK, axis=2)
        k_sel = k_blocks[b_idx, h_idx, sel_c]
        v_sel = v_blocks[b_idx, h_idx, sel_c]
        s_sel = jnp.einsum('bhqd,bhqjkd->bhqjk', q_c, k_sel) * scale
        s_sel = jnp.where(valid_c[None, None, :, :, None], s_sel, -jnp.inf)
        s_own = jnp.einsum('bhqd,bhkd->bhqk', q_c, k_own) * scale
        q_pos = start + jnp.arange(QC)
        k_pos = own * MOBA_BLOCK + jnp.arange(MOBA_BLOCK)
        s_own = jnp.where(k_pos[None, :] <= q_pos[:, None], s_own, -jnp.inf)
        logits = jnp.concatenate([s_sel.reshape(b, h, QC, topk * MOBA_BLOCK), s_own], axis=-1)
        p = jax.nn.softmax(logits, axis=-1)
        p_sel = p[..., :topk * MOBA_BLOCK].reshape(b, h, QC, topk, MOBA_BLOCK)
        p_own = p[..., topk * MOBA_BLOCK:]
        return (jnp.einsum('bhqjk,bhqjkd->bhqd', p_sel, v_sel)
                + jnp.einsum('bhqk,bhkd->bhqd', p_own, v_own))

    out = lax.map(chunk, jnp.arange(s // QC))
    return out.transpose(1, 2, 0, 3, 4).reshape(b, h, s, hd)


def mixer_moba(xn, w_qkv, w_o):
    b, s, _ = xn.shape
    qkv = (xn @ w_qkv).astype(jnp.float32).reshape(b, s, 3, ATTN_HEADS, ATTN_HEAD_DIM)
    q = qkv[:, :, 0].transpose(0, 2, 1, 3)
    k = qkv[:, :, 1].transpose(0, 2, 1, 3)
    v = qkv[:, :, 2].transpose(0, 2, 1, 3)
    pos = jnp.arange(s)
    o = moba_attention(rope(q, pos), rope(k, pos), v)
    o = o.transpose(0, 2, 1, 3).reshape(b, s, D_MODEL).astype(xn.dtype)
    return o @ w_o


def setup_inputs(seed: int = 0) -> dict:
    key = jax.random.key(seed)
    ks = jax.random.split(key, 32)
    f32 = jnp.float32

    def dense(k, fan_in, fan_out):
        return jax.random.normal(k, (fan_in, fan_out), f32) * fan_in ** -0.5

    def gain(k, n):
        return 1.0 + 0.02 * jax.random.normal(k, (n,), f32)

    x = jax.random.normal(ks[0], (BATCH, SEQ, D_MODEL), f32)
    i_bias = 0.1 * jax.random.normal(ks[1], (MLSTM_HEADS,), f32)
    f_bias = jnp.linspace(3.0, 6.0, MLSTM_HEADS, dtype=f32) + 0.1 * jax.random.normal(ks[2], (MLSTM_HEADS,), f32)
    u = jax.random.uniform(ks[3], (SSD_HEADS,), f32)
    dt0 = jnp.exp(u * (math.log(0.1) - math.log(1e-3)) + math.log(1e-3))
    dt_bias = dt0 + jnp.log(-jnp.expm1(-dt0))
    a_log = jnp.log(jax.random.uniform(ks[4], (SSD_HEADS,), f32, 1.0, 16.0))
    return {
        'x': x,
        'l0_norm_mix': gain(ks[5], D_MODEL),
        'l0_w_in': dense(ks[6], D_MODEL, IN_COLS),
        'l0_mlstm_gate_bias': jnp.concatenate([i_bias, f_bias]),
        'l0_mlstm_norm': gain(ks[7], MLSTM_V_W),
        'l0_ssd_conv_w': jax.random.normal(ks[8], (SSD_CONV, SSD_XBC_W), f32) * SSD_CONV ** -0.5,
        'l0_ssd_conv_b': 0.02 * jax.random.normal(ks[9], (SSD_XBC_W,), f32),
        'l0_ssd_dt_bias': dt_bias,
        'l0_ssd_a_log': a_log,
        'l0_ssd_d': 1.0 + 0.1 * jax.random.normal(ks[10], (SSD_HEADS,), f32),
        'l0_ssd_norm': gain(ks[11], SSD_D_INNER),
        'l0_w_out': dense(ks[12], MIX_W, D_MODEL),
        'l0_norm_ffn': gain(ks[13], D_MODEL),
        'l0_ffn_gate': dense(ks[14], D_MODEL, D_FF),
        'l0_ffn_up': dense(ks[15], D_MODEL, D_FF),
        'l0_ffn_down': dense(ks[16], D_FF, D_MODEL),
        'l1_norm_mix': gain(ks[17], D_MODEL),
        'l1_w_qkv': dense(ks[18], D_MODEL, 3 * D_MODEL),
        'l1_w_o': dense(ks[19], D_MODEL, D_MODEL),
        'l1_norm_ffn': gain(ks[20], D_MODEL),
        'l1_ffn_gate': dense(ks[21], D_MODEL, D_FF),
        'l1_ffn_up': dense(ks[22], D_MODEL, D_FF),
        'l1_ffn_down': dense(ks[23], D_FF, D_MODEL),
        'final_norm': gain(ks[24], D_MODEL),
    }


def reference(x, l0_norm_mix, l0_w_in, l0_mlstm_gate_bias, l0_mlstm_norm, l0_ssd_conv_w,
              l0_ssd_conv_b, l0_ssd_dt_bias, l0_ssd_a_log, l0_ssd_d, l0_ssd_norm, l0_w_out,
              l0_norm_ffn, l0_ffn_gate, l0_ffn_up, l0_ffn_down,
              l1_norm_mix, l1_w_qkv, l1_w_o, l1_norm_ffn, l1_ffn_gate, l1_ffn_up, l1_ffn_down,
              final_norm):
    norm_mix = (l0_norm_mix, l1_norm_mix)
    norm_ffn = (l0_norm_ffn, l1_norm_ffn)
    mixer_params = ((l0_w_in, l0_mlstm_gate_bias, l0_mlstm_norm, l0_ssd_conv_w, l0_ssd_conv_b,
                     l0_ssd_dt_bias, l0_ssd_a_log, l0_ssd_d, l0_ssd_norm, l0_w_out),
                    (l1_w_qkv, l1_w_o))
    ffn_params = ((l0_ffn_gate, l0_ffn_up, l0_ffn_down), (l1_ffn_gate, l1_ffn_up, l1_ffn_down))
    for layer in range(DEPTH):
        xn = rms_norm(x, norm_mix[layer])
        if layer % 2 == 0:
            x = x + mixer_mlstm_ssd(xn, *mixer_params[layer])
        else:
            x = x + mixer_moba(xn, *mixer_params[layer])
        x = x + swiglu(rms_norm(x, norm_ffn[layer]), *ffn_params[layer])
    return rms_norm(x, final_norm)
```

```python
import numpy as np
import ml_dtypes
from contextlib import ExitStack
import concourse.bass as bass
import concourse.mybir as mybir
from concourse.bass_utils import run_bass_kernel_spmd

F32 = mybir.dt.float32
BF16 = mybir.dt.bfloat16
AF = mybir.ActivationFunctionType
ALU = mybir.AluOpType
AX = mybir.AxisListType
NPBF = ml_dtypes.bfloat16


class Buf:
    __slots__ = ("name", "w", "r")

    def __init__(self, name=""):
        self.name = name
        self.w = None
        self.r = {}


class Prog:
    COMPUTE = ("pe", "dve", "act", "pool")
    NDMA = 40

    def __init__(self, nc, es):
        self.nc = nc
        self.obj = {"pe": nc.tensor, "dve": nc.vector, "act": nc.scalar, "pool": nc.gpsimd, "sp": nc.sync}
        self.streams = {e: [] for e in self.obj}
        self.cnt = {e: 0 for e in self.COMPUTE}
        self.seen = {e: {} for e in self.obj}
        self.sem = {}
        self.es = es
        self.epoch = 0
        self.key = {}
        for e in self.COMPUTE:
            self.key[e] = (e, 0)
            self.sem[(e, 0)] = es.enter_context(nc.semaphore("s_" + e))
        self.dsem = [es.enter_context(nc.semaphore("s_dma%d" % i)) for i in range(self.NDMA)]
        self.dcnt = [0] * self.NDMA
        self.dnext = 0
        self.dnext_sw = 0
        self.NHW = 24
        self.out_tokens = []

    def _semobj(self, key):
        return self.sem[key] if isinstance(key, tuple) else self.dsem[key]

    def new_epoch(self):
        self.epoch += 1
        for e in self.COMPUTE:
            self.key[e] = (e, self.epoch)
            self.sem[(e, self.epoch)] = self.es.enter_context(self.nc.semaphore("s_%s_%d" % (e, self.epoch)))
            self.cnt[e] = 0

    def _deps(self, eng, reads, writes):
        deps = {}

        def add(t):
            if t is None:
                return
            k, v = t
            if deps.get(k, 0) < v:
                deps[k] = v

        for b in reads:
            add(b.w)
        for b in writes:
            add(b.w)
            for k, v in b.r.items():
                add((k, v))
        return deps

    def _emit_waits(self, eng, deps):
        seen = self.seen[eng]
        for k, v in deps.items():
            if eng == "pe" and isinstance(k, tuple) and k[0] == "pe":
                continue
            if seen.get(k, 0) >= v:
                continue
            seen[k] = v
            self.streams[eng].append(("wait", k, v))

    def _record(self, tok, reads, writes):
        k, v = tok
        for b in reads:
            if b.r.get(k, 0) < v:
                b.r[k] = v
        for b in writes:
            b.w = tok
            b.r = {}

    def op(self, eng, emit, reads=(), writes=()):
        deps = self._deps(eng, reads, writes)
        self._emit_waits(eng, deps)
        self.cnt[eng] += 1
        tok = (self.key[eng], self.cnt[eng])
        self.streams[eng].append(("op", emit, self.key[eng], 1))
        self._record(tok, reads, writes)
        return tok

    def dma(self, eng, out, in_, reads=(), writes=(), is_output=False):
        if eng == "pool":
            s = self.NHW + self.dnext_sw
            self.dnext_sw = (self.dnext_sw + 1) % (self.NDMA - self.NHW)
        else:
            s = self.dnext
            self.dnext = (self.dnext + 1) % self.NHW
        deps = self._deps(eng, reads, writes)
        if self.dcnt[s] > 0:
            deps[s] = max(deps.get(s, 0), self.dcnt[s])
        self._emit_waits(eng, deps)
        self.dcnt[s] += 16
        tok = (s, self.dcnt[s])
        self.streams[eng].append(("op", lambda e: e.dma_start(out=out, in_=in_), s, 16))
        self._record(tok, reads, writes)
        if is_output:
            self.out_tokens.append(tok)
        return tok

    def coll(self, kind, ins, outs, groups, reads=(), writes=()):
        eng = "pool"
        s = self.dnext
        self.dnext = (self.dnext + 1) % self.NDMA
        deps = self._deps(eng, reads, writes)
        if self.dcnt[s] > 0:
            deps[s] = max(deps.get(s, 0), self.dcnt[s])
        self._emit_waits(eng, deps)
        self.dcnt[s] += 16
        tok = (s, self.dcnt[s])
        self.streams[eng].append(("op", lambda e: e.collective_compute(kind, ALU.bypass, replica_groups=groups, ins=ins, outs=outs), s, 16))
        self._record(tok, reads, writes)
        return tok

    def barrier(self, engines=("pe", "dve", "act", "pool", "sp")):
        deps = {self.key[e]: self.cnt[e] for e in self.COMPUTE if self.cnt[e] > 0}
        for s in range(self.NDMA):
            if self.dcnt[s] > 0:
                deps[s] = self.dcnt[s]
        for e in engines:
            d = {k: v for k, v in deps.items() if k != self.key.get(e)}
            self._emit_waits(e, d)

    def finish(self):
        deps = {}
        for s in range(self.NDMA):
            if self.dcnt[s] > 0:
                deps[s] = self.dcnt[s]
        for e in self.COMPUTE:
            if self.cnt[e] > 0:
                deps[self.key[e]] = self.cnt[e]
        self._emit_waits("sp", deps)

    def emit(self, block):
        def run(name):
            def f(e):
                for item in self.streams[name]:
                    if item[0] == "wait":
                        e.wait_ge(self._semobj(item[1]), item[2])
                    else:
                        ins = item[1](e)
                        ins.then_inc(self._semobj(item[2]), item[3])
            return f

        block.tensor(run("pe"))
        block.vector(run("dve"))
        block.scalar(run("act"))
        block.gpsimd(run("pool"))
        block.sync(run("sp"))


class Ctx:
    def __init__(self):
        self.nc = bass.Bass("TRN2", target_bir_lowering=False)
        self.es = ExitStack()
        self.P = Prog(self.nc, self.es)
        self.uid = 0
        self.cur = self.es
        self.bind = {}
        self.prefix = ""
        self.ext = {}
        self.has_consts = False

    def sb(self, shape, dt, name=None):
        self.uid += 1
        return self.cur.enter_context(self.nc.sbuf_tensor("%s_%d" % (name or "t", self.uid), list(shape), dt))

    def ps(self, shape, dt, name=None):
        self.uid += 1
        return self.cur.enter_context(self.nc.psum_tensor("%s_%d" % (name or "p", self.uid), list(shape), dt))

    def dram_in(self, name, shape, dt):
        if name in self.bind:
            ap = self.bind[name]
            assert list(ap.shape) == list(shape), (name, ap.shape, shape)
            return ap
        full = self.prefix + name
        if full not in self.ext:
            self.ext[full] = self.nc.dram_tensor(full, list(shape), dt, kind="ExternalInput").ap()
        return self.ext[full]

    def dram_out(self, name, shape, dt):
        if name in self.bind:
            ap = self.bind[name]
            assert list(ap.shape) == list(shape), (name, ap.shape, shape)
            return ap
        return self.nc.dram_tensor(self.prefix + name, list(shape), dt, kind="ExternalOutput").ap()

    def scratch(self, name, shape, dt):
        return self.nc.dram_tensor(name, list(shape), dt).ap()

    def phase_begin(self, bind=None, prefix=""):
        self.cur = ExitStack()
        self.bind = bind or {}
        self.prefix = prefix

    def phase_end(self):
        self.P.barrier()
        self.P.new_epoch()
        self.cur.close()
        self.cur = self.es
        self.bind = {}
        self.prefix = ""

    def finish(self):
        self.P.finish()
        block = self.es.enter_context(self.nc.Block())
        self.P.emit(block)
        self.es.close()
        return self.nc


class Ring:
    def __init__(self, tiles):
        self.tiles = tiles
        self.bufs = [Buf() for _ in tiles]
        self.i = 0

    def next(self):
        t, b = self.tiles[self.i], self.bufs[self.i]
        self.i = (self.i + 1) % len(self.tiles)
        return t, b


def make_consts(C):
    if C.has_consts:
        return
    C.has_consts = True
    P = C.P
    saved, C.cur = C.cur, C.es
    ident_b = C.sb([128, 128], BF16, "identb")
    ident_f = C.sb([128, 128], F32, "identf")
    C.cur = saved
    bi = Buf("ident")
    P.op("pool", lambda e: e.memset(ident_f[:, :], 0.0), writes=[bi])
    P.op("pool", lambda e: e.affine_select(out=ident_f[:, :], in_=ident_f[:, :], pattern=[[-1, 128]],
                                           compare_op=ALU.not_equal, fill=1.0, base=0, channel_multiplier=1),
         reads=[bi], writes=[bi])
    P.op("pool", lambda e: e.tensor_copy(out=ident_b[:, :], in_=ident_f[:, :]), reads=[bi], writes=[bi])
    C.ident_b, C.ident_f, C.ident_buf = ident_b, ident_f, bi


def load_colvec(C, vec_ap, n, psum_ring):
    P = C.P
    rows = C.sb([n, 128], F32, "cvrows")
    col = C.sb([128, n], F32, "cvcol")
    b_rows, b_col = Buf(), Buf()
    P.dma("sp", rows[:, :], vec_ap.rearrange("(k p) -> k p", p=128), writes=[b_rows])
    pt, pb = psum_ring.next()
    P.op("pe", lambda e: e.matmul(pt[:, 0:n], lhsT=rows[:, :], rhs=C.ident_f[0:n, 0:n], start=True, stop=True),
         reads=[b_rows, C.ident_buf], writes=[pb])
    P.op("dve", lambda e: e.tensor_copy(out=col[:, :], in_=pt[:, 0:n]), reads=[pb], writes=[b_col])
    return col, b_col


def rmsnorm_T(C, x_res, xbufs, gcol, gbuf, xnT, xnT_bufs, ntt, tp_ring, junk_t, junk_b, eps=1e-6, D=2048):
    P = C.P
    nk = D // 128
    ssq = C.sb([128, ntt], F32, "ssq")
    rstd = C.sb([128, ntt], F32, "rstd")
    junk = junk_t[:, 0:D]
    xs = [C.sb([128, D], BF16, "xs") for _ in range(2)]
    b_ssq0, b_rstd = Buf(), Buf()
    b_ssq = [Buf() for _ in range(ntt)]
    b_xs = [Buf(), Buf()]
    P.op("dve", lambda e: e.memset(ssq[:, :], 0.0), writes=[b_ssq0])
    for t in range(ntt):
        P.op("act", lambda e, t=t: e.activation(out=junk, in_=x_res[:, t, :], func=AF.Square,
                                                accum_out=ssq[:, t:t + 1]),
             reads=[xbufs[t], b_ssq0], writes=[b_ssq[t], junk_b])
    P.op("dve", lambda e: e.tensor_scalar(out=rstd[:, :], in0=ssq[:, :], scalar1=1.0 / D, scalar2=eps,
                                          op0=ALU.mult, op1=ALU.add), reads=b_ssq, writes=[b_rstd])
    P.op("act", lambda e: e.activation(out=rstd[:, :], in_=rstd[:, :], func=AF.Sqrt), reads=[b_rstd], writes=[b_rstd])
    P.op("dve", lambda e: e.reciprocal(out=rstd[:, :], in_=rstd[:, :]), reads=[b_rstd], writes=[b_rstd])
    for t in range(ntt):
        xst, xsb = xs[t % 2], b_xs[t % 2]
        P.op("act", lambda e, t=t, xst=xst: e.activation(out=xst[:, :], in_=x_res[:, t, :], func=AF.Copy,
                                                         scale=rstd[:, t:t + 1]),
             reads=[xbufs[t], b_rstd], writes=[xsb])
        for k0 in range(0, nk, 4):
            pt, pb = tp_ring.next()
            for j in range(4):
                P.op("pe", lambda e, pt=pt, j=j, k0=k0, xst=xst: e.transpose(
                    pt[:, j * 128:(j + 1) * 128], xst[:, (k0 + j) * 128:(k0 + j + 1) * 128], C.ident_b[:, :]),
                    reads=[xsb, C.ident_buf], writes=[pb])
            P.op("dve", lambda e, pt=pt, k0=k0, t=t: e.tensor_tensor(
                out=xnT[:, k0:k0 + 4, t * 128:(t + 1) * 128],
                in0=pt[:, :].rearrange("p (a b) -> p a b", a=4),
                in1=gcol[:, k0:k0 + 4].unsqueeze(2).to_broadcast([128, 4, 128]), op=ALU.mult),
                reads=[pb, gbuf], writes=[xnT_bufs[t]])


NT = 1024
NTT = NT // 128
DM = 2048
DFF = 5632
WSLOT = 8192


class WStream:
    def __init__(self, C, nslots=4):
        self.C = C
        self.slots = [C.sb([128, WSLOT], BF16, "wslot") for _ in range(nslots)]
        self.bufs = [Buf() for _ in range(nslots)]
        self.i = 0

    def load(self, W, r0, nkc, c0, ncols):
        assert nkc * ncols <= WSLOT
        s = self.i
        self.i = (self.i + 1) % len(self.slots)
        view = self.slots[s][:, 0:nkc * ncols].rearrange("p (k c) -> p k c", k=nkc)
        src = W[r0:r0 + nkc * 128, c0:c0 + ncols].rearrange("(k p) c -> p k c", p=128)
        self.C.P.dma("pool", view, src, writes=[self.bufs[s]])
        return view, self.bufs[s]


def dense_program(cfg):
    C = Ctx()
    dense_phase(C, cfg)
    return C.finish()


def dense_phase(C, cfg):
    P = C.P
    nc = C.nc
    x_in = C.dram_in("x", [NT, DM], F32)
    make_consts(C)
    tp_ring = Ring([C.ps([128, 512], BF16, "tp") for _ in range(2)])
    mm_ring = Ring([C.ps([128, 512], F32, "mm") for _ in range(6)])
    ws = WStream(C, 3)

    x_res = C.sb([128, NTT, DM], F32, "xres")
    xb = [Buf("x%d" % t) for t in range(NTT)]
    for t in range(NTT):
        P.dma("sp", x_res[:, t, :], x_in[t * 128:(t + 1) * 128, :], writes=[xb[t]])
    FQ = 11
    arena = C.sb([128, 16 * NT + FQ * NT], BF16, "arena")
    xnT = arena[:, 0:16 * NT].rearrange("p (k t) -> p k t", k=16)
    hT = arena[:, 16 * NT:16 * NT + FQ * NT].rearrange("p (k t) -> p k t", k=FQ)
    xnb = [Buf("xn%d" % t) for t in range(NTT)]
    hb = [Buf("h%d" % f) for f in range(FQ)]
    stg = [C.sb([128, 4096], BF16, "stg") for _ in range(2)]
    stgb = [Buf(), Buf()]
    evac_i = [0]

    def evac(out, in_, reads, writes):
        evac_i[0] += 1
        if evac_i[0] % 2:
            P.op("act", lambda e: e.activation(out=out, in_=in_, func=AF.Copy), reads=reads, writes=writes)
        else:
            P.op("dve", lambda e: e.tensor_copy(out=out, in_=in_), reads=reads, writes=writes)

    def add_into_x(t, c0, pt, pb):
        P.op("dve", lambda e: e.tensor_tensor(out=x_res[:, t, c0:c0 + 512], in0=pt[:, :], in1=x_res[:, t, c0:c0 + 512],
                                              op=ALU.add), reads=[pb, xb[t]], writes=[xb[t]])

    if cfg.get("mix"):
        Kin = cfg["mix"]
        mixT_in = C.dram_in("mixT", [Kin, NT], BF16)
        w_mix = C.dram_in("w_mix", [Kin, DM], F32)
        mixT = xnT
        for kh in range(Kin // 2048):
            P.dma("sp", mixT, mixT_in[kh * 2048:(kh + 1) * 2048, :].rearrange("(k p) t -> p k t", p=128), writes=xnb)
            for cb in range(DM // 512):
                wv, wb = ws.load(w_mix, kh * 2048, 16, cb * 512, 512)
                for t in range(NTT):
                    pt, pb = mm_ring.next()
                    for kc in range(16):
                        P.op("pe", lambda e, pt=pt, wv=wv, kc=kc, t=t: e.matmul(
                            pt[:, :], lhsT=mixT[:, kc, t * 128:(t + 1) * 128], rhs=wv[:, kc, :],
                            start=(kc == 0), stop=(kc == 15)), reads=[xnb[t], wb], writes=[pb])
                    add_into_x(t, cb * 512, pt, pb)

    if cfg.get("ffn"):
        g_ffn = C.dram_in("g_ffn", [DM], F32)
        w_gate = C.dram_in("w_gate", [DM, DFF], F32)
        w_up = C.dram_in("w_up", [DM, DFF], F32)
        w_down = C.dram_in("w_down", [DFF, DM], F32)
        gcol, gbuf = load_colvec(C, g_ffn, DM // 128, mm_ring)
        rmsnorm_T(C, x_res, xb, gcol, gbuf, xnT, xnb, NTT, tp_ring, stg[1], stgb[1])
        NQ = DFF // 128 // FQ
        sg = [C.sb([128, 512], F32, "sg") for _ in range(2)]
        sgb = [Buf(), Buf()]
        sgi = 0
        for q in range(NQ):
            f0 = q * FQ
            for p0 in range(0, FQ, 4):
                npc = min(4, FQ - p0)
                c0 = (f0 + p0) * 128
                gv, gb_ = ws.load(w_gate, 0, 16, c0, npc * 128)
                uv, ub_ = ws.load(w_up, 0, 16, c0, npc * 128)
                for ci in range(npc):
                    f = p0 + ci
                    for th in range(NT // 512):
                        pg, pgb = mm_ring.next()
                        pu, pub = mm_ring.next()
                        for kc in range(16):
                            P.op("pe", lambda e, pg=pg, gv=gv, kc=kc, ci=ci, th=th: e.matmul(
                                pg[:, :], lhsT=gv[:, kc, ci * 128:(ci + 1) * 128], rhs=xnT[:, kc, th * 512:(th + 1) * 512],
                                start=(kc == 0), stop=(kc == 15)), reads=[gb_] + xnb[th * 4:th * 4 + 4], writes=[pgb])
                        for kc in range(16):
                            P.op("pe", lambda e, pu=pu, uv=uv, kc=kc, ci=ci, th=th: e.matmul(
                                pu[:, :], lhsT=uv[:, kc, ci * 128:(ci + 1) * 128], rhs=xnT[:, kc, th * 512:(th + 1) * 512],
                                start=(kc == 0), stop=(kc == 15)), reads=[ub_] + xnb[th * 4:th * 4 + 4], writes=[pub])
                        s_t, s_b = sg[sgi % 2], sgb[sgi % 2]
                        sgi += 1
                        P.op("act", lambda e, s_t=s_t, pg=pg: e.activation(out=s_t[:, :], in_=pg[:, :], func=AF.Silu),
                             reads=[pgb], writes=[s_b])
                        P.op("dve", lambda e, s_t=s_t, pu=pu, f=f, th=th: e.tensor_tensor(
                            out=hT[:, f, th * 512:(th + 1) * 512], in0=pu[:, :], in1=s_t[:, :], op=ALU.mult),
                            reads=[pub, s_b], writes=[hb[f]])
            for cb in range(DM // 512):
                wv, wb = ws.load(w_down, f0 * 128, FQ, cb * 512, 512)
                for t in range(NTT):
                    pt, pb = mm_ring.next()
                    for f in range(FQ):
                        P.op("pe", lambda e, pt=pt, wv=wv, f=f, t=t: e.matmul(
                            pt[:, :], lhsT=hT[:, f, t * 128:(t + 1) * 128], rhs=wv[:, f, :],
                            start=(f == 0), stop=(f == FQ - 1)), reads=[hb[f], wb], writes=[pb])
                    add_into_x(t, cb * 512, pt, pb)

    if cfg.get("x_out"):
        x_out = C.dram_out("x_out", [NT, DM], F32)
        for t in range(NTT):
            P.dma("sp", x_out[t * 128:(t + 1) * 128, :], x_res[:, t, :], reads=[xb[t]], is_output=True)

    if cfg.get("final"):
        g_fin = C.dram_in("g_fin", [DM], F32)
        y_out = C.dram_out("y", [NT, DM], F32)
        gb_t = arena[:, 16 * NT:16 * NT + 2 * DM].bitcast(F32)
        gbb = Buf()
        P.dma("sp", gb_t, g_fin.partition_broadcast(128), writes=hb + [gbb])
        ssq = C.sb([128, NTT], F32, "fssq")
        rstd = C.sb([128, NTT], F32, "frstd")
        junk = C.sb([128, DM], BF16, "fjunk")
        fjunk_b = Buf()
        yt = [stg[i][:, :].bitcast(F32) for i in range(2)]
        ytb = stgb
        b0, b_r = Buf(), Buf()
        b_s = [Buf() for _ in range(NTT)]
        P.op("dve", lambda e: e.memset(ssq[:, :], 0.0), writes=[b0])
        for t in range(NTT):
            P.op("act", lambda e, t=t: e.activation(out=junk[:, :], in_=x_res[:, t, :], func=AF.Square,
                                                    accum_out=ssq[:, t:t + 1]), reads=[xb[t], b0], writes=[b_s[t], fjunk_b])
        P.op("dve", lambda e: e.tensor_scalar(out=rstd[:, :], in0=ssq[:, :], scalar1=1.0 / DM, scalar2=1e-6,
                                              op0=ALU.mult, op1=ALU.add), reads=b_s, writes=[b_r])
        P.op("act", lambda e: e.activation(out=rstd[:, :], in_=rstd[:, :], func=AF.Sqrt), reads=[b_r], writes=[b_r])
        P.op("dve", lambda e: e.reciprocal(out=rstd[:, :], in_=rstd[:, :]), reads=[b_r], writes=[b_r])
        for t in range(NTT):
            y_t, y_b = yt[t % 2], ytb[t % 2]
            P.op("dve", lambda e, t=t, y_t=y_t: e.scalar_tensor_tensor(
                out=y_t, in0=x_res[:, t, :], scalar=rstd[:, t:t + 1], in1=gb_t,
                op0=ALU.mult, op1=ALU.mult), reads=[xb[t], b_r, gbb], writes=[y_b])
            P.dma("sp", y_out[t * 128:(t + 1) * 128, :], y_t, reads=[y_b], is_output=True)

    if cfg.get("proj"):
        pn = cfg.get("proj_names", ("g_proj", "w_proj"))
        g_mix = C.dram_in(pn[0], [DM], F32)
        w_proj = C.dram_in(pn[1], [DM, cfg["wcols"]], F32)
        gcol2, gbuf2 = load_colvec(C, g_mix, DM // 128, mm_ring)
        rmsnorm_T(C, x_res, xb, gcol2, gbuf2, xnT, xnb, NTT, tp_ring, stg[1], stgb[1])
        sti = 0
        for (name, c0, ncols, layout, dt) in cfg["proj"]:
            esz = 2 if dt == BF16 else 4
            if layout == "T":
                dst = C.dram_out(name, [ncols, NT], dt)
                for p0 in range(0, ncols, 512):
                    npc = min(512, ncols - p0) // 128
                    wv, wb = ws.load(w_proj, 0, 16, c0 + p0, npc * 128)
                    st, sb_ = stg[sti % 2], stgb[sti % 2]
                    sti += 1
                    stv = st[:, 0:npc * NT].rearrange("p (c t) -> p c t", c=npc)
                    for ci in range(npc):
                        for th in range(NT // 512):
                            pt, pb = mm_ring.next()
                            for kc in range(16):
                                P.op("pe", lambda e, pt=pt, wv=wv, kc=kc, ci=ci, th=th: e.matmul(
                                    pt[:, :], lhsT=wv[:, kc, ci * 128:(ci + 1) * 128],
                                    rhs=xnT[:, kc, th * 512:(th + 1) * 512],
                                    start=(kc == 0), stop=(kc == 15)), reads=[wb] + xnb[th * 4:th * 4 + 4], writes=[pb])
                            evac(stv[:, ci, th * 512:(th + 1) * 512], pt[:, :], [pb], [sb_])
                    P.dma("sp", dst[p0:p0 + npc * 128, :].rearrange("(c p) t -> p c t", p=128), stv, reads=[sb_],
                          is_output=True)
            else:
                dst = C.dram_out(name, [NT, ncols], dt)
                for p0 in range(0, ncols, 512):
                    pw = min(512, ncols - p0)
                    wv, wb = ws.load(w_proj, 0, 16, c0 + p0, pw)
                    st, sb_ = stg[sti % 2], stgb[sti % 2]
                    sti += 1
                    if dt == BF16:
                        stv = st[:, 0:NTT * pw].rearrange("p (t c) -> p t c", t=NTT)
                    else:
                        stv = st[:, :].bitcast(F32)[:, 0:NTT * pw].rearrange("p (t c) -> p t c", t=NTT)
                    for t in range(NTT):
                        pt, pb = mm_ring.next()
                        for kc in range(16):
                            P.op("pe", lambda e, pt=pt, wv=wv, kc=kc, t=t, pw=pw: e.matmul(
                                pt[:, 0:pw], lhsT=xnT[:, kc, t * 128:(t + 1) * 128], rhs=wv[:, kc, :],
                                start=(kc == 0), stop=(kc == 15)), reads=[wb, xnb[t]], writes=[pb])
                        evac(stv[:, t, :], pt[:, 0:pw], [pb], [sb_])
                    P.dma("sp", dst[:, p0:p0 + pw].rearrange("(t p) c -> p t c", p=128), stv, reads=[sb_],
                          is_output=True)
    return


S_LEN = 2048
ROPE_THETA = 10000.0


def attn_program(nheads=8, dbg=False):
    C = Ctx()
    attn_phase(C, nheads, dbg)
    return C.finish()


def attn_phase(C, nheads=8, dbg=False):
    P = C.P
    HD = 128
    NB = 8
    BLK = 256
    qT_in = C.dram_in("qT", [nheads * HD, S_LEN], BF16)
    kT_in = C.dram_in("kT", [nheads * HD, S_LEN], BF16)
    v_in = C.dram_in("v", [S_LEN, nheads * HD], BF16)
    oT_out = C.dram_out("oT", [nheads * HD, S_LEN], BF16)
    make_consts(C)
    cb = Buf("consts")
    ones_b = C.sb([128, 128], BF16, "ones")
    P.op("pool", lambda e: e.memset(ones_b[:, :], 1.0), writes=[cb])
    rt_f = C.sb([128, 128], F32, "rtf")
    rt_b = C.sb([128, 128], BF16, "rtb")
    P.op("pool", lambda e: e.memset(rt_f[:, :], 0.0), writes=[cb])
    P.op("pool", lambda e: e.affine_select(out=rt_f[:, :], in_=rt_f[:, :], pattern=[[-1, 128]], compare_op=ALU.not_equal,
                                           fill=1.0, base=64, channel_multiplier=1), reads=[cb], writes=[cb])
    P.op("pool", lambda e: e.affine_select(out=rt_f[:, :], in_=rt_f[:, :], pattern=[[-1, 128]], compare_op=ALU.not_equal,
                                           fill=-1.0, base=-64, channel_multiplier=1), reads=[cb], writes=[cb])
    P.op("pool", lambda e: e.tensor_copy(out=rt_b[:, :], in_=rt_f[:, :]), reads=[cb], writes=[cb])
    cm_f = C.sb([128, 2, BLK], F32, "cmf")
    cm_b = C.sb([128, 2, BLK], BF16, "cmb")
    P.op("pool", lambda e: e.memset(cm_f[:, :, :], 1.0), writes=[cb])
    for c in range(2):
        P.op("pool", lambda e, c=c: e.affine_select(out=cm_f[:, c, :], in_=cm_f[:, c, :], pattern=[[1, BLK]],
                                                    compare_op=ALU.is_ge, fill=0.0, base=-128 * c, channel_multiplier=-1),
             reads=[cb], writes=[cb])
    P.op("pool", lambda e: e.tensor_copy(out=cm_b[:, :, :], in_=cm_f[:, :, :]), reads=[cb], writes=[cb])
    oh_f = C.sb([8, NB, 128], F32, "ohf")
    oh_b = C.sb([8, NB, 128], BF16, "ohb")
    P.op("pool", lambda e: e.memset(oh_f[:, :, :], 0.0), writes=[cb])
    P.op("pool", lambda e: e.affine_select(out=oh_f[:, :, :], in_=oh_f[:, :, :], pattern=[[-1, NB], [0, 128]],
                                           compare_op=ALU.not_equal, fill=1.0, base=0, channel_multiplier=1),
         reads=[cb], writes=[cb])
    P.op("pool", lambda e: e.tensor_copy(out=oh_b[:, :, :], in_=oh_f[:, :, :]), reads=[cb], writes=[cb])
    negm = C.sb([128, 16, NB], F32, "negm")
    P.op("pool", lambda e: e.memset(negm[:, :, :], 0.0), writes=[cb])
    for t in range(16):
        P.op("pool", lambda e, t=t: e.memset(negm[:, t, t // 2:NB], -1e30), reads=[cb], writes=[cb])
    pos_i = C.sb([128, S_LEN], mybir.dt.int32, "posi")
    ang = C.sb([128, S_LEN], F32, "ang")
    cosT = C.sb([128, S_LEN], F32, "cosT")
    sinT = C.sb([128, S_LEN], F32, "sinT")
    pid_i = C.sb([128, 1], mybir.dt.int32, "pidi")
    pid_f = C.sb([128, 1], F32, "pidf")
    inv = C.sb([128, 1], F32, "inv")
    P.op("pool", lambda e: e.iota(pos_i[:, :], pattern=[[1, S_LEN]], base=0, channel_multiplier=0), writes=[cb])
    P.op("pool", lambda e: e.iota(pid_i[:, :], pattern=[[0, 1]], base=0, channel_multiplier=1), reads=[cb], writes=[cb])
    P.op("pool", lambda e: e.tensor_copy(out=ang[:, :], in_=pos_i[:, :]), reads=[cb], writes=[cb])
    P.op("pool", lambda e: e.tensor_copy(out=pid_f[:, :], in_=pid_i[:, :]), reads=[cb], writes=[cb])
    pid_g = C.sb([128, 1], F32, "pidg")
    P.op("dve", lambda e: e.tensor_scalar(out=pid_g[:, :], in0=pid_f[:, :], scalar1=64.0, scalar2=64.0, op0=ALU.is_ge, op1=ALU.mult),
         reads=[cb], writes=[cb])
    P.op("dve", lambda e: e.tensor_tensor(out=pid_f[:, :], in0=pid_f[:, :], in1=pid_g[:, :], op=ALU.subtract), reads=[cb], writes=[cb])
    P.op("act", lambda e: e.activation(out=inv[:, :], in_=pid_f[:, :], func=AF.Exp, scale=-float(np.log(ROPE_THETA)) / 64.0),
         reads=[cb], writes=[cb])
    PI = float(np.pi)
    MAGIC = 12582912.0
    P.op("dve", lambda e: e.tensor_scalar(out=inv[:, :], in0=inv[:, :], scalar1=1.0 / (2 * PI), scalar2=None, op0=ALU.mult),
         reads=[cb], writes=[cb])
    P.op("dve", lambda e: e.tensor_scalar(out=ang[:, :], in0=ang[:, :], scalar1=inv[:, 0:1], scalar2=None, op0=ALU.mult),
         reads=[cb], writes=[cb])
    for (dst, off) in ((sinT, 0.0), (cosT, 0.25)):
        P.op("dve", lambda e, dst=dst, off=off: e.tensor_scalar(out=dst[:, :], in0=ang[:, :], scalar1=off, scalar2=MAGIC, op0=ALU.add, op1=ALU.add),
             reads=[cb], writes=[cb])
        P.op("dve", lambda e, dst=dst: e.tensor_scalar(out=dst[:, :], in0=dst[:, :], scalar1=-MAGIC, scalar2=None, op0=ALU.add),
             reads=[cb], writes=[cb])
        P.op("dve", lambda e, dst=dst, off=off: e.scalar_tensor_tensor(out=dst[:, :], in0=ang[:, :], scalar=off, in1=dst[:, :], op0=ALU.add, op1=ALU.subtract),
             reads=[cb], writes=[cb])
        P.op("act", lambda e, dst=dst: e.activation(out=dst[:, :], in_=dst[:, :], func=AF.Sin, scale=6.28318), reads=[cb], writes=[cb])

    bankA = C.ps([128, 512], F32, "bA")
    bankB = C.ps([128, 512], F32, "bB")
    bankC = C.ps([128, 512], F32, "bC")
    bankD = C.ps([128, 512], F32, "bD")
    bankG = C.ps([128, 512], F32, "bG")
    bankH = C.ps([128, 512], F32, "bH")
    bankE = C.ps([128, 512], F32, "bE")
    bankF = C.ps([128, 512], F32, "bF")
    rot_ring = Ring([bankE, bankF])
    gate_b = rot_ring.bufs[0]
    bF_b = rot_ring.bufs[1]
    st_ring = Ring([bankA[:, 0:256], bankB[:, 0:256]])

    NSLOT = 2
    scale = 1.0 / float(np.sqrt(HD))
    t1 = [C.sb([128, 512], F32, "t1") for _ in range(2)]
    t2 = [C.sb([128, 512], F32, "t2") for _ in range(2)]
    t1b = [Buf(), Buf()]
    t2b = [Buf(), Buf()]
    tstate = {"ti": 0}
    slots = []
    for sl_ in range(NSLOT):
        d = dict(
            raw_q=C.sb([128, S_LEN], BF16, "rawq"), raw_k=C.sb([128, S_LEN], BF16, "rawk"), v=C.sb([128, 16, HD], BF16, "vsb"),
            rq_b=Buf(), rk_b=Buf(), v_b=Buf(),
            qr=C.sb([128, S_LEN], BF16, "qr"), kr=C.sb([128, S_LEN], BF16, "kr"),
            qr_b=[Buf() for _ in range(4)], kr_b=[Buf() for _ in range(4)],
            kmT=C.sb([128, NB], F32, "kmT"), kmT_bf=C.sb([128, NB], BF16, "kmTb"), km_b=Buf(),
            gm=C.sb([128, 16, NB], F32, "gm"), mx8=C.sb([128, 16, 8], F32, "mx8"), bias_f=C.sb([128, 16, NB], F32, "biasf"),
            bias_bf=C.sb([128, 16, NB], BF16, "biasb"), gm_b=Buf(),
            biasT=C.sb([8, S_LEN], BF16, "biasT"), biasT_b=Buf(),
            pT=[C.sb([128, BLK], BF16, "pT") for _ in range(3)], pT_b=[Buf() for _ in range(3)],
            rden=C.sb([128, BLK], F32, "rden"), rden_b=Buf(),
            oT_sb=C.sb([128, S_LEN], BF16, "oTsb"), oT_b=Buf(),
            acc=(bankC, bankG) if sl_ == 0 else (bankD, bankH), acc_b=Buf(), acc_b2=Buf(),
            rden2=C.sb([128, BLK], F32, "rden2"), rden2_b=Buf(),
        )
        slots.append(d)

    def head_gen(h, D):
        raw_q, raw_k, v_t = D["raw_q"], D["raw_k"], D["v"]
        qr, kr = D["qr"], D["kr"]
        qr_b, kr_b = D["qr_b"], D["kr_b"]
        P.dma("sp", raw_q[:, :], qT_in[h * HD:(h + 1) * HD, :], writes=[D["rq_b"]])
        P.dma("sp", raw_k[:, :], kT_in[h * HD:(h + 1) * HD, :], writes=[D["rk_b"]])
        P.dma("sp", v_t[:, :, :], v_in[:, h * HD:(h + 1) * HD].rearrange("(t p) c -> p t c", p=128), writes=[D["v_b"]])
        yield
        for (raw, rawb, dst, dstb) in ((raw_q, D["rq_b"], qr, qr_b), (raw_k, D["rk_b"], kr, kr_b)):
            for blk in range(4):
                sl = slice(blk * 512, (blk + 1) * 512)
                pr, prb = rot_ring.next()
                P.op("pe", lambda e, pr=pr, raw=raw, sl=sl: e.matmul(pr[:, :], lhsT=rt_b[:, :], rhs=raw[:, sl], start=True, stop=True),
                     reads=[cb, rawb], writes=[prb])
                ti = tstate["ti"]
                tstate["ti"] += 1
                a, ab = t1[ti % 2], t1b[ti % 2]
                b2, bb = t2[ti % 2], t2b[ti % 2]
                P.op("pool", lambda e, a=a, raw=raw, sl=sl: e.tensor_tensor(out=a[:, :], in0=raw[:, sl], in1=cosT[:, sl], op=ALU.mult),
                     reads=[rawb, cb], writes=[ab])
                P.op("dve", lambda e, b2=b2, pr=pr, sl=sl: e.tensor_tensor(out=b2[:, :], in0=pr[:, :], in1=sinT[:, sl], op=ALU.mult),
                     reads=[prb, cb], writes=[bb])
                P.op("pool", lambda e, a=a, b2=b2, dst=dst, sl=sl: e.tensor_tensor(out=dst[:, sl], in0=a[:, :], in1=b2[:, :], op=ALU.add),
                     reads=[ab, bb], writes=[dstb[blk]])
                yield
        kmT, kmT_bf, km_b = D["kmT"], D["kmT_bf"], D["km_b"]
        gm, mx8, bias_f, bias_bf, gm_b = D["gm"], D["mx8"], D["bias_f"], D["bias_bf"], D["gm_b"]
        biasT, biasT_b = D["biasT"], D["biasT_b"]
        P.op("dve", lambda e: e.tensor_reduce(out=kmT[:, :], in_=kr[:, :].rearrange("p (n s) -> p n s", n=NB), axis=AX.X, op=ALU.add),
             reads=kr_b, writes=[km_b])
        P.op("dve", lambda e: e.tensor_scalar(out=kmT_bf[:, :], in0=kmT[:, :], scalar1=1.0 / BLK, scalar2=None, op0=ALU.mult),
             reads=[km_b], writes=[km_b])
        yield
        for t in range(16):
            P.op("pe", lambda e, t=t: e.matmul(bankE[:, t * NB:(t + 1) * NB], lhsT=qr[:, t * 128:(t + 1) * 128], rhs=kmT_bf[:, :],
                                               start=True, stop=True), reads=[qr_b[t // 4], km_b], writes=[gate_b])
        P.op("dve", lambda e: e.tensor_tensor(out=gm[:, :, :], in0=bankE[:, 0:16 * NB].rearrange("p (t n) -> p t n", t=16),
                                              in1=negm[:, :, :], op=ALU.add), reads=[gate_b, cb], writes=[gm_b])
        yield
        for t in range(16):
            P.op("dve", lambda e, t=t: e.max(out=mx8[:, t, :], in_=gm[:, t, :]), reads=[gm_b], writes=[gm_b])
            if t % 4 == 3:
                yield
        P.op("dve", lambda e: e.tensor_tensor(out=bias_f[:, :, :], in0=gm[:, :, :],
                                              in1=mx8[:, :, 2:3].to_broadcast([128, 16, NB]), op=ALU.is_ge),
             reads=[gm_b], writes=[gm_b])
        P.op("dve", lambda e: e.tensor_scalar(out=bias_bf[:, :, :], in0=bias_f[:, :, :], scalar1=-1.0, scalar2=30000.0,
                                              op0=ALU.add, op1=ALU.mult), reads=[gm_b], writes=[gm_b])
        yield
        for g4 in range(4):
            for j in range(4):
                t = g4 * 4 + j
                P.op("pe", lambda e, t=t, j=j: e.matmul(bankF[0:8, j * 128:(j + 1) * 128], lhsT=bias_bf[:, t, :], rhs=C.ident_b[:, :],
                                                        start=True, stop=True), reads=[gm_b, C.ident_buf], writes=[bF_b])
            P.op("act", lambda e, g4=g4: e.activation(out=biasT[0:8, g4 * 512:(g4 + 1) * 512], in_=bankF[0:8, :], func=AF.Copy),
                 reads=[bF_b], writes=[biasT_b])
            yield
        osb, osbb = D["oT_sb"], D["oT_b"]
        while mstate["busy"]:
            yield
        mstate["busy"] = True
        acc_opts = [((bankC, bankG), accbufs[0]), ((bankD, bankH), accbufs[1])]
        rd_opts = [(D["rden"], D["rden_b"]), (D["rden2"], D["rden2_b"])]
        pT, pT_b = D["pT"], D["pT_b"]
        pairs = [(qb, kc) for qb in range(NB) for kc in range(2 * qb + 2)]

        def emit_st(i):
            qb, kc = pairs[i]
            own = kc >= 2 * qb
            qsl = slice(qb * BLK, (qb + 1) * BLK)
            stt, stb = st_ring.next()
            P.op("pe", lambda e: e.matmul(stt, lhsT=kr[:, kc * 128:(kc + 1) * 128], rhs=qr[:, qsl], start=True, stop=own),
                 reads=[kr_b[kc // 4], qr_b[qb // 2]], writes=[stb])
            if not own:
                P.op("pe", lambda e: e.matmul(stt, lhsT=oh_b[:, kc // 2, :], rhs=biasT[0:8, qsl], start=False, stop=True),
                     reads=[cb, biasT_b], writes=[stb])
            return stt, stb

        LOOK = C.attn_look if hasattr(C, "attn_look") else 1
        pend = [emit_st(j) for j in range(min(LOOK, len(pairs)))]
        for i, (qb, kc) in enumerate(pairs):
            if i + LOOK < len(pairs):
                pend.append(emit_st(i + LOOK))
            if not pend:
                pend.append(emit_st(i))
            stt, stb = pend.pop(0)
            (acc, accd), accb = acc_opts[qb % len(acc_opts)]
            own = kc >= 2 * qb
            nkc = 2 * qb + 2
            qsl = slice(qb * BLK, (qb + 1) * BLK)
            p_t, p_b = pT[i % 3], pT_b[i % 3]
            P.op("act", lambda e, p_t=p_t, stt=stt: e.activation(out=p_t[:, :], in_=stt, func=AF.Exp, scale=scale),
                 reads=[stb], writes=[p_b])
            if own:
                P.op("dve", lambda e, p_t=p_t, c=kc - 2 * qb: e.tensor_tensor(out=p_t[:, :], in0=p_t[:, :], in1=cm_b[:, c, :], op=ALU.mult),
                     reads=[p_b, cb], writes=[p_b])
            P.op("pe", lambda e, kc=kc, p_t=p_t, nkc=nkc, acc=acc: e.matmul(
                acc[:, 0:BLK], lhsT=v_t[:, kc, :], rhs=p_t[:, :], start=(kc == 0), stop=(kc == nkc - 1)),
                reads=[D["v_b"], p_b], writes=[accb])
            P.op("pe", lambda e, kc=kc, p_t=p_t, nkc=nkc, accd=accd: e.matmul(
                accd[:, 0:BLK], lhsT=ones_b[:, :], rhs=p_t[:, :], start=(kc == 0), stop=(kc == nkc - 1)),
                reads=[cb, p_b], writes=[accb])
            if kc == nkc - 1:
                rd, rdb = rd_opts[qb % 2]
                P.op("dve", lambda e, rd=rd, accd=accd: e.reciprocal(out=rd[:, :], in_=accd[:, 0:BLK]), reads=[accb], writes=[rdb])
                P.op("dve", lambda e, qsl=qsl, rd=rd, acc=acc: e.tensor_tensor(out=osb[:, qsl], in0=acc[:, 0:BLK], in1=rd[:, :], op=ALU.mult),
                     reads=[accb, rdb], writes=[osbb])
            yield
        P.dma("sp", oT_out[h * HD:(h + 1) * HD, :], osb[:, :], reads=[osbb], is_output=True)
        mstate["busy"] = False

    mstate = {"busy": False}
    accbufs = [Buf(), Buf()]
    active = []
    next_h = 0
    free_slots = list(range(NSLOT))
    while next_h < nheads or active:
        while free_slots and next_h < nheads:
            sl_ = free_slots.pop(0)
            active.append((head_gen(next_h, slots[sl_]), sl_))
            next_h += 1
        for item in list(active):
            try:
                next(item[0])
            except StopIteration:
                active.remove(item)
                free_slots.append(item[1])
    qr, kr = slots[0]["qr"], slots[0]["kr"]
    gm, mx8, bias_f, biasT = slots[0]["gm"], slots[0]["mx8"], slots[0]["bias_f"], slots[0]["biasT"]
    kmT = slots[0]["kmT"]
    v_sb = [slots[0]["v"], slots[-1]["v"]]
    if dbg:
        P.barrier()
        for nm, t, shp, dt in (("d_sin", sinT, [128, S_LEN], F32), ("d_cos", cosT, [128, S_LEN], F32), ("d_qr", qr, [128, S_LEN], BF16),
                               ("d_kr", kr, [128, S_LEN], BF16), ("d_gm", gm, [128, 16 * NB], F32), ("d_mx8", mx8, [128, 128], F32),
                               ("d_biasf", bias_f, [128, 128], F32), ("d_biasT", biasT, [8, S_LEN], BF16), ("d_rt", rt_f, [128, 128], F32),
                               ("d_cm", cm_f, [128, 512], F32), ("d_v0", v_sb[0], [128, 2048], BF16), ("d_v1", v_sb[1], [128, 2048], BF16), ("d_oh", oh_f, [8, 8 * 128], F32), ("d_kmT", kmT, [128, 8], F32)):
            d = C.dram_out(nm, shp, dt)
            src = t
            if len(t.shape) == 3:
                src = t[:, :, :].rearrange("p a b -> p (a b)")
            else:
                src = t[:, :]
            P.dma("sp", d, src, is_output=True)
    return


def mixer_program(do_mlstm=True, do_ssd=True):
    C = Ctx()
    mixer_phase(C, do_mlstm, do_ssd)
    return C.finish()


def mixer_phase(C, do_mlstm=True, do_ssd=True):
    P = C.P
    S = S_LEN
    NCH = S // 128
    MH, DK, DV = 4, 128, 256
    SG, SR, SP_, SN = 4, 4, 64, 128
    SHL = SG * SR
    mixT_m = C.dram_out("mixT_m", [MH * DV, S], BF16) if do_mlstm else None
    mixT_s = C.dram_out("mixT_s", [SG * SR * SP_, S], BF16) if do_ssd else None
    make_consts(C)
    cb = Buf("consts")
    ones_f = C.sb([128, 128], F32, "onesf")
    tri_f = C.sb([128, 128], F32, "trif")
    tri_b = C.sb([128, 128], BF16, "trib")
    negm_f = C.sb([128, 128], F32, "negmf")
    P.op("pool", lambda e: e.memset(ones_f[:, :], 1.0), writes=[cb])
    P.op("pool", lambda e: e.memset(tri_f[:, :], 1.0), reads=[cb], writes=[cb])
    P.op("pool", lambda e: e.affine_select(out=tri_f[:, :], in_=tri_f[:, :], pattern=[[1, 128]], compare_op=ALU.is_ge,
                                           fill=0.0, base=0, channel_multiplier=-1), reads=[cb], writes=[cb])
    P.op("pool", lambda e: e.tensor_copy(out=tri_b[:, :], in_=tri_f[:, :]), reads=[cb], writes=[cb])
    P.op("pool", lambda e: e.tensor_scalar(out=negm_f[:, :], in0=tri_f[:, :], scalar1=-1.0, scalar2=30000.0, op0=ALU.add, op1=ALU.mult),
         reads=[cb], writes=[cb])
    mm_ring = Ring([C.ps([128, 512], F32, "mm") for _ in range(6)])
    tp_ring = Ring([C.ps([128, 512], BF16, "tp") for _ in range(2)])
    big_ring = mm_ring
    ost = [C.sb([128, 2, S], BF16, "ost") for _ in range(2)]
    ost_b = [Buf(), Buf()]
    osti = 0

    def bcast_load(vec_ap, n, name):
        t = C.sb([128, n], F32, name)
        b = Buf()
        P.dma("sp", t[:, :], vec_ap.partition_broadcast(128), writes=[b])
        return t, b

    def softplus_inplace(t_ap, b, sign):
        P.op("act", lambda e: e.activation(out=t_ap, in_=t_ap, func=AF.Exp, scale=float(sign)), reads=[b], writes=[b])
        P.op("act", lambda e: e.activation(out=t_ap, in_=t_ap, func=AF.Ln, bias=1.0), reads=[b], writes=[b])

    if do_mlstm:
        qT_in = C.dram_in("m_qT", [MH * DK, S], BF16)
        kT_in = C.dram_in("m_kT", [MH * DK, S], BF16)
        v_in = C.dram_in("m_v", [S, MH * DV], BF16)
        o_in = C.dram_in("m_o", [S, MH * DV], BF16)
        gi_in = C.dram_in("m_ig", [S, MH], F32)
        gf_in = C.dram_in("m_fg", [S, MH], F32)
        gb_in = C.dram_in("m_gbias", [2 * MH], F32)
        nrm_in = C.dram_in("m_norm", [MH * DV], F32)
        gbias, gbias_b = bcast_load(gb_in, 2 * MH, "gbias")
        mnorm, mnorm_b = bcast_load(nrm_in, MH * DV, "mnorm")
        gates = C.sb([128, NCH, 2 * MH], F32, "gates")
        gt_b = Buf()
        P.dma("sp", gates[:, :, 0:MH], gi_in.rearrange("(c p) g -> p c g", p=128), writes=[gt_b])
        P.dma("sp", gates[:, :, MH:2 * MH], gf_in.rearrange("(c p) g -> p c g", p=128), writes=[gt_b])
        P.op("dve", lambda e: e.tensor_tensor(out=gates[:, :, :], in0=gates[:, :, :],
                                              in1=gbias[:, :].unsqueeze(1).to_broadcast([128, NCH, 2 * MH]), op=ALU.add),
             reads=[gt_b, gbias_b], writes=[gt_b])
        logf = C.sb([128, NCH, MH], F32, "logf")
        lf_b = Buf()
        P.op("dve", lambda e: e.tensor_copy(out=logf[:, :, :], in_=gates[:, :, MH:2 * MH]), reads=[gt_b], writes=[lf_b])
        softplus_inplace(logf[:, :, :], lf_b, -1.0)
        P.op("dve", lambda e: e.tensor_scalar(out=logf[:, :, :], in0=logf[:, :, :], scalar1=-1.0, scalar2=None, op0=ALU.mult),
             reads=[lf_b], writes=[lf_b])
        pa, pab = big_ring.next()
        P.op("pe", lambda e: e.matmul(pa[:, 0:NCH * MH], lhsT=tri_f[:, :], rhs=logf[:, :, :].rearrange("p c h -> p (c h)"),
                                      start=True, stop=True), reads=[cb, lf_b], writes=[pab])
        pt_, ptb = big_ring.next()
        P.op("pe", lambda e: e.matmul(pt_[:, 0:NCH * MH], lhsT=ones_f[:, :], rhs=logf[:, :, :].rearrange("p c h -> p (c h)"),
                                      start=True, stop=True), reads=[cb, lf_b], writes=[ptb])
        rowfac = C.sb([128, NCH, MH], F32, "rowfac")
        colfac = C.sb([128, NCH, MH], F32, "colfac")
        eaL = C.sb([128, NCH, MH], F32, "eaL")
        fac_b = Buf()
        P.op("act", lambda e: e.activation(out=rowfac[:, :, :].rearrange("p c h -> p (c h)"), in_=pa[:, 0:NCH * MH], func=AF.Exp),
             reads=[pab], writes=[fac_b])
        P.op("dve", lambda e: e.tensor_scalar(out=rowfac[:, :, :], in0=rowfac[:, :, :], scalar1=float(DK) ** -0.5, scalar2=None, op0=ALU.mult),
             reads=[fac_b], writes=[fac_b])
        P.op("dve", lambda e: e.tensor_scalar(out=colfac[:, :, :].rearrange("p c h -> p (c h)"), in0=pa[:, 0:NCH * MH], scalar1=-1.0,
                                              scalar2=None, op0=ALU.mult), reads=[pab, fac_b], writes=[fac_b])
        P.op("dve", lambda e: e.tensor_tensor(out=colfac[:, :, :], in0=colfac[:, :, :], in1=gates[:, :, 0:MH], op=ALU.add),
             reads=[fac_b, gt_b], writes=[fac_b])
        P.op("act", lambda e: e.activation(out=colfac[:, :, :], in_=colfac[:, :, :], func=AF.Exp), reads=[fac_b], writes=[fac_b])
        P.op("act", lambda e: e.activation(out=eaL[:, :, :].rearrange("p c h -> p (c h)"), in_=pt_[:, 0:NCH * MH], func=AF.Exp),
             reads=[ptb, fac_b], writes=[fac_b])

        qT_t = [C.sb([128, S], BF16, "mqT") for _ in range(2)]
        kT_t = [C.sb([128, S], BF16, "mkT") for _ in range(2)]
        vx_t = [C.sb([128, NCH, DV + 1], BF16, "mvx") for _ in range(2)]
        o_t = [C.sb([128, NCH, DV], BF16, "mo") for _ in range(2)]
        ld_b = [[Buf() for _ in range(4)] for _ in range(2)]
        for i in range(2):
            P.op("pool", lambda e, i=i: e.memset(vx_t[i][:, :, DV:DV + 1], 1.0), writes=[ld_b[i][2]])
        cst = [C.sb([128, DV + 1], F32, "cst") for _ in range(2)]
        cst_bf = C.sb([128, DV + 1], BF16, "cstbf")
        cst_b = [Buf(), Buf()]
        cstbf_b = Buf()
        kcs = [C.sb([128, DK], BF16, "kcs") for _ in range(2)]
        kcs_b = [Buf(), Buf()]
        pTm = [C.sb([128, 128], BF16, "pTm") for _ in range(2)]
        pTm_b = [Buf(), Buf()]
        hbuf = [C.sb([128, DV], F32, "hbuf") for _ in range(2)]
        hbuf_b = [Buf(), Buf()]
        sig = [C.sb([128, DV], F32, "sig") for _ in range(2)]
        sig_b = [Buf(), Buf()]
        sm = [C.sb([128, 8], F32, "sm") for _ in range(2)]
        sm_b = [Buf(), Buf()]
        hjunk = C.sb([128, DV], BF16, "hjunk")
        hjunk_b = Buf()
        hout = [C.sb([128, DV], BF16, "hout") for _ in range(2)]
        hout_b = [Buf(), Buf()]
        it = 0
        sgall = C.sb([128, NCH, DV], F32, "sgall")
        sgall_b = Buf()
        for hh in range(MH):
            hb_ = hh % 2
            qt, kt, vx, ot = qT_t[hb_], kT_t[hb_], vx_t[hb_], o_t[hb_]
            lb = ld_b[hb_]
            P.dma("sp", qt[:, :], qT_in[hh * DK:(hh + 1) * DK, :], writes=[lb[0]])
            P.dma("sp", kt[:, :], kT_in[hh * DK:(hh + 1) * DK, :], writes=[lb[1]])
            P.dma("sp", vx[:, :, 0:DV], v_in[:, hh * DV:(hh + 1) * DV].rearrange("(c p) d -> p c d", p=128), writes=[lb[2]])
            P.dma("sp", ot[:, :, :], o_in[:, hh * DV:(hh + 1) * DV].rearrange("(c p) d -> p c d", p=128), writes=[lb[3]])
            os_t, os_b = ost[osti % 2], ost_b[osti % 2]
            osti += 1
            P.op("act", lambda e, ot=ot: e.activation(out=sgall[:, :, :].rearrange("p c d -> p (c d)"), in_=ot[:, :, :].rearrange("p c d -> p (c d)"),
                                                     func=AF.Sigmoid), reads=[lb[3]], writes=[sgall_b])
            P.op("pool", lambda e, hh=hh: e.tensor_tensor(out=sgall[:, :, :], in0=sgall[:, :, :],
                                                          in1=mnorm[:, hh * DV:(hh + 1) * DV].unsqueeze(1).to_broadcast([128, NCH, DV]), op=ALU.mult),
                 reads=[sgall_b, mnorm_b], writes=[sgall_b])
            for c in range(NCH):
                csl = slice(c * 128, (c + 1) * 128)
                i2 = it % 2
                it += 1
                tp, tpb = tp_ring.next()
                P.op("pe", lambda e, tp=tp, kt=kt, csl=csl: e.transpose(tp[:, 0:128], kt[:, csl], C.ident_b[:, :]),
                     reads=[lb[1], C.ident_buf], writes=[tpb])
                P.op("dve", lambda e, tp=tp, i2=i2, c=c, hh=hh: e.tensor_scalar(out=kcs[i2][:, :], in0=tp[:, 0:128],
                                                                            scalar1=colfac[:, c, hh:hh + 1], scalar2=None, op0=ALU.mult),
                     reads=[tpb, fac_b], writes=[kcs_b[i2]])
                pg, pgb = mm_ring.next()
                P.op("pe", lambda e, pg=pg, kt=kt, qt=qt, csl=csl: e.matmul(pg[:, 0:128], lhsT=kt[:, csl], rhs=qt[:, csl], start=True, stop=True),
                     reads=[lb[0], lb[1]], writes=[pgb])
                P.op("dve", lambda e, pg=pg, i2=i2, c=c, hh=hh: e.scalar_tensor_tensor(
                    out=pTm[i2][:, :], in0=pg[:, 0:128], scalar=colfac[:, c, hh:hh + 1], in1=tri_f[:, :], op0=ALU.mult, op1=ALU.mult),
                    reads=[pgb, fac_b, cb], writes=[pTm_b[i2]])
                pn, pnb = mm_ring.next()
                P.op("pe", lambda e, pn=pn, i2=i2, vx=vx, c=c: e.matmul(pn[:, 0:DV + 1], lhsT=pTm[i2][:, :], rhs=vx[:, c, :],
                                                                    start=True, stop=(c == 0)), reads=[pTm_b[i2], lb[2]], writes=[pnb])
                if c > 0:
                    P.op("pe", lambda e, pn=pn, qt=qt, csl=csl: e.matmul(pn[:, 0:DV + 1], lhsT=qt[:, csl], rhs=cst_bf[:, :], start=False, stop=True),
                         reads=[lb[0], cstbf_b], writes=[pnb])
                pu, pub = mm_ring.next()
                P.op("pe", lambda e, pu=pu, i2=i2, vx=vx, c=c: e.matmul(pu[:, 0:DV + 1], lhsT=kcs[i2][:, :], rhs=vx[:, c, :], start=True, stop=True),
                     reads=[kcs_b[i2], lb[2]], writes=[pub])
                s_t, s_b = sm[i2], sm_b[i2]
                P.op("act", lambda e, s_t=s_t, pn=pn, c=c, hh=hh: e.activation(
                    out=s_t[:, 0:1], in_=pn[:, DV:DV + 1], func=AF.Abs, scale=rowfac[:, c, hh:hh + 1]),
                    reads=[pnb, fac_b], writes=[s_b])
                P.op("dve", lambda e, s_t=s_t: e.tensor_scalar(out=s_t[:, 0:1], in0=s_t[:, 0:1], scalar1=1.0, scalar2=None, op0=ALU.max),
                     reads=[s_b], writes=[s_b])
                P.op("dve", lambda e, s_t=s_t: e.reciprocal(out=s_t[:, 1:2], in_=s_t[:, 0:1]), reads=[s_b], writes=[s_b])
                P.op("dve", lambda e, s_t=s_t, c=c, hh=hh: e.tensor_tensor(out=s_t[:, 2:3], in0=s_t[:, 1:2], in1=rowfac[:, c, hh:hh + 1], op=ALU.mult),
                     reads=[s_b, fac_b], writes=[s_b])
                hb2, hbb = hbuf[i2], hbuf_b[i2]
                P.op("act", lambda e, hb2=hb2, pn=pn, s_t=s_t: e.activation(out=hb2[:, :], in_=pn[:, 0:DV], func=AF.Copy, scale=s_t[:, 2:3]),
                     reads=[pnb, s_b], writes=[hbb])
                P.op("dve", lambda e, s_t=s_t: e.memset(s_t[:, 3:4], 0.0), reads=[s_b], writes=[s_b])
                P.op("act", lambda e, hb2=hb2, s_t=s_t: e.activation(out=hjunk[:, :], in_=hb2[:, :], func=AF.Square, accum_out=s_t[:, 3:4]),
                     reads=[hbb, s_b], writes=[s_b, hjunk_b])
                P.op("dve", lambda e, s_t=s_t: e.tensor_scalar(out=s_t[:, 4:5], in0=s_t[:, 3:4], scalar1=1.0 / DV, scalar2=1e-6, op0=ALU.mult, op1=ALU.add),
                     reads=[s_b], writes=[s_b])
                P.op("act", lambda e, s_t=s_t: e.activation(out=s_t[:, 5:6], in_=s_t[:, 4:5], func=AF.Ln), reads=[s_b], writes=[s_b])
                P.op("act", lambda e, s_t=s_t: e.activation(out=s_t[:, 6:7], in_=s_t[:, 5:6], func=AF.Exp, scale=-0.5), reads=[s_b], writes=[s_b])
                ho, hob = hout[i2], hout_b[i2]
                P.op("dve", lambda e, ho=ho, hb2=hb2, s_t=s_t, c=c: e.scalar_tensor_tensor(
                    out=ho[:, :], in0=hb2[:, :], scalar=s_t[:, 6:7], in1=sgall[:, c, :], op0=ALU.mult, op1=ALU.mult),
                    reads=[hbb, s_b, sgall_b], writes=[hob])
                tp2, tp2b = tp_ring.next()
                for j in range(2):
                    P.op("pe", lambda e, tp2=tp2, ho=ho, j=j: e.transpose(tp2[:, j * 128:(j + 1) * 128], ho[:, j * 128:(j + 1) * 128], C.ident_b[:, :]),
                         reads=[hob, C.ident_buf], writes=[tp2b])
                P.op("act", lambda e, tp2=tp2, os_t=os_t, csl=csl: e.activation(out=os_t[:, :, csl], in_=tp2[:, 0:256].rearrange("p (a b) -> p a b", a=2),
                                                                         func=AF.Copy), reads=[tp2b], writes=[os_b])
                if c < NCH - 1:
                    if c == 0:
                        P.op("dve", lambda e, pu=pu: e.tensor_copy(out=cst[1][:, :], in_=pu[:, 0:DV + 1]), reads=[pub], writes=[cst_b[1]])
                    else:
                        P.op("dve", lambda e, pu=pu: e.tensor_tensor(out=cst[1][:, :], in0=pu[:, 0:DV + 1], in1=cst[0][:, :], op=ALU.add),
                             reads=[pub, cst_b[0]], writes=[cst_b[1]])
                    P.op("act", lambda e, c=c, hh=hh: e.activation(out=cst[0][:, :], in_=cst[1][:, :], func=AF.Copy, scale=eaL[:, c, hh:hh + 1]),
                         reads=[cst_b[1], fac_b], writes=[cst_b[0]])
                    P.op("act", lambda e, c=c, hh=hh: e.activation(out=cst_bf[:, :], in_=cst[1][:, :], func=AF.Copy, scale=eaL[:, c, hh:hh + 1]),
                         reads=[cst_b[1], fac_b], writes=[cstbf_b])
            P.dma("sp", mixT_m[hh * DV:(hh + 1) * DV, :].rearrange("(a p) t -> p a t", p=128), os_t[:, :, :], reads=[os_b], is_output=True)

    if do_ssd:
        XC = SG * SR * SP_
        NCC = (XC + 2 * SG * SN) // 128
        sxT_in = C.dram_in("s_xT", [XC, S], BF16)
        sBT_in = C.dram_in("s_BT", [SG * SN, S], BF16)
        sCT_in = C.dram_in("s_CT", [SG * SN, S], BF16)

        def xbc_rows(cc):
            if cc < 8:
                return sxT_in[cc * 128:(cc + 1) * 128, :]
            if cc < 12:
                return sBT_in[(cc - 8) * 128:(cc - 7) * 128, :]
            return sCT_in[(cc - 12) * 128:(cc - 11) * 128, :]

        cw_in = C.dram_in("s_convw", [4, NCC * 128], F32)
        cbias_in = C.dram_in("s_convb", [NCC * 128], F32)
        z_in = C.dram_in("s_z", [S, XC], BF16)
        dt_in = C.dram_in("s_dt", [S, SHL], F32)
        dtb_in = C.dram_in("s_dtb", [SHL], F32)
        alog_in = C.dram_in("s_alog", [SHL], F32)
        dd_in = C.dram_in("s_d", [SHL], F32)
        snorm_in = C.dram_in("s_norm", [XC], F32)
        dtb, dtb_b = bcast_load(dtb_in, SHL, "dtb")
        aneg, aneg_b = bcast_load(alog_in, SHL, "aneg")
        dcol, dcol_b = bcast_load(dd_in, SHL, "dcol")
        snorm, snorm_b = bcast_load(snorm_in, XC, "snorm")
        P.op("act", lambda e: e.activation(out=aneg[:, :], in_=aneg[:, :], func=AF.Exp), reads=[aneg_b], writes=[aneg_b])
        P.op("dve", lambda e: e.tensor_scalar(out=aneg[:, :], in0=aneg[:, :], scalar1=-1.0, scalar2=None, op0=ALU.mult),
             reads=[aneg_b], writes=[aneg_b])
        cw = []
        for k in range(4):
            cw.append(load_colvec(C, cw_in[k, :], NCC, mm_ring))
        cbias, cbias_b = load_colvec(C, cbias_in, NCC, mm_ring)
        dtv = C.sb([128, NCH, SHL], F32, "dtv")
        dt_b = Buf()
        P.dma("sp", dtv[:, :, :], dt_in.rearrange("(c p) h -> p c h", p=128), writes=[dt_b])
        P.op("dve", lambda e: e.tensor_tensor(out=dtv[:, :, :], in0=dtv[:, :, :], in1=dtb[:, :].unsqueeze(1).to_broadcast([128, NCH, SHL]), op=ALU.add),
             reads=[dt_b, dtb_b], writes=[dt_b])
        softplus_inplace(dtv[:, :, :], dt_b, 1.0)
        da = C.sb([128, NCH, SHL], F32, "da")
        da_b = Buf()
        P.op("dve", lambda e: e.tensor_tensor(out=da[:, :, :], in0=dtv[:, :, :], in1=aneg[:, :].unsqueeze(1).to_broadcast([128, NCH, SHL]), op=ALU.mult),
             reads=[dt_b, aneg_b], writes=[da_b])
        da_hi = C.sb([128, NCH, SHL], BF16, "dahi")
        da_mid = C.sb([128, NCH, SHL], BF16, "damid")
        da_lo = C.sb([128, NCH, SHL], BF16, "dalo")
        da_r = C.sb([128, NCH, SHL], F32, "dar")
        negm_b = C.sb([128, 128], BF16, "negmb")
        P.op("dve", lambda e: e.tensor_copy(out=negm_b[:, :], in_=negm_f[:, :]), reads=[cb], writes=[cb])
        P.op("dve", lambda e: e.tensor_copy(out=da_hi[:, :, :], in_=da[:, :, :]), reads=[da_b], writes=[da_b])
        P.op("dve", lambda e: e.tensor_tensor(out=da_r[:, :, :], in0=da[:, :, :], in1=da_hi[:, :, :], op=ALU.subtract), reads=[da_b], writes=[da_b])
        P.op("dve", lambda e: e.tensor_copy(out=da_mid[:, :, :], in_=da_r[:, :, :]), reads=[da_b], writes=[da_b])
        P.op("dve", lambda e: e.tensor_tensor(out=da_r[:, :, :], in0=da_r[:, :, :], in1=da_mid[:, :, :], op=ALU.subtract), reads=[da_b], writes=[da_b])
        P.op("dve", lambda e: e.tensor_copy(out=da_lo[:, :, :], in_=da_r[:, :, :]), reads=[da_b], writes=[da_b])
        pcs, pcsb = big_ring.next()
        P.op("pe", lambda e: e.matmul(pcs[:, 0:NCH * SHL], lhsT=tri_f[:, :], rhs=da[:, :, :].rearrange("p c h -> p (c h)"), start=True, stop=True),
             reads=[cb, da_b], writes=[pcsb])
        ptot, ptotb = big_ring.next()
        P.op("pe", lambda e: e.matmul(ptot[:, 0:NCH * SHL], lhsT=ones_f[:, :], rhs=da[:, :, :].rearrange("p c h -> p (c h)"), start=True, stop=True),
             reads=[cb, da_b], writes=[ptotb])
        ncs = C.sb([128, NCH, SHL], F32, "ncs")
        ecs = C.sb([128, NCH, SHL], F32, "ecs")
        etot = C.sb([128, NCH, SHL], F32, "etot")
        dend = C.sb([128, NCH, SHL], F32, "dend")
        sc_b = Buf()
        fl = lambda t: t[:, :, :].rearrange("p c h -> p (c h)")
        P.op("dve", lambda e: e.tensor_scalar(out=fl(ncs), in0=pcs[:, 0:NCH * SHL], scalar1=-1.0, scalar2=None, op0=ALU.mult), reads=[pcsb], writes=[sc_b])
        P.op("act", lambda e: e.activation(out=fl(ecs), in_=pcs[:, 0:NCH * SHL], func=AF.Exp), reads=[pcsb, sc_b], writes=[sc_b])
        P.op("act", lambda e: e.activation(out=fl(etot), in_=ptot[:, 0:NCH * SHL], func=AF.Exp), reads=[ptotb, sc_b], writes=[sc_b])
        P.op("dve", lambda e: e.tensor_tensor(out=fl(dend), in0=ptot[:, 0:NCH * SHL], in1=fl(ncs), op=ALU.add), reads=[ptotb, sc_b], writes=[sc_b])
        P.op("act", lambda e: e.activation(out=fl(dend), in_=fl(dend), func=AF.Exp), reads=[sc_b], writes=[sc_b])

        raw = [C.sb([128, S + 4], BF16, "craw") for _ in range(2)]
        raw_b = [Buf(), Buf()]
        for i in range(2):
            P.op("pool", lambda e, i=i: e.memset(raw[i][:, 0:4], 0.0), writes=[raw_b[i]])
        cacc = [C.sb([128, S], F32, "cacc") for _ in range(2)]
        cacc_b = [Buf(), Buf()]
        cout = C.sb([128, S], BF16, "cout")
        cout_b = Buf()
        BT = C.sb([128, S], BF16, "BT")
        CT = C.sb([128, S], BF16, "CT")
        BT_b, CT_b = Buf(), Buf()
        x_tok = C.sb([128, NCH, SR * SP_], BF16, "xtok")
        xtok_b = Buf()
        B_tok = C.sb([128, NCH, SN], BF16, "Btok")
        Btok_b = Buf()
        z_t = C.sb([128, NCH, SR * SP_], BF16, "zt")
        z_b = Buf()
        state = [C.sb([128, SR * SP_], F32, "sst") for _ in range(2)]
        state_b = [Buf(), Buf()]
        state_bf = C.sb([128, SR * SP_], BF16, "sstbf")
        statebf_b = Buf()
        rbc = [C.sb([128, 128], F32, "rbc") for _ in range(2)]
        rbc_b = [Buf(), Buf()]
        Et = [C.sb([128, 128], F32, "Et") for _ in range(2)]
        Et_b = [Buf(), Buf()]
        MT = [C.sb([128, 128], BF16, "MT") for _ in range(2)]
        MT_b = [Buf(), Buf()]
        xd = [C.sb([128, SR * SP_], BF16, "xd") for _ in range(2)]
        xd_b = [Buf(), Buf()]
        xdd = [C.sb([128, SR * SP_], BF16, "xdd") for _ in range(2)]
        xdd_b = [Buf(), Buf()]
        yt = [C.sb([128, SR * SP_], F32, "yt") for _ in range(2)]
        yt_b = [Buf(), Buf()]
        y2 = [C.sb([128, SR * SP_], F32, "y2") for _ in range(2)]
        y2_b = [Buf(), Buf()]
        zs = [C.sb([128, SR * SP_], F32, "zs") for _ in range(2)]
        zs_b = [Buf(), Buf()]
        ssm = [C.sb([128, 8], F32, "ssm") for _ in range(2)]
        ssm_b = [Buf(), Buf()]
        yjunk = C.sb([128, SR * SP_], BF16, "yjunk")
        yjunk_b = Buf()
        yout = [C.sb([128, SR * SP_], BF16, "yout") for _ in range(2)]
        yout_b = [Buf(), Buf()]
        ci = 0
        ri = 0
        gi = 0
        zsall = C.sb([128, NCH, SR * SP_], F32, "zsall")
        zsall_b = Buf()

        def conv_chunk(cc, dst_ap, dst_b):
            nonlocal ci
            i2 = ci % 2
            ci += 1
            r_t, r_b = raw[i2], raw_b[i2]
            a_t, a_b = cacc[i2], cacc_b[i2]
            P.dma("sp", r_t[:, 4:4 + S], xbc_rows(cc), writes=[r_b])
            eng = "dve"
            P.op(eng, lambda e: e.tensor_scalar(out=a_t[:, :], in0=r_t[:, 1:1 + S], scalar1=cw[0][0][:, cc:cc + 1], scalar2=None, op0=ALU.mult),
                 reads=[r_b, cw[0][1]], writes=[a_b])
            for k in range(1, 4):
                P.op(eng, lambda e, k=k: e.scalar_tensor_tensor(out=a_t[:, :], in0=r_t[:, 1 + k:1 + k + S], scalar=cw[k][0][:, cc:cc + 1],
                                                                in1=a_t[:, :], op0=ALU.mult, op1=ALU.add),
                     reads=[r_b, cw[k][1], a_b], writes=[a_b])
            P.op("act", lambda e: e.activation(out=dst_ap, in_=a_t[:, :], func=AF.Silu, bias=cbias[:, cc:cc + 1]),
                 reads=[a_b, cbias_b], writes=[dst_b])

        for g in range(SG):
            h0 = g * SR
            conv_chunk(8 + g, BT[:, :], BT_b)
            conv_chunk(12 + g, CT[:, :], CT_b)
            for half in range(2):
                conv_chunk(2 * g + half, cout[:, :], cout_b)
                for c4 in range(0, NCH, 4):
                    tp, tpb = tp_ring.next()
                    for j in range(4):
                        c = c4 + j
                        P.op("pe", lambda e, tp=tp, j=j, c=c: e.transpose(tp[:, j * 128:(j + 1) * 128], cout[:, c * 128:(c + 1) * 128], C.ident_b[:, :]),
                             reads=[cout_b, C.ident_buf], writes=[tpb])
                    P.op("act", lambda e, tp=tp, c4=c4, half=half: e.activation(
                        out=x_tok[:, c4:c4 + 4, half * 128:(half + 1) * 128], in_=tp[:, :].rearrange("p (a b) -> p a b", a=4), func=AF.Copy),
                        reads=[tpb], writes=[xtok_b])
            for c4 in range(0, NCH, 4):
                tp, tpb = tp_ring.next()
                for j in range(4):
                    c = c4 + j
                    P.op("pe", lambda e, tp=tp, j=j, c=c: e.transpose(tp[:, j * 128:(j + 1) * 128], BT[:, c * 128:(c + 1) * 128], C.ident_b[:, :]),
                         reads=[BT_b, C.ident_buf], writes=[tpb])
                P.op("act", lambda e, tp=tp, c4=c4: e.activation(out=B_tok[:, c4:c4 + 4, :], in_=tp[:, :].rearrange("p (a b) -> p a b", a=4), func=AF.Copy),
                     reads=[tpb], writes=[Btok_b])
            P.dma("sp", z_t[:, :, :], z_in[:, g * SR * SP_:(g + 1) * SR * SP_].rearrange("(c p) d -> p c d", p=128), writes=[z_b])
            P.op("act", lambda e: e.activation(out=zsall[:, :, :].rearrange("p c d -> p (c d)"), in_=z_t[:, :, :].rearrange("p c d -> p (c d)"), func=AF.Silu),
                 reads=[z_b], writes=[zsall_b])
            os_t, os_b = ost[osti % 2], ost_b[osti % 2]
            osti += 1
            for c in range(NCH):
                csl = slice(c * 128, (c + 1) * 128)
                i2 = gi % 2
                gi += 1
                P.op("pool", lambda e, i2=i2, c=c, h0=h0: e.tensor_tensor(
                    out=xd[i2][:, :].rearrange("p (r d) -> p r d", r=SR), in0=x_tok[:, c, :].rearrange("p (r d) -> p r d", r=SR),
                    in1=dtv[:, c, h0:h0 + SR].unsqueeze(2).to_broadcast([128, SR, SP_]), op=ALU.mult),
                    reads=[xtok_b, dt_b], writes=[xd_b[i2]])
                P.op("pool", lambda e, i2=i2, c=c, h0=h0: e.tensor_tensor(
                    out=xdd[i2][:, :].rearrange("p (r d) -> p r d", r=SR), in0=xd[i2][:, :].rearrange("p (r d) -> p r d", r=SR),
                    in1=dend[:, c, h0:h0 + SR].unsqueeze(2).to_broadcast([128, SR, SP_]), op=ALU.mult),
                    reads=[xd_b[i2], sc_b], writes=[xdd_b[i2]])
                bkA, bkAb = mm_ring.tiles[2 * i2], mm_ring.bufs[2 * i2]
                bkB, bkBb = mm_ring.tiles[2 * i2 + 1], mm_ring.bufs[2 * i2 + 1]
                pcb, pcbb = bkA[:, 0:128], bkAb
                P.op("pe", lambda e, pcb=pcb, csl=csl: e.matmul(pcb, lhsT=BT[:, csl], rhs=CT[:, csl], start=True, stop=True),
                     reads=[BT_b, CT_b], writes=[pcbb])
                pyo, pyob = bkA[:, 128:128 + SR * SP_], bkAb
                if c > 0:
                    P.op("pe", lambda e, pyo=pyo, csl=csl: e.matmul(pyo, lhsT=CT[:, csl], rhs=state_bf[:, :], start=True, stop=True),
                         reads=[CT_b, statebf_b], writes=[pyob])
                pyd, pydb = bkB[:, 0:SR * SP_], bkBb
                for r in range(SR):
                    hh = h0 + r
                    j2 = ri % 2
                    ri += 1
                    pe_, peb = mm_ring.tiles[4 + j2], mm_ring.bufs[4 + j2]
                    for di, dpart in enumerate((da_hi, da_mid, da_lo)):
                        P.op("pe", lambda e, pe_=pe_, c=c, hh=hh, dpart=dpart, di=di: e.matmul(
                            pe_[:, 0:128], lhsT=dpart[:, c, hh:hh + 1].to_broadcast([128, 128]), rhs=tri_b[:, :],
                            start=(di == 0), stop=False), reads=[cb, da_b], writes=[peb])
                    P.op("pe", lambda e, pe_=pe_: e.matmul(pe_[:, 0:128], lhsT=C.ident_b[:, :], rhs=negm_b[:, :], start=False, stop=True),
                         reads=[cb, C.ident_buf], writes=[peb])
                    P.op("act", lambda e, pe_=pe_, j2=j2, c=c, hh=hh: e.activation(out=Et[j2][:, :], in_=pe_[:, 0:128], func=AF.Exp, bias=ncs[:, c, hh:hh + 1]),
                         reads=[peb, sc_b], writes=[Et_b[j2]])
                    P.op("dve", lambda e, j2=j2, pcb=pcb: e.tensor_tensor(out=MT[j2][:, :], in0=Et[j2][:, :], in1=pcb, op=ALU.mult),
                         reads=[Et_b[j2], pcbb], writes=[MT_b[j2]])
                    P.op("pe", lambda e, pyd=pyd, j2=j2, i2=i2, r=r: e.matmul(pyd[:, r * SP_:(r + 1) * SP_], lhsT=MT[j2][:, :], rhs=xd[i2][:, r * SP_:(r + 1) * SP_],
                                                                          start=True, stop=True), reads=[MT_b[j2], xd_b[i2]], writes=[pydb])
                pu, pub = bkB[:, SR * SP_:2 * SR * SP_], bkBb
                P.op("pe", lambda e, pu=pu, c=c, i2=i2: e.matmul(pu, lhsT=B_tok[:, c, :], rhs=xdd[i2][:, :], start=True, stop=True),
                     reads=[Btok_b, xdd_b[i2]], writes=[pub])
                y_t, y_b = yt[i2], yt_b[i2]
                P.op("dve", lambda e, y_t=y_t, c=c, h0=h0: e.tensor_tensor(
                    out=y_t[:, :].rearrange("p (r d) -> p r d", r=SR), in0=x_tok[:, c, :].rearrange("p (r d) -> p r d", r=SR),
                    in1=dcol[:, h0:h0 + SR].unsqueeze(2).to_broadcast([128, SR, SP_]), op=ALU.mult), reads=[xtok_b, dcol_b], writes=[y_b])
                if c > 0:
                    y2t, y2b = y2[i2], y2_b[i2]
                    P.op("dve", lambda e, y2t=y2t, pyo=pyo, c=c, h0=h0: e.tensor_tensor(
                        out=y2t[:, :].rearrange("p (r d) -> p r d", r=SR), in0=pyo.rearrange("p (r d) -> p r d", r=SR),
                        in1=ecs[:, c, h0:h0 + SR].unsqueeze(2).to_broadcast([128, SR, SP_]), op=ALU.mult), reads=[pyob, sc_b], writes=[y2b])
                    P.op("pool", lambda e, y_t=y_t, y2t=y2t: e.tensor_tensor(out=y_t[:, :], in0=y_t[:, :], in1=y2t[:, :], op=ALU.add),
                         reads=[y_b, y2b], writes=[y_b])
                P.op("dve", lambda e, y_t=y_t, pyd=pyd: e.tensor_tensor(out=y_t[:, :], in0=pyd, in1=y_t[:, :], op=ALU.add),
                     reads=[pydb, y_b], writes=[y_b])
                P.op("pool", lambda e, y_t=y_t, c=c: e.tensor_tensor(out=y_t[:, :], in0=y_t[:, :], in1=zsall[:, c, :], op=ALU.mult),
                     reads=[y_b, zsall_b], writes=[y_b])
                s_t, s_b = ssm[i2], ssm_b[i2]
                P.op("dve", lambda e, s_t=s_t: e.memset(s_t[:, 0:1], 0.0), writes=[s_b])
                P.op("act", lambda e, s_t=s_t, y_t=y_t: e.activation(out=yjunk[:, :], in_=y_t[:, :], func=AF.Square, accum_out=s_t[:, 0:1]),
                     reads=[y_b, s_b], writes=[s_b, yjunk_b])
                P.op("dve", lambda e, s_t=s_t: e.tensor_scalar(out=s_t[:, 1:2], in0=s_t[:, 0:1], scalar1=1.0 / (SR * SP_), scalar2=1e-6, op0=ALU.mult, op1=ALU.add),
                     reads=[s_b], writes=[s_b])
                P.op("act", lambda e, s_t=s_t: e.activation(out=s_t[:, 2:3], in_=s_t[:, 1:2], func=AF.Ln), reads=[s_b], writes=[s_b])
                P.op("act", lambda e, s_t=s_t: e.activation(out=s_t[:, 3:4], in_=s_t[:, 2:3], func=AF.Exp, scale=-0.5), reads=[s_b], writes=[s_b])
                yo, yob = yout[i2], yout_b[i2]
                P.op("dve", lambda e, yo=yo, y_t=y_t, s_t=s_t, g=g: e.scalar_tensor_tensor(
                    out=yo[:, :], in0=y_t[:, :], scalar=s_t[:, 3:4], in1=snorm[:, g * SR * SP_:(g + 1) * SR * SP_], op0=ALU.mult, op1=ALU.mult),
                    reads=[y_b, s_b, snorm_b], writes=[yob])
                tp2, tp2b = tp_ring.next()
                for j in range(2):
                    P.op("pe", lambda e, tp2=tp2, yo=yo, j=j: e.transpose(tp2[:, j * 128:(j + 1) * 128], yo[:, j * 128:(j + 1) * 128], C.ident_b[:, :]),
                         reads=[yob, C.ident_buf], writes=[tp2b])
                P.op("act", lambda e, tp2=tp2, os_t=os_t, csl=csl: e.activation(out=os_t[:, :, csl], in_=tp2[:, 0:256].rearrange("p (a b) -> p a b", a=2),
                                                                         func=AF.Copy), reads=[tp2b], writes=[os_b])
                if c < NCH - 1:
                    if c == 0:
                        P.op("dve", lambda e, pu=pu: e.tensor_copy(out=state[0][:, :], in_=pu), reads=[pub], writes=[state_b[0]])
                    else:
                        P.op("pool", lambda e, c=c, h0=h0: e.tensor_tensor(
                            out=state[1][:, :].rearrange("p (r d) -> p r d", r=SR), in0=state[0][:, :].rearrange("p (r d) -> p r d", r=SR),
                            in1=etot[:, c, h0:h0 + SR].unsqueeze(2).to_broadcast([128, SR, SP_]), op=ALU.mult), reads=[state_b[0], sc_b], writes=[state_b[1]])
                        P.op("dve", lambda e, pu=pu: e.tensor_tensor(out=state[0][:, :], in0=pu, in1=state[1][:, :], op=ALU.add),
                             reads=[pub, state_b[1]], writes=[state_b[0]])
                    P.op("act", lambda e: e.activation(out=state_bf[:, :], in_=state[0][:, :], func=AF.Copy), reads=[state_b[0]], writes=[statebf_b])
            P.dma("sp", mixT_s[g * SR * SP_:(g + 1) * SR * SP_, :].rearrange("(a p) t -> p a t", p=128), os_t[:, :, :],
                  reads=[os_b], is_output=True)
    return


C_Q, C_K, C_V, C_O, C_IG, C_FG, C_Z, C_XBC, C_DT = 0, 1024, 2048, 4096, 6144, 6152, 6160, 8208, 12304


def _xbc_cols(g):
    return np.concatenate([np.arange(g * 1024, (g + 1) * 1024), 2048 + np.arange(g * 512, (g + 1) * 512),
                           3072 + np.arange(g * 512, (g + 1) * 512)])


def mixer_params(inp, g):
    gb = np.asarray(inp["l0_mlstm_gate_bias"], np.float32)
    xc = _xbc_cols(g)
    return {
        "m_gbias": np.ascontiguousarray(np.concatenate([gb[4 * g:4 * g + 4], gb[8 + 4 * g:8 + 4 * g + 4]])),
        "m_norm": np.ascontiguousarray(np.asarray(inp["l0_mlstm_norm"], np.float32)[g * 1024:(g + 1) * 1024]),
        "s_convw": np.ascontiguousarray(np.asarray(inp["l0_ssd_conv_w"], np.float32)[:, xc]),
        "s_convb": np.ascontiguousarray(np.asarray(inp["l0_ssd_conv_b"], np.float32)[xc]),
        "s_dtb": np.ascontiguousarray(np.asarray(inp["l0_ssd_dt_bias"], np.float32)[16 * g:16 * g + 16]),
        "s_alog": np.ascontiguousarray(np.asarray(inp["l0_ssd_a_log"], np.float32)[16 * g:16 * g + 16]),
        "s_d": np.ascontiguousarray(np.asarray(inp["l0_ssd_d"], np.float32)[16 * g:16 * g + 16]),
        "s_norm": np.ascontiguousarray(np.asarray(inp["l0_ssd_norm"], np.float32)[g * 1024:(g + 1) * 1024]),
    }


def mixer_inputs(proj, inp, g):
    pb = proj.astype(NPBF)
    d = mixer_params(inp, g)
    d.update({
        "m_qT": np.ascontiguousarray(pb[:, C_Q + g * 512:C_Q + (g + 1) * 512].T),
        "m_kT": np.ascontiguousarray(pb[:, C_K + g * 512:C_K + (g + 1) * 512].T),
        "m_v": np.ascontiguousarray(pb[:, C_V + g * 1024:C_V + (g + 1) * 1024]),
        "m_o": np.ascontiguousarray(pb[:, C_O + g * 1024:C_O + (g + 1) * 1024]),
        "m_ig": np.ascontiguousarray(proj[:, C_IG + 4 * g:C_IG + 4 * g + 4]),
        "m_fg": np.ascontiguousarray(proj[:, C_FG + 4 * g:C_FG + 4 * g + 4]),
        "s_xT": np.ascontiguousarray(pb[:, C_XBC + g * 1024:C_XBC + (g + 1) * 1024].T),
        "s_BT": np.ascontiguousarray(pb[:, C_XBC + 2048 + g * 512:C_XBC + 2048 + (g + 1) * 512].T),
        "s_CT": np.ascontiguousarray(pb[:, C_XBC + 3072 + g * 512:C_XBC + 3072 + (g + 1) * 512].T),
        "s_z": np.ascontiguousarray(pb[:, C_Z + g * 1024:C_Z + (g + 1) * 1024]),
        "s_dt": np.ascontiguousarray(proj[:, C_DT + 16 * g:C_DT + 16 * g + 16]),
    })
    return d


def fused_program():
    C = Ctx()
    S = S_LEN
    x_in = C.nc.dram_tensor("x", [S, DM], F32, kind="ExternalInput").ap()
    y_out = C.nc.dram_tensor("y", [S, DM], F32, kind="ExternalOutput").ap()
    sc = C.scratch
    qT0, kT0 = sc("qT0", [1024, S], BF16), sc("kT0", [1024, S], BF16)
    v0, o0, z0 = sc("v0", [S, 2048], BF16), sc("o0", [S, 2048], BF16), sc("z0", [S, 2048], BF16)
    gates0, dt0 = sc("gates0", [S, 16], F32), sc("dt0", [S, 32], F32)
    xbcT0 = sc("xbcT0", [4096, S], BF16)
    mixT = sc("mixT", [4096, S], BF16)
    x1 = sc("x1", [S, DM], F32)
    qT1, kT1, oT1 = sc("qT1", [2048, S], BF16), sc("kT1", [2048, S], BF16), sc("oT1", [2048, S], BF16)
    v1 = sc("v1", [S, 2048], BF16)
    make_consts(C)
    hs = lambda h: slice(h * NT, (h + 1) * NT)
    for h in range(2):
        C.phase_begin(bind={"x": x_in[hs(h), :], "qT": qT0[:, hs(h)], "kT": kT0[:, hs(h)], "v": v0[hs(h), :], "o": o0[hs(h), :],
                            "gates": gates0[hs(h), :], "z": z0[hs(h), :], "xbcT": xbcT0[:, hs(h)], "dt": dt0[hs(h), :]}, prefix="l0_")
        dense_phase(C, dict(proj=[("qT", C_Q, 1024, "T", BF16), ("kT", C_K, 1024, "T", BF16), ("v", C_V, 2048, "N", BF16),
                                  ("o", C_O, 2048, "N", BF16), ("gates", C_IG, 16, "N", F32), ("z", C_Z, 2048, "N", BF16),
                                  ("xbcT", C_XBC, 4096, "T", BF16), ("dt", C_DT, 32, "N", F32)], wcols=12336))
        C.phase_end()
    for g in range(2):
        C.phase_begin(bind={"m_qT": qT0[g * 512:(g + 1) * 512, :], "m_kT": kT0[g * 512:(g + 1) * 512, :],
                            "m_v": v0[:, g * 1024:(g + 1) * 1024], "m_o": o0[:, g * 1024:(g + 1) * 1024],
                            "m_ig": gates0[:, 4 * g:4 * g + 4], "m_fg": gates0[:, 8 + 4 * g:8 + 4 * g + 4],
                            "s_xT": xbcT0[g * 1024:(g + 1) * 1024, :], "s_BT": xbcT0[2048 + g * 512:2048 + (g + 1) * 512, :],
                            "s_CT": xbcT0[3072 + g * 512:3072 + (g + 1) * 512, :], "s_z": z0[:, g * 1024:(g + 1) * 1024],
                            "s_dt": dt0[:, 16 * g:16 * g + 16],
                            "mixT_m": mixT[g * 1024:(g + 1) * 1024, :], "mixT_s": mixT[2048 + g * 1024:2048 + (g + 1) * 1024, :]},
                      prefix="g%d_" % g)
        mixer_phase(C)
        C.phase_end()
    for h in range(2):
        C.phase_begin(bind={"x": x_in[hs(h), :], "mixT": mixT[:, hs(h)], "x_out": x1[hs(h), :],
                            "qT": qT1[:, hs(h)], "kT": kT1[:, hs(h)], "v": v1[hs(h), :]}, prefix="l0_")
        dense_phase(C, dict(mix=4096, ffn=True, x_out=True, proj_names=("g_qkv", "w_qkv"),
                            proj=[("qT", 0, 2048, "T", BF16), ("kT", 2048, 2048, "T", BF16), ("v", 4096, 2048, "N", BF16)], wcols=6144))
        C.phase_end()
    C.phase_begin(bind={"qT": qT1, "kT": kT1, "v": v1, "oT": oT1})
    attn_phase(C, 16)
    C.phase_end()
    for h in range(2):
        C.phase_begin(bind={"x": x1[hs(h), :], "mixT": oT1[:, hs(h)], "y": y_out[hs(h), :]}, prefix="l1_")
        dense_phase(C, dict(mix=2048, ffn=True, final=True))
        C.phase_end()
    return C.finish()


_NC = None


def kernel(**inp):
    global _NC
    f32 = lambda a: np.ascontiguousarray(np.asarray(a, dtype=np.float32))
    if _NC is None:
        _NC = fused_program()
    x = f32(inp["x"])
    shared = {
        "l0_g_proj": f32(inp["l0_norm_mix"]), "l0_w_proj": f32(inp["l0_w_in"]),
        "l0_w_mix": f32(inp["l0_w_out"]), "l0_g_ffn": f32(inp["l0_norm_ffn"]), "l0_w_gate": f32(inp["l0_ffn_gate"]),
        "l0_w_up": f32(inp["l0_ffn_up"]), "l0_w_down": f32(inp["l0_ffn_down"]),
        "l0_g_qkv": f32(inp["l1_norm_mix"]), "l0_w_qkv": f32(inp["l1_w_qkv"]),
        "l1_w_mix": f32(inp["l1_w_o"]), "l1_g_ffn": f32(inp["l1_norm_ffn"]), "l1_w_gate": f32(inp["l1_ffn_gate"]),
        "l1_w_up": f32(inp["l1_ffn_up"]), "l1_w_down": f32(inp["l1_ffn_down"]), "l1_g_fin": f32(inp["final_norm"]),
    }
    for g in range(2):
        for k, v in mixer_params(inp, g).items():
            shared["g%d_%s" % (g, k)] = v
    owner = [0, 1, 4, 5]
    zero_x = np.zeros_like(x[0])
    in_maps = []
    for c in range(8):
        d = dict(shared)
        d["x"] = x[owner.index(c)] if c in owner else zero_x
        in_maps.append(d)
    res = run_bass_kernel_spmd(_NC, in_maps, core_ids=list(range(8)))
    out = np.stack([res.results[c]["y"] for c in owner], axis=0)
    return np.ascontiguousarray(out.astype(np.float32))
```

```python
import numpy as np
import ml_dtypes
from contextlib import ExitStack
import concourse.bass as bass
import concourse.mybir as mybir
from concourse.bass_utils import run_bass_kernel_spmd

F32 = mybir.dt.float32
BF16 = mybir.dt.bfloat16
AF = mybir.ActivationFunctionType
ALU = mybir.AluOpType
AX = mybir.AxisListType
NPBF = ml_dtypes.bfloat16


class Buf:
    __slots__ = ("name", "w", "r")

    def __init__(self, name=""):
        self.name = name
        self.w = None
        self.r = {}


class Prog:
    COMPUTE = ("pe", "dve", "act", "pool")
    NDMA = 40

    def __init__(self, nc, es):
        self.nc = nc
        self.obj = {"pe": nc.tensor, "dve": nc.vector, "act": nc.scalar, "pool": nc.gpsimd, "sp": nc.sync}
        self.streams = {e: [] for e in self.obj}
        self.cnt = {e: 0 for e in self.COMPUTE}
        self.seen = {e: {} for e in self.obj}
        self.sem = {}
        self.es = es
        self.epoch = 0
        self.key = {}
        for e in self.COMPUTE:
            self.key[e] = (e, 0)
            self.sem[(e, 0)] = es.enter_context(nc.semaphore("s_" + e))
        self.dsem = [es.enter_context(nc.semaphore("s_dma%d" % i)) for i in range(self.NDMA)]
        self.dcnt = [0] * self.NDMA
        self.dnext = 0
        self.dnext_sw = 0
        self.NHW = 24
        self.out_tokens = []

    def _semobj(self, key):
        return self.sem[key] if isinstance(key, tuple) else self.dsem[key]

    def new_epoch(self):
        self.epoch += 1
        for e in self.COMPUTE:
            self.key[e] = (e, self.epoch)
            self.sem[(e, self.epoch)] = self.es.enter_context(self.nc.semaphore("s_%s_%d" % (e, self.epoch)))
            self.cnt[e] = 0

    def _deps(self, eng, reads, writes):
        deps = {}

        def add(t):
            if t is None:
                return
            k, v = t
            if deps.get(k, 0) < v:
                deps[k] = v

        for b in reads:
            add(b.w)
        for b in writes:
            add(b.w)
            for k, v in b.r.items():
                add((k, v))
        return deps

    def _emit_waits(self, eng, deps):
        seen = self.seen[eng]
        for k, v in deps.items():
            if eng == "pe" and isinstance(k, tuple) and k[0] == "pe":
                continue
            if seen.get(k, 0) >= v:
                continue
            seen[k] = v
            self.streams[eng].append(("wait", k, v))

    def _record(self, tok, reads, writes):
        k, v = tok
        for b in reads:
            if b.r.get(k, 0) < v:
                b.r[k] = v
        for b in writes:
            b.w = tok
            b.r = {}

    def op(self, eng, emit, reads=(), writes=()):
        deps = self._deps(eng, reads, writes)
        self._emit_waits(eng, deps)
        self.cnt[eng] += 1
        tok = (self.key[eng], self.cnt[eng])
        self.streams[eng].append(("op", emit, self.key[eng], 1))
        self._record(tok, reads, writes)
        return tok

    def dma(self, eng, out, in_, reads=(), writes=(), is_output=False):
        if eng == "pool":
            s = self.NHW + self.dnext_sw
            self.dnext_sw = (self.dnext_sw + 1) % (self.NDMA - self.NHW)
        else:
            s = self.dnext
            self.dnext = (self.dnext + 1) % self.NHW
        deps = self._deps(eng, reads, writes)
        if self.dcnt[s] > 0:
            deps[s] = max(deps.get(s, 0), self.dcnt[s])
        self._emit_waits(eng, deps)
        self.dcnt[s] += 16
        tok = (s, self.dcnt[s])
        self.streams[eng].append(("op", lambda e: e.dma_start(out=out, in_=in_), s, 16))
        self._record(tok, reads, writes)
        if is_output:
            self.out_tokens.append(tok)
        return tok

    def coll(self, kind, ins, outs, groups, reads=(), writes=()):
        eng = "pool"
        s = self.dnext
        self.dnext = (self.dnext + 1) % self.NDMA
        deps = self._deps(eng, reads, writes)
        if self.dcnt[s] > 0:
            deps[s] = max(deps.get(s, 0), self.dcnt[s])
        self._emit_waits(eng, deps)
        self.dcnt[s] += 16
        tok = (s, self.dcnt[s])
        self.streams[eng].append(("op", lambda e: e.collective_compute(kind, ALU.bypass, replica_groups=groups, ins=ins, outs=outs), s, 16))
        self._record(tok, reads, writes)
        return tok

    def barrier(self, engines=("pe", "dve", "act", "pool", "sp")):
        deps = {self.key[e]: self.cnt[e] for e in self.COMPUTE if self.cnt[e] > 0}
        for s in range(self.NDMA):
            if self.dcnt[s] > 0:
                deps[s] = self.dcnt[s]
        for e in engines:
            d = {k: v for k, v in deps.items() if k != self.key.get(e)}
            self._emit_waits(e, d)

    def finish(self):
        deps = {}
        for s in range(self.NDMA):
            if self.dcnt[s] > 0:
                deps[s] = self.dcnt[s]
        for e in self.COMPUTE:
            if self.cnt[e] > 0:
                deps[self.key[e]] = self.cnt[e]
        self._emit_waits("sp", deps)

    def emit(self, block):
        def run(name):
            def f(e):
                for item in self.streams[name]:
                    if item[0] == "wait":
                        e.wait_ge(self._semobj(item[1]), item[2])
                    else:
                        ins = item[1](e)
                        ins.then_inc(self._semobj(item[2]), item[3])
            return f

        block.tensor(run("pe"))
        block.vector(run("dve"))
        block.scalar(run("act"))
        block.gpsimd(run("pool"))
        block.sync(run("sp"))


class Ctx:
    def __init__(self):
        self.nc = bass.Bass("TRN2", target_bir_lowering=False)
        self.es = ExitStack()
        self.P = Prog(self.nc, self.es)
        self.uid = 0
        self.cur = self.es
        self.bind = {}
        self.prefix = ""
        self.ext = {}
        self.has_consts = False

    def sb(self, shape, dt, name=None):
        self.uid += 1
        return self.cur.enter_context(self.nc.sbuf_tensor("%s_%d" % (name or "t", self.uid), list(shape), dt))

    def ps(self, shape, dt, name=None):
        self.uid += 1
        return self.cur.enter_context(self.nc.psum_tensor("%s_%d" % (name or "p", self.uid), list(shape), dt))

    def dram_in(self, name, shape, dt):
        if name in self.bind:
            ap = self.bind[name]
            assert list(ap.shape) == list(shape), (name, ap.shape, shape)
            return ap
        full = self.prefix + name
        if full not in self.ext:
            self.ext[full] = self.nc.dram_tensor(full, list(shape), dt, kind="ExternalInput").ap()
        return self.ext[full]

    def dram_out(self, name, shape, dt):
        if name in self.bind:
            ap = self.bind[name]
            assert list(ap.shape) == list(shape), (name, ap.shape, shape)
            return ap
        return self.nc.dram_tensor(self.prefix + name, list(shape), dt, kind="ExternalOutput").ap()

    def scratch(self, name, shape, dt):
        return self.nc.dram_tensor(name, list(shape), dt).ap()

    def phase_begin(self, bind=None, prefix=""):
        self.cur = ExitStack()
        self.bind = bind or {}
        self.prefix = prefix

    def phase_end(self):
        self.P.barrier()
        self.P.new_epoch()
        self.cur.close()
        self.cur = self.es
        self.bind = {}
        self.prefix = ""

    def finish(self):
        self.P.finish()
        block = self.es.enter_context(self.nc.Block())
        self.P.emit(block)
        self.es.close()
        return self.nc


class Ring:
    def __init__(self, tiles):
        self.tiles = tiles
        self.bufs = [Buf() for _ in tiles]
        self.i = 0

    def next(self):
        t, b = self.tiles[self.i], self.bufs[self.i]
        self.i = (self.i + 1) % len(self.tiles)
        return t, b


def make_consts(C):
    if C.has_consts:
        return
    C.has_consts = True
    P = C.P
    saved, C.cur = C.cur, C.es
    ident_b = C.sb([128, 128], BF16, "identb")
    ident_f = C.sb([128, 128], F32, "identf")
    C.cur = saved
    bi = Buf("ident")
    P.op("pool", lambda e: e.memset(ident_f[:, :], 0.0), writes=[bi])
    P.op("pool", lambda e: e.affine_select(out=ident_f[:, :], in_=ident_f[:, :], pattern=[[-1, 128]],
                                           compare_op=ALU.not_equal, fill=1.0, base=0, channel_multiplier=1),
         reads=[bi], writes=[bi])
    P.op("pool", lambda e: e.tensor_copy(out=ident_b[:, :], in_=ident_f[:, :]), reads=[bi], writes=[bi])
    C.ident_b, C.ident_f, C.ident_buf = ident_b, ident_f, bi


def load_colvec(C, vec_ap, n, psum_ring):
    P = C.P
    rows = C.sb([n, 128], F32, "cvrows")
    col = C.sb([128, n], F32, "cvcol")
    b_rows, b_col = Buf(), Buf()
    P.dma("sp", rows[:, :], vec_ap.rearrange("(k p) -> k p", p=128), writes=[b_rows])
    pt, pb = psum_ring.next()
    P.op("pe", lambda e: e.matmul(pt[:, 0:n], lhsT=rows[:, :], rhs=C.ident_f[0:n, 0:n], start=True, stop=True),
         reads=[b_rows, C.ident_buf], writes=[pb])
    P.op("dve", lambda e: e.tensor_copy(out=col[:, :], in_=pt[:, 0:n]), reads=[pb], writes=[b_col])
    return col, b_col


def rmsnorm_T(C, x_res, xbufs, gcol, gbuf, xnT, xnT_bufs, ntt, tp_ring, junk_t, junk_b, eps=1e-6, D=2048):
    P = C.P
    nk = D // 128
    ssq = C.sb([128, ntt], F32, "ssq")
    rstd = C.sb([128, ntt], F32, "rstd")
    junk = junk_t[:, 0:D]
    xs = [C.sb([128, D], BF16, "xs") for _ in range(2)]
    b_ssq0, b_rstd = Buf(), Buf()
    b_ssq = [Buf() for _ in range(ntt)]
    b_xs = [Buf(), Buf()]
    P.op("dve", lambda e: e.memset(ssq[:, :], 0.0), writes=[b_ssq0])
    for t in range(ntt):
        P.op("act", lambda e, t=t: e.activation(out=junk, in_=x_res[:, t, :], func=AF.Square,
                                                accum_out=ssq[:, t:t + 1]),
             reads=[xbufs[t], b_ssq0], writes=[b_ssq[t], junk_b])
    P.op("dve", lambda e: e.tensor_scalar(out=rstd[:, :], in0=ssq[:, :], scalar1=1.0 / D, scalar2=eps,
                                          op0=ALU.mult, op1=ALU.add), reads=b_ssq, writes=[b_rstd])
    P.op("act", lambda e: e.activation(out=rstd[:, :], in_=rstd[:, :], func=AF.Sqrt), reads=[b_rstd], writes=[b_rstd])
    P.op("dve", lambda e: e.reciprocal(out=rstd[:, :], in_=rstd[:, :]), reads=[b_rstd], writes=[b_rstd])
    for t in range(ntt):
        xst, xsb = xs[t % 2], b_xs[t % 2]
        P.op("act", lambda e, t=t, xst=xst: e.activation(out=xst[:, :], in_=x_res[:, t, :], func=AF.Copy,
                                                         scale=rstd[:, t:t + 1]),
             reads=[xbufs[t], b_rstd], writes=[xsb])
        for k0 in range(0, nk, 4):
            pt, pb = tp_ring.next()
            for j in range(4):
                P.op("pe", lambda e, pt=pt, j=j, k0=k0, xst=xst: e.transpose(
                    pt[:, j * 128:(j + 1) * 128], xst[:, (k0 + j) * 128:(k0 + j + 1) * 128], C.ident_b[:, :]),
                    reads=[xsb, C.ident_buf], writes=[pb])
            P.op("dve", lambda e, pt=pt, k0=k0, t=t: e.tensor_tensor(
                out=xnT[:, k0:k0 + 4, t * 128:(t + 1) * 128],
                in0=pt[:, :].rearrange("p (a b) -> p a b", a=4),
                in1=gcol[:, k0:k0 + 4].unsqueeze(2).to_broadcast([128, 4, 128]), op=ALU.mult),
                reads=[pb, gbuf], writes=[xnT_bufs[t]])


NT = 1024
NTT = NT // 128
DM = 2048
DFF = 5632
WSLOT = 8192


class WStream:
    def __init__(self, C, nslots=4):
        self.C = C
        self.slots = [C.sb([128, WSLOT], BF16, "wslot") for _ in range(nslots)]
        self.bufs = [Buf() for _ in range(nslots)]
        self.i = 0

    def load(self, W, r0, nkc, c0, ncols):
        assert nkc * ncols <= WSLOT
        s = self.i
        self.i = (self.i + 1) % len(self.slots)
        view = self.slots[s][:, 0:nkc * ncols].rearrange("p (k c) -> p k c", k=nkc)
        src = W[r0:r0 + nkc * 128, c0:c0 + ncols].rearrange("(k p) c -> p k c", p=128)
        self.C.P.dma("pool", view, src, writes=[self.bufs[s]])
        return view, self.bufs[s]


def dense_program(cfg):
    C = Ctx()
    dense_phase(C, cfg)
    return C.finish()


def dense_phase(C, cfg):
    P = C.P
    nc = C.nc
    x_in = C.dram_in("x", [NT, DM], F32)
    make_consts(C)
    tp_ring = Ring([C.ps([128, 512], BF16, "tp") for _ in range(2)])
    mm_ring = Ring([C.ps([128, 512], F32, "mm") for _ in range(6)])
    ws = WStream(C, 3)

    x_res = C.sb([128, NTT, DM], F32, "xres")
    xb = [Buf("x%d" % t) for t in range(NTT)]
    for t in range(NTT):
        P.dma("sp", x_res[:, t, :], x_in[t * 128:(t + 1) * 128, :], writes=[xb[t]])
    FQ = 11
    arena = C.sb([128, 16 * NT + FQ * NT], BF16, "arena")
    xnT = arena[:, 0:16 * NT].rearrange("p (k t) -> p k t", k=16)
    hT = arena[:, 16 * NT:16 * NT + FQ * NT].rearrange("p (k t) -> p k t", k=FQ)
    xnb = [Buf("xn%d" % t) for t in range(NTT)]
    hb = [Buf("h%d" % f) for f in range(FQ)]
    stg = [C.sb([128, 4096], BF16, "stg") for _ in range(2)]
    stgb = [Buf(), Buf()]
    evac_i = [0]

    def evac(out, in_, reads, writes):
        evac_i[0] += 1
        if evac_i[0] % 2:
            P.op("act", lambda e: e.activation(out=out, in_=in_, func=AF.Copy), reads=reads, writes=writes)
        else:
            P.op("dve", lambda e: e.tensor_copy(out=out, in_=in_), reads=reads, writes=writes)

    def add_into_x(t, c0, pt, pb):
        P.op("dve", lambda e: e.tensor_tensor(out=x_res[:, t, c0:c0 + 512], in0=pt[:, :], in1=x_res[:, t, c0:c0 + 512],
                                              op=ALU.add), reads=[pb, xb[t]], writes=[xb[t]])

    if cfg.get("mix"):
        Kin = cfg["mix"]
        mixT_in = C.dram_in("mixT", [Kin, NT], BF16)
        w_mix = C.dram_in("w_mix", [Kin, DM], F32)
        mixT = xnT
        for kh in range(Kin // 2048):
            P.dma("sp", mixT, mixT_in[kh * 2048:(kh + 1) * 2048, :].rearrange("(k p) t -> p k t", p=128), writes=xnb)
            for cb in range(DM // 512):
                wv, wb = ws.load(w_mix, kh * 2048, 16, cb * 512, 512)
                for t in range(NTT):
                    pt, pb = mm_ring.next()
                    for kc in range(16):
                        P.op("pe", lambda e, pt=pt, wv=wv, kc=kc, t=t: e.matmul(
                            pt[:, :], lhsT=mixT[:, kc, t * 128:(t + 1) * 128], rhs=wv[:, kc, :],
                            start=(kc == 0), stop=(kc == 15)), reads=[xnb[t], wb], writes=[pb])
                    add_into_x(t, cb * 512, pt, pb)

    if cfg.get("ffn"):
        g_ffn = C.dram_in("g_ffn", [DM], F32)
        w_gate = C.dram_in("w_gate", [DM, DFF], F32)
        w_up = C.dram_in("w_up", [DM, DFF], F32)
        w_down = C.dram_in("w_down", [DFF, DM], F32)
        gcol, gbuf = load_colvec(C, g_ffn, DM // 128, mm_ring)
        rmsnorm_T(C, x_res, xb, gcol, gbuf, xnT, xnb, NTT, tp_ring, stg[1], stgb[1])
        NQ = DFF // 128 // FQ
        sg = [C.sb([128, 512], F32, "sg") for _ in range(2)]
        sgb = [Buf(), Buf()]
        sgi = 0
        for q in range(NQ):
            f0 = q * FQ
            for p0 in range(0, FQ, 4):
                npc = min(4, FQ - p0)
                c0 = (f0 + p0) * 128
                gv, gb_ = ws.load(w_gate, 0, 16, c0, npc * 128)
                uv, ub_ = ws.load(w_up, 0, 16, c0, npc * 128)
                for ci in range(npc):
                    f = p0 + ci
                    for th in range(NT // 512):
                        pg, pgb = mm_ring.next()
                        pu, pub = mm_ring.next()
                        for kc in range(16):
                            P.op("pe", lambda e, pg=pg, gv=gv, kc=kc, ci=ci, th=th: e.matmul(
                                pg[:, :], lhsT=gv[:, kc, ci * 128:(ci + 1) * 128], rhs=xnT[:, kc, th * 512:(th + 1) * 512],
                                start=(kc == 0), stop=(kc == 15)), reads=[gb_] + xnb[th * 4:th * 4 + 4], writes=[pgb])
                        for kc in range(16):
                            P.op("pe", lambda e, pu=pu, uv=uv, kc=kc, ci=ci, th=th: e.matmul(
                                pu[:, :], lhsT=uv[:, kc, ci * 128:(ci + 1) * 128], rhs=xnT[:, kc, th * 512:(th + 1) * 512],
                                start=(kc == 0), stop=(kc == 15)), reads=[ub_] + xnb[th * 4:th * 4 + 4], writes=[pub])
                        s_t, s_b = sg[sgi % 2], sgb[sgi % 2]
                        sgi += 1
                        P.op("act", lambda e, s_t=s_t, pg=pg: e.activation(out=s_t[:, :], in_=pg[:, :], func=AF.Silu),
                             reads=[pgb], writes=[s_b])
                        P.op("dve", lambda e, s_t=s_t, pu=pu, f=f, th=th: e.tensor_tensor(
                            out=hT[:, f, th * 512:(th + 1) * 512], in0=pu[:, :], in1=s_t[:, :], op=ALU.mult),
                            reads=[pub, s_b], writes=[hb[f]])
            for cb in range(DM // 512):
                wv, wb = ws.load(w_down, f0 * 128, FQ, cb * 512, 512)
                for t in range(NTT):
                    pt, pb = mm_ring.next()
                    for f in range(FQ):
                        P.op("pe", lambda e, pt=pt, wv=wv, f=f, t=t: e.matmul(
                            pt[:, :], lhsT=hT[:, f, t * 128:(t + 1) * 128], rhs=wv[:, f, :],
                            start=(f == 0), stop=(f == FQ - 1)), reads=[hb[f], wb], writes=[pb])
                    add_into_x(t, cb * 512, pt, pb)

    if cfg.get("x_out"):
        x_out = C.dram_out("x_out", [NT, DM], F32)
        for t in range(NTT):
            P.dma("sp", x_out[t * 128:(t + 1) * 128, :], x_res[:, t, :], reads=[xb[t]], is_output=True)

    if cfg.get("final"):
        g_fin = C.dram_in("g_fin", [DM], F32)
        y_out = C.dram_out("y", [NT, DM], F32)
        gb_t = arena[:, 16 * NT:16 * NT + 2 * DM].bitcast(F32)
        gbb = Buf()
        P.dma("sp", gb_t, g_fin.partition_broadcast(128), writes=hb + [gbb])
        ssq = C.sb([128, NTT], F32, "fssq")
        rstd = C.sb([128, NTT], F32, "frstd")
        junk = C.sb([128, DM], BF16, "fjunk")
        fjunk_b = Buf()
        yt = [stg[i][:, :].bitcast(F32) for i in range(2)]
        ytb = stgb
        b0, b_r = Buf(), Buf()
        b_s = [Buf() for _ in range(NTT)]
        P.op("dve", lambda e: e.memset(ssq[:, :], 0.0), writes=[b0])
        for t in range(NTT):
            P.op("act", lambda e, t=t: e.activation(out=junk[:, :], in_=x_res[:, t, :], func=AF.Square,
                                                    accum_out=ssq[:, t:t + 1]), reads=[xb[t], b0], writes=[b_s[t], fjunk_b])
        P.op("dve", lambda e: e.tensor_scalar(out=rstd[:, :], in0=ssq[:, :], scalar1=1.0 / DM, scalar2=1e-6,
                                              op0=ALU.mult, op1=ALU.add), reads=b_s, writes=[b_r])
        P.op("act", lambda e: e.activation(out=rstd[:, :], in_=rstd[:, :], func=AF.Sqrt), reads=[b_r], writes=[b_r])
        P.op("dve", lambda e: e.reciprocal(out=rstd[:, :], in_=rstd[:, :]), reads=[b_r], writes=[b_r])
        for t in range(NTT):
            y_t, y_b = yt[t % 2], ytb[t % 2]
            P.op("dve", lambda e, t=t, y_t=y_t: e.scalar_tensor_tensor(
                out=y_t, in0=x_res[:, t, :], scalar=rstd[:, t:t + 1], in1=gb_t,
                op0=ALU.mult, op1=ALU.mult), reads=[xb[t], b_r, gbb], writes=[y_b])
            P.dma("sp", y_out[t * 128:(t + 1) * 128, :], y_t, reads=[y_b], is_output=True)

    if cfg.get("proj"):
        pn = cfg.get("proj_names", ("g_proj", "w_proj"))
        g_mix = C.dram_in(pn[0], [DM], F32)
        w_proj = C.dram_in(pn[1], [DM, cfg["wcols"]], F32)
        gcol2, gbuf2 = load_colvec(C, g_mix, DM // 128, mm_ring)
        rmsnorm_T(C, x_res, xb, gcol2, gbuf2, xnT, xnb, NTT, tp_ring, stg[1], stgb[1])
        sti = 0
        for (name, c0, ncols, layout, dt) in cfg["proj"]:
            esz = 2 if dt == BF16 else 4
            if layout == "T":
                dst = C.dram_out(name, [ncols, NT], dt)
                for p0 in range(0, ncols, 512):
                    npc = min(512, ncols - p0) // 128
                    wv, wb = ws.load(w_proj, 0, 16, c0 + p0, npc * 128)
                    st, sb_ = stg[sti % 2], stgb[sti % 2]
                    sti += 1
                    stv = st[:, 0:npc * NT].rearrange("p (c t) -> p c t", c=npc)
                    for ci in range(npc):
                        for th in range(NT // 512):
                            pt, pb = mm_ring.next()
                            for kc in range(16):
                                P.op("pe", lambda e, pt=pt, wv=wv, kc=kc, ci=ci, th=th: e.matmul(
                                    pt[:, :], lhsT=wv[:, kc, ci * 128:(ci + 1) * 128],
                                    rhs=xnT[:, kc, th * 512:(th + 1) * 512],
                                    start=(kc == 0), stop=(kc == 15)), reads=[wb] + xnb[th * 4:th * 4 + 4], writes=[pb])
                            evac(stv[:, ci, th * 512:(th + 1) * 512], pt[:, :], [pb], [sb_])
                    P.dma("sp", dst[p0:p0 + npc * 128, :].rearrange("(c p) t -> p c t", p=128), stv, reads=[sb_],
                          is_output=True)
            else:
                dst = C.dram_out(name, [NT, ncols], dt)
                for p0 in range(0, ncols, 512):
                    pw = min(512, ncols - p0)
                    wv, wb = ws.load(w_proj, 0, 16, c0 + p0, pw)
                    st, sb_ = stg[sti % 2], stgb[sti % 2]
                    sti += 1
                    if dt == BF16:
                        stv = st[:, 0:NTT * pw].rearrange("p (t c) -> p t c", t=NTT)
                    else:
                        stv = st[:, :].bitcast(F32)[:, 0:NTT * pw].rearrange("p (t c) -> p t c", t=NTT)
                    for t in range(NTT):
                        pt, pb = mm_ring.next()
                        for kc in range(16):
                            P.op("pe", lambda e, pt=pt, wv=wv, kc=kc, t=t, pw=pw: e.matmul(
                                pt[:, 0:pw], lhsT=xnT[:, kc, t * 128:(t + 1) * 128], rhs=wv[:, kc, :],
                                start=(kc == 0), stop=(kc == 15)), reads=[wb, xnb[t]], writes=[pb])
                        evac(stv[:, t, :], pt[:, 0:pw], [pb], [sb_])
                    P.dma("sp", dst[:, p0:p0 + pw].rearrange("(t p) c -> p t c", p=128), stv, reads=[sb_],
                          is_output=True)
    return


S_LEN = 2048
ROPE_THETA = 10000.0


def attn_program(nheads=8, dbg=False):
    C = Ctx()
    attn_phase(C, nheads, dbg)
    return C.finish()


def attn_phase(C, nheads=8, dbg=False):
    P = C.P
    HD = 128
    NB = 8
    BLK = 256
    qT_in = C.dram_in("qT", [nheads * HD, S_LEN], BF16)
    kT_in = C.dram_in("kT", [nheads * HD, S_LEN], BF16)
    v_in = C.dram_in("v", [S_LEN, nheads * HD], BF16)
    oT_out = C.dram_out("oT", [nheads * HD, S_LEN], BF16)
    make_consts(C)
    cb = Buf("consts")
    ones_b = C.sb([128, 128], BF16, "ones")
    P.op("pool", lambda e: e.memset(ones_b[:, :], 1.0), writes=[cb])
    rt_f = C.sb([128, 128], F32, "rtf")
    rt_b = C.sb([128, 128], BF16, "rtb")
    P.op("pool", lambda e: e.memset(rt_f[:, :], 0.0), writes=[cb])
    P.op("pool", lambda e: e.affine_select(out=rt_f[:, :], in_=rt_f[:, :], pattern=[[-1, 128]], compare_op=ALU.not_equal,
                                           fill=1.0, base=64, channel_multiplier=1), reads=[cb], writes=[cb])
    P.op("pool", lambda e: e.affine_select(out=rt_f[:, :], in_=rt_f[:, :], pattern=[[-1, 128]], compare_op=ALU.not_equal,
                                           fill=-1.0, base=-64, channel_multiplier=1), reads=[cb], writes=[cb])
    P.op("pool", lambda e: e.tensor_copy(out=rt_b[:, :], in_=rt_f[:, :]), reads=[cb], writes=[cb])
    cm_f = C.sb([128, 2, BLK], F32, "cmf")
    cm_b = C.sb([128, 2, BLK], BF16, "cmb")
    P.op("pool", lambda e: e.memset(cm_f[:, :, :], 1.0), writes=[cb])
    for c in range(2):
        P.op("pool", lambda e, c=c: e.affine_select(out=cm_f[:, c, :], in_=cm_f[:, c, :], pattern=[[1, BLK]],
                                                    compare_op=ALU.is_ge, fill=0.0, base=-128 * c, channel_multiplier=-1),
             reads=[cb], writes=[cb])
    P.op("pool", lambda e: e.tensor_copy(out=cm_b[:, :, :], in_=cm_f[:, :, :]), reads=[cb], writes=[cb])
    oh_f = C.sb([8, NB, 128], F32, "ohf")
    oh_b = C.sb([8, NB, 128], BF16, "ohb")
    P.op("pool", lambda e: e.memset(oh_f[:, :, :], 0.0), writes=[cb])
    P.op("pool", lambda e: e.affine_select(out=oh_f[:, :, :], in_=oh_f[:, :, :], pattern=[[-1, NB], [0, 128]],
                                           compare_op=ALU.not_equal, fill=1.0, base=0, channel_multiplier=1),
         reads=[cb], writes=[cb])
    P.op("pool", lambda e: e.tensor_copy(out=oh_b[:, :, :], in_=oh_f[:, :, :]), reads=[cb], writes=[cb])
    negm = C.sb([128, 16, NB], F32, "negm")
    P.op("pool", lambda e: e.memset(negm[:, :, :], 0.0), writes=[cb])
    for t in range(16):
        P.op("pool", lambda e, t=t: e.memset(negm[:, t, t // 2:NB], -1e30), reads=[cb], writes=[cb])
    pos_i = C.sb([128, S_LEN], mybir.dt.int32, "posi")
    ang = C.sb([128, S_LEN], F32, "ang")
    cosT = C.sb([128, S_LEN], F32, "cosT")
    sinT = C.sb([128, S_LEN], F32, "sinT")
    pid_i = C.sb([128, 1], mybir.dt.int32, "pidi")
    pid_f = C.sb([128, 1], F32, "pidf")
    inv = C.sb([128, 1], F32, "inv")
    P.op("pool", lambda e: e.iota(pos_i[:, :], pattern=[[1, S_LEN]], base=0, channel_multiplier=0), writes=[cb])
    P.op("pool", lambda e: e.iota(pid_i[:, :], pattern=[[0, 1]], base=0, channel_multiplier=1), reads=[cb], writes=[cb])
    P.op("pool", lambda e: e.tensor_copy(out=ang[:, :], in_=pos_i[:, :]), reads=[cb], writes=[cb])
    P.op("pool", lambda e: e.tensor_copy(out=pid_f[:, :], in_=pid_i[:, :]), reads=[cb], writes=[cb])
    pid_g = C.sb([128, 1], F32, "pidg")
    P.op("dve", lambda e: e.tensor_scalar(out=pid_g[:, :], in0=pid_f[:, :], scalar1=64.0, scalar2=64.0, op0=ALU.is_ge, op1=ALU.mult),
         reads=[cb], writes=[cb])
    P.op("dve", lambda e: e.tensor_tensor(out=pid_f[:, :], in0=pid_f[:, :], in1=pid_g[:, :], op=ALU.subtract), reads=[cb], writes=[cb])
    P.op("act", lambda e: e.activation(out=inv[:, :], in_=pid_f[:, :], func=AF.Exp, scale=-float(np.log(ROPE_THETA)) / 64.0),
         reads=[cb], writes=[cb])
    PI = float(np.pi)
    MAGIC = 12582912.0
    P.op("dve", lambda e: e.tensor_scalar(out=inv[:, :], in0=inv[:, :], scalar1=1.0 / (2 * PI), scalar2=None, op0=ALU.mult),
         reads=[cb], writes=[cb])
    P.op("dve", lambda e: e.tensor_scalar(out=ang[:, :], in0=ang[:, :], scalar1=inv[:, 0:1], scalar2=None, op0=ALU.mult),
         reads=[cb], writes=[cb])
    for (dst, off) in ((sinT, 0.0), (cosT, 0.25)):
        P.op("dve", lambda e, dst=dst, off=off: e.tensor_scalar(out=dst[:, :], in0=ang[:, :], scalar1=off, scalar2=MAGIC, op0=ALU.add, op1=ALU.add),
             reads=[cb], writes=[cb])
        P.op("dve", lambda e, dst=dst: e.tensor_scalar(out=dst[:, :], in0=dst[:, :], scalar1=-MAGIC, scalar2=None, op0=ALU.add),
             reads=[cb], writes=[cb])
        P.op("dve", lambda e, dst=dst, off=off: e.scalar_tensor_tensor(out=dst[:, :], in0=ang[:, :], scalar=off, in1=dst[:, :], op0=ALU.add, op1=ALU.subtract),
             reads=[cb], writes=[cb])
        P.op("act", lambda e, dst=dst: e.activation(out=dst[:, :], in_=dst[:, :], func=AF.Sin, scale=6.28318), reads=[cb], writes=[cb])

    bankA = C.ps([128, 512], F32, "bA")
    bankB = C.ps([128, 512], F32, "bB")
    bankC = C.ps([128, 512], F32, "bC")
    bankD = C.ps([128, 512], F32, "bD")
    bankG = C.ps([128, 512], F32, "bG")
    bankH = C.ps([128, 512], F32, "bH")
    bankE = C.ps([128, 512], F32, "bE")
    bankF = C.ps([128, 512], F32, "bF")
    rot_ring = Ring([bankE, bankF])
    gate_b = rot_ring.bufs[0]
    bF_b = rot_ring.bufs[1]
    st_ring = Ring([bankA[:, 0:256], bankB[:, 0:256]])

    NSLOT = 2
    scale = 1.0 / float(np.sqrt(HD))
    t1 = [C.sb([128, 512], F32, "t1") for _ in range(2)]
    t2 = [C.sb([128, 512], F32, "t2") for _ in range(2)]
    t1b = [Buf(), Buf()]
    t2b = [Buf(), Buf()]
    tstate = {"ti": 0}
    slots = []
    for sl_ in range(NSLOT):
        d = dict(
            raw_q=C.sb([128, S_LEN], BF16, "rawq"), raw_k=C.sb([128, S_LEN], BF16, "rawk"), v=C.sb([128, 16, HD], BF16, "vsb"),
            rq_b=Buf(), rk_b=Buf(), v_b=Buf(),
            qr=C.sb([128, S_LEN], BF16, "qr"), kr=C.sb([128, S_LEN], BF16, "kr"),
            qr_b=[Buf() for _ in range(4)], kr_b=[Buf() for _ in range(4)],
            kmT=C.sb([128, NB], F32, "kmT"), kmT_bf=C.sb([128, NB], BF16, "kmTb"), km_b=Buf(),
            gm=C.sb([128, 16, NB], F32, "gm"), mx8=C.sb([128, 16, 8], F32, "mx8"), bias_f=C.sb([128, 16, NB], F32, "biasf"),
            bias_bf=C.sb([128, 16, NB], BF16, "biasb"), gm_b=Buf(),
            biasT=C.sb([8, S_LEN], BF16, "biasT"), biasT_b=Buf(),
            pT=[C.sb([128, BLK], BF16, "pT") for _ in range(3)], pT_b=[Buf() for _ in range(3)],
            rden=C.sb([128, BLK], F32, "rden"), rden_b=Buf(),
            oT_sb=C.sb([128, S_LEN], BF16, "oTsb"), oT_b=Buf(),
            acc=(bankC, bankG) if sl_ == 0 else (bankD, bankH), acc_b=Buf(), acc_b2=Buf(),
            rden2=C.sb([128, BLK], F32, "rden2"), rden2_b=Buf(),
        )
        slots.append(d)

    def head_gen(h, D):
        raw_q, raw_k, v_t = D["raw_q"], D["raw_k"], D["v"]
        qr, kr = D["qr"], D["kr"]
        qr_b, kr_b = D["qr_b"], D["kr_b"]
        P.dma("sp", raw_q[:, :], qT_in[h * HD:(h + 1) * HD, :], writes=[D["rq_b"]])
        P.dma("sp", raw_k[:, :], kT_in[h * HD:(h + 1) * HD, :], writes=[D["rk_b"]])
        P.dma("sp", v_t[:, :, :], v_in[:, h * HD:(h + 1) * HD].rearrange("(t p) c -> p t c", p=128), writes=[D["v_b"]])
        yield
        for (raw, rawb, dst, dstb) in ((raw_q, D["rq_b"], qr, qr_b), (raw_k, D["rk_b"], kr, kr_b)):
            for blk in range(4):
                sl = slice(blk * 512, (blk + 1) * 512)
                pr, prb = rot_ring.next()
                P.op("pe", lambda e, pr=pr, raw=raw, sl=sl: e.matmul(pr[:, :], lhsT=rt_b[:, :], rhs=raw[:, sl], start=True, stop=True),
                     reads=[cb, rawb], writes=[prb])
                ti = tstate["ti"]
                tstate["ti"] += 1
                a, ab = t1[ti % 2], t1b[ti % 2]
                b2, bb = t2[ti % 2], t2b[ti % 2]
                P.op("pool", lambda e, a=a, raw=raw, sl=sl: e.tensor_tensor(out=a[:, :], in0=raw[:, sl], in1=cosT[:, sl], op=ALU.mult),
                     reads=[rawb, cb], writes=[ab])
                P.op("dve", lambda e, b2=b2, pr=pr, sl=sl: e.tensor_tensor(out=b2[:, :], in0=pr[:, :], in1=sinT[:, sl], op=ALU.mult),
                     reads=[prb, cb], writes=[bb])
                P.op("pool", lambda e, a=a, b2=b2, dst=dst, sl=sl: e.tensor_tensor(out=dst[:, sl], in0=a[:, :], in1=b2[:, :], op=ALU.add),
                     reads=[ab, bb], writes=[dstb[blk]])
                yield
        kmT, kmT_bf, km_b = D["kmT"], D["kmT_bf"], D["km_b"]
        gm, mx8, bias_f, bias_bf, gm_b = D["gm"], D["mx8"], D["bias_f"], D["bias_bf"], D["gm_b"]
        biasT, biasT_b = D["biasT"], D["biasT_b"]
        P.op("dve", lambda e: e.tensor_reduce(out=kmT[:, :], in_=kr[:, :].rearrange("p (n s) -> p n s", n=NB), axis=AX.X, op=ALU.add),
             reads=kr_b, writes=[km_b])
        P.op("dve", lambda e: e.tensor_scalar(out=kmT_bf[:, :], in0=kmT[:, :], scalar1=1.0 / BLK, scalar2=None, op0=ALU.mult),
             reads=[km_b], writes=[km_b])
        yield
        for t in range(16):
            P.op("pe", lambda e, t=t: e.matmul(bankE[:, t * NB:(t + 1) * NB], lhsT=qr[:, t * 128:(t + 1) * 128], rhs=kmT_bf[:, :],
                                               start=True, stop=True), reads=[qr_b[t // 4], km_b], writes=[gate_b])
        P.op("dve", lambda e: e.tensor_tensor(out=gm[:, :, :], in0=bankE[:, 0:16 * NB].rearrange("p (t n) -> p t n", t=16),
                                              in1=negm[:, :, :], op=ALU.add), reads=[gate_b, cb], writes=[gm_b])
        yield
        for t in range(16):
            P.op("dve", lambda e, t=t: e.max(out=mx8[:, t, :], in_=gm[:, t, :]), reads=[gm_b], writes=[gm_b])
            if t % 4 == 3:
                yield
        P.op("dve", lambda e: e.tensor_tensor(out=bias_f[:, :, :], in0=gm[:, :, :],
                                              in1=mx8[:, :, 2:3].to_broadcast([128, 16, NB]), op=ALU.is_ge),
             reads=[gm_b], writes=[gm_b])
        P.op("dve", lambda e: e.tensor_scalar(out=bias_bf[:, :, :], in0=bias_f[:, :, :], scalar1=-1.0, scalar2=30000.0,
                                              op0=ALU.add, op1=ALU.mult), reads=[gm_b], writes=[gm_b])
        yield
        for g4 in range(4):
            for j in range(4):
                t = g4 * 4 + j
                P.op("pe", lambda e, t=t, j=j: e.matmul(bankF[0:8, j * 128:(j + 1) * 128], lhsT=bias_bf[:, t, :], rhs=C.ident_b[:, :],
                                                        start=True, stop=True), reads=[gm_b, C.ident_buf], writes=[bF_b])
            P.op("act", lambda e, g4=g4: e.activation(out=biasT[0:8, g4 * 512:(g4 + 1) * 512], in_=bankF[0:8, :], func=AF.Copy),
                 reads=[bF_b], writes=[biasT_b])
            yield
        osb, osbb = D["oT_sb"], D["oT_b"]
        while mstate["busy"]:
            yield
        mstate["busy"] = True
        acc_opts = [((bankC, bankG), accbufs[0]), ((bankD, bankH), accbufs[1])]
        rd_opts = [(D["rden"], D["rden_b"]), (D["rden2"], D["rden2_b"])]
        pT, pT_b = D["pT"], D["pT_b"]
        pairs = [(qb, kc) for qb in range(NB) for kc in range(2 * qb + 2)]

        def emit_st(i):
            qb, kc = pairs[i]
            own = kc >= 2 * qb
            qsl = slice(qb * BLK, (qb + 1) * BLK)
            stt, stb = st_ring.next()
            need_bias = (not own) and qb >= 4
            P.op("pe", lambda e: e.matmul(stt, lhsT=kr[:, kc * 128:(kc + 1) * 128], rhs=qr[:, qsl], start=True, stop=not need_bias),
                 reads=[kr_b[kc // 4], qr_b[qb // 2]], writes=[stb])
            if need_bias:
                P.op("pe", lambda e: e.matmul(stt, lhsT=oh_b[:, kc // 2, :], rhs=biasT[0:8, qsl], start=False, stop=True),
                     reads=[cb, biasT_b], writes=[stb])
            return stt, stb

        LOOK = C.attn_look if hasattr(C, "attn_look") else 1
        pend = [emit_st(j) for j in range(min(LOOK, len(pairs)))]
        for i, (qb, kc) in enumerate(pairs):
            if i + LOOK < len(pairs):
                pend.append(emit_st(i + LOOK))
            if not pend:
                pend.append(emit_st(i))
            stt, stb = pend.pop(0)
            (acc, accd), accb = acc_opts[qb % len(acc_opts)]
            own = kc >= 2 * qb
            nkc = 2 * qb + 2
            qsl = slice(qb * BLK, (qb + 1) * BLK)
            p_t, p_b = pT[i % 3], pT_b[i % 3]
            P.op("act", lambda e, p_t=p_t, stt=stt: e.activation(out=p_t[:, :], in_=stt, func=AF.Exp, scale=scale),
                 reads=[stb], writes=[p_b])
            if own:
                P.op("dve", lambda e, p_t=p_t, c=kc - 2 * qb: e.tensor_tensor(out=p_t[:, :], in0=p_t[:, :], in1=cm_b[:, c, :], op=ALU.mult),
                     reads=[p_b, cb], writes=[p_b])
            P.op("pe", lambda e, kc=kc, p_t=p_t, nkc=nkc, acc=acc: e.matmul(
                acc[:, 0:BLK], lhsT=v_t[:, kc, :], rhs=p_t[:, :], start=(kc == 0), stop=(kc == nkc - 1)),
                reads=[D["v_b"], p_b], writes=[accb])
            P.op("pe", lambda e, kc=kc, p_t=p_t, nkc=nkc, accd=accd: e.matmul(
                accd[:, 0:BLK], lhsT=ones_b[:, :], rhs=p_t[:, :], start=(kc == 0), stop=(kc == nkc - 1)),
                reads=[cb, p_b], writes=[accb])
            if kc == nkc - 1:
                rd, rdb = rd_opts[qb % 2]
                P.op("dve", lambda e, rd=rd, accd=accd: e.reciprocal(out=rd[:, :], in_=accd[:, 0:BLK]), reads=[accb], writes=[rdb])
                P.op("dve", lambda e, qsl=qsl, rd=rd, acc=acc: e.tensor_tensor(out=osb[:, qsl], in0=acc[:, 0:BLK], in1=rd[:, :], op=ALU.mult),
                     reads=[accb, rdb], writes=[osbb])
            yield
        P.dma("sp", oT_out[h * HD:(h + 1) * HD, :], osb[:, :], reads=[osbb], is_output=True)
        mstate["busy"] = False

    mstate = {"busy": False}
    accbufs = [Buf(), Buf()]
    active = []
    next_h = 0
    free_slots = list(range(NSLOT))
    while next_h < nheads or active:
        while free_slots and next_h < nheads:
            sl_ = free_slots.pop(0)
            active.append((head_gen(next_h, slots[sl_]), sl_))
            next_h += 1
        for item in list(active):
            try:
                next(item[0])
            except StopIteration:
                active.remove(item)
                free_slots.append(item[1])
    qr, kr = slots[0]["qr"], slots[0]["kr"]
    gm, mx8, bias_f, biasT = slots[0]["gm"], slots[0]["mx8"], slots[0]["bias_f"], slots[0]["biasT"]
    kmT = slots[0]["kmT"]
    v_sb = [slots[0]["v"], slots[-1]["v"]]
    if dbg:
        P.barrier()
        for nm, t, shp, dt in (("d_sin", sinT, [128, S_LEN], F32), ("d_cos", cosT, [128, S_LEN], F32), ("d_qr", qr, [128, S_LEN], BF16),
                               ("d_kr", kr, [128, S_LEN], BF16), ("d_gm", gm, [128, 16 * NB], F32), ("d_mx8", mx8, [128, 128], F32),
                               ("d_biasf", bias_f, [128, 128], F32), ("d_biasT", biasT, [8, S_LEN], BF16), ("d_rt", rt_f, [128, 128], F32),
                               ("d_cm", cm_f, [128, 512], F32), ("d_v0", v_sb[0], [128, 2048], BF16), ("d_v1", v_sb[1], [128, 2048], BF16), ("d_oh", oh_f, [8, 8 * 128], F32), ("d_kmT", kmT, [128, 8], F32)):
            d = C.dram_out(nm, shp, dt)
            src = t
            if len(t.shape) == 3:
                src = t[:, :, :].rearrange("p a b -> p (a b)")
            else:
                src = t[:, :]
            P.dma("sp", d, src, is_output=True)
    return


def mixer_program(do_mlstm=True, do_ssd=True):
    C = Ctx()
    mixer_phase(C, do_mlstm, do_ssd)
    return C.finish()


def mixer_phase(C, do_mlstm=True, do_ssd=True):
    P = C.P
    S = S_LEN
    NCH = S // 128
    MH, DK, DV = 4, 128, 256
    SG, SR, SP_, SN = 4, 4, 64, 128
    SHL = SG * SR
    mixT_m = C.dram_out("mixT_m", [MH * DV, S], BF16) if do_mlstm else None
    mixT_s = C.dram_out("mixT_s", [SG * SR * SP_, S], BF16) if do_ssd else None
    make_consts(C)
    cb = Buf("consts")
    ones_f = C.sb([128, 128], F32, "onesf")
    tri_f = C.sb([128, 128], F32, "trif")
    tri_b = C.sb([128, 128], BF16, "trib")
    negm_f = C.sb([128, 128], F32, "negmf")
    P.op("pool", lambda e: e.memset(ones_f[:, :], 1.0), writes=[cb])
    P.op("pool", lambda e: e.memset(tri_f[:, :], 1.0), reads=[cb], writes=[cb])
    P.op("pool", lambda e: e.affine_select(out=tri_f[:, :], in_=tri_f[:, :], pattern=[[1, 128]], compare_op=ALU.is_ge,
                                           fill=0.0, base=0, channel_multiplier=-1), reads=[cb], writes=[cb])
    P.op("pool", lambda e: e.tensor_copy(out=tri_b[:, :], in_=tri_f[:, :]), reads=[cb], writes=[cb])
    P.op("pool", lambda e: e.tensor_scalar(out=negm_f[:, :], in0=tri_f[:, :], scalar1=-1.0, scalar2=30000.0, op0=ALU.add, op1=ALU.mult),
         reads=[cb], writes=[cb])
    mm_ring = Ring([C.ps([128, 512], F32, "mm") for _ in range(6)])
    tp_ring = Ring([C.ps([128, 512], BF16, "tp") for _ in range(2)])
    big_ring = mm_ring
    ost = [C.sb([128, 2, S], BF16, "ost") for _ in range(2)]
    ost_b = [Buf(), Buf()]
    osti = 0

    def bcast_load(vec_ap, n, name):
        t = C.sb([128, n], F32, name)
        b = Buf()
        P.dma("sp", t[:, :], vec_ap.partition_broadcast(128), writes=[b])
        return t, b

    def softplus_inplace(t_ap, b, sign):
        P.op("act", lambda e: e.activation(out=t_ap, in_=t_ap, func=AF.Exp, scale=float(sign)), reads=[b], writes=[b])
        P.op("act", lambda e: e.activation(out=t_ap, in_=t_ap, func=AF.Ln, bias=1.0), reads=[b], writes=[b])

    if do_mlstm:
        qT_in = C.dram_in("m_qT", [MH * DK, S], BF16)
        kT_in = C.dram_in("m_kT", [MH * DK, S], BF16)
        v_in = C.dram_in("m_v", [S, MH * DV], BF16)
        o_in = C.dram_in("m_o", [S, MH * DV], BF16)
        gi_in = C.dram_in("m_ig", [S, MH], F32)
        gf_in = C.dram_in("m_fg", [S, MH], F32)
        gb_in = C.dram_in("m_gbias", [2 * MH], F32)
        nrm_in = C.dram_in("m_norm", [MH * DV], F32)
        gbias, gbias_b = bcast_load(gb_in, 2 * MH, "gbias")
        mnorm, mnorm_b = bcast_load(nrm_in, MH * DV, "mnorm")
        gates = C.sb([128, NCH, 2 * MH], F32, "gates")
        gt_b = Buf()
        P.dma("sp", gates[:, :, 0:MH], gi_in.rearrange("(c p) g -> p c g", p=128), writes=[gt_b])
        P.dma("sp", gates[:, :, MH:2 * MH], gf_in.rearrange("(c p) g -> p c g", p=128), writes=[gt_b])
        P.op("dve", lambda e: e.tensor_tensor(out=gates[:, :, :], in0=gates[:, :, :],
                                              in1=gbias[:, :].unsqueeze(1).to_broadcast([128, NCH, 2 * MH]), op=ALU.add),
             reads=[gt_b, gbias_b], writes=[gt_b])
        logf = C.sb([128, NCH, MH], F32, "logf")
        lf_b = Buf()
        P.op("dve", lambda e: e.tensor_copy(out=logf[:, :, :], in_=gates[:, :, MH:2 * MH]), reads=[gt_b], writes=[lf_b])
        softplus_inplace(logf[:, :, :], lf_b, -1.0)
        P.op("dve", lambda e: e.tensor_scalar(out=logf[:, :, :], in0=logf[:, :, :], scalar1=-1.0, scalar2=None, op0=ALU.mult),
             reads=[lf_b], writes=[lf_b])
        pa, pab = big_ring.next()
        P.op("pe", lambda e: e.matmul(pa[:, 0:NCH * MH], lhsT=tri_f[:, :], rhs=logf[:, :, :].rearrange("p c h -> p (c h)"),
                                      start=True, stop=True), reads=[cb, lf_b], writes=[pab])
        pt_, ptb = big_ring.next()
        P.op("pe", lambda e: e.matmul(pt_[:, 0:NCH * MH], lhsT=ones_f[:, :], rhs=logf[:, :, :].rearrange("p c h -> p (c h)"),
                                      start=True, stop=True), reads=[cb, lf_b], writes=[ptb])
        rowfac = C.sb([128, NCH, MH], F32, "rowfac")
        colfac = C.sb([128, NCH, MH], F32, "colfac")
        eaL = C.sb([128, NCH, MH], F32, "eaL")
        fac_b = Buf()
        P.op("act", lambda e: e.activation(out=rowfac[:, :, :].rearrange("p c h -> p (c h)"), in_=pa[:, 0:NCH * MH], func=AF.Exp),
             reads=[pab], writes=[fac_b])
        P.op("dve", lambda e: e.tensor_scalar(out=rowfac[:, :, :], in0=rowfac[:, :, :], scalar1=float(DK) ** -0.5, scalar2=None, op0=ALU.mult),
             reads=[fac_b], writes=[fac_b])
        P.op("dve", lambda e: e.tensor_scalar(out=colfac[:, :, :].rearrange("p c h -> p (c h)"), in0=pa[:, 0:NCH * MH], scalar1=-1.0,
                                              scalar2=None, op0=ALU.mult), reads=[pab, fac_b], writes=[fac_b])
        P.op("dve", lambda e: e.tensor_tensor(out=colfac[:, :, :], in0=colfac[:, :, :], in1=gates[:, :, 0:MH], op=ALU.add),
             reads=[fac_b, gt_b], writes=[fac_b])
        P.op("act", lambda e: e.activation(out=colfac[:, :, :], in_=colfac[:, :, :], func=AF.Exp), reads=[fac_b], writes=[fac_b])
        P.op("act", lambda e: e.activation(out=eaL[:, :, :].rearrange("p c h -> p (c h)"), in_=pt_[:, 0:NCH * MH], func=AF.Exp),
             reads=[ptb, fac_b], writes=[fac_b])

        qT_t = [C.sb([128, S], BF16, "mqT") for _ in range(2)]
        kT_t = [C.sb([128, S], BF16, "mkT") for _ in range(2)]
        vx_t = [C.sb([128, NCH, DV + 1], BF16, "mvx") for _ in range(2)]
        o_t = [C.sb([128, NCH, DV], BF16, "mo") for _ in range(2)]
        ld_b = [[Buf() for _ in range(4)] for _ in range(2)]
        for i in range(2):
            P.op("pool", lambda e, i=i: e.memset(vx_t[i][:, :, DV:DV + 1], 1.0), writes=[ld_b[i][2]])
        cst = [C.sb([128, DV + 1], F32, "cst") for _ in range(2)]
        cst_bf = C.sb([128, DV + 1], BF16, "cstbf")
        cst_b = [Buf(), Buf()]
        cstbf_b = Buf()
        kcs = [C.sb([128, DK], BF16, "kcs") for _ in range(2)]
        kcs_b = [Buf(), Buf()]
        pTm = [C.sb([128, 128], BF16, "pTm") for _ in range(2)]
        pTm_b = [Buf(), Buf()]
        hbuf = [C.sb([128, DV], F32, "hbuf") for _ in range(2)]
        hbuf_b = [Buf(), Buf()]
        sig = [C.sb([128, DV], F32, "sig") for _ in range(2)]
        sig_b = [Buf(), Buf()]
        sm = [C.sb([128, 8], F32, "sm") for _ in range(2)]
        sm_b = [Buf(), Buf()]
        hjunk = C.sb([128, DV], BF16, "hjunk")
        hjunk_b = Buf()
        hout = [C.sb([128, DV], BF16, "hout") for _ in range(2)]
        hout_b = [Buf(), Buf()]
        it = 0
        sgall = C.sb([128, NCH, DV], F32, "sgall")
        sgall_b = Buf()
        for hh in range(MH):
            hb_ = hh % 2
            qt, kt, vx, ot = qT_t[hb_], kT_t[hb_], vx_t[hb_], o_t[hb_]
            lb = ld_b[hb_]
            P.dma("sp", qt[:, :], qT_in[hh * DK:(hh + 1) * DK, :], writes=[lb[0]])
            P.dma("sp", kt[:, :], kT_in[hh * DK:(hh + 1) * DK, :], writes=[lb[1]])
            P.dma("sp", vx[:, :, 0:DV], v_in[:, hh * DV:(hh + 1) * DV].rearrange("(c p) d -> p c d", p=128), writes=[lb[2]])
            P.dma("sp", ot[:, :, :], o_in[:, hh * DV:(hh + 1) * DV].rearrange("(c p) d -> p c d", p=128), writes=[lb[3]])
            os_t, os_b = ost[osti % 2], ost_b[osti % 2]
            osti += 1
            P.op("act", lambda e, ot=ot: e.activation(out=sgall[:, :, :].rearrange("p c d -> p (c d)"), in_=ot[:, :, :].rearrange("p c d -> p (c d)"),
                                                     func=AF.Sigmoid), reads=[lb[3]], writes=[sgall_b])
            P.op("pool", lambda e, hh=hh: e.tensor_tensor(out=sgall[:, :, :], in0=sgall[:, :, :],
                                                          in1=mnorm[:, hh * DV:(hh + 1) * DV].unsqueeze(1).to_broadcast([128, NCH, DV]), op=ALU.mult),
                 reads=[sgall_b, mnorm_b], writes=[sgall_b])
            for c in range(NCH):
                csl = slice(c * 128, (c + 1) * 128)
                i2 = it % 2
                it += 1
                tp, tpb = tp_ring.next()
                P.op("pe", lambda e, tp=tp, kt=kt, csl=csl: e.transpose(tp[:, 0:128], kt[:, csl], C.ident_b[:, :]),
                     reads=[lb[1], C.ident_buf], writes=[tpb])
                P.op("dve", lambda e, tp=tp, i2=i2, c=c, hh=hh: e.tensor_scalar(out=kcs[i2][:, :], in0=tp[:, 0:128],
                                                                            scalar1=colfac[:, c, hh:hh + 1], scalar2=None, op0=ALU.mult),
                     reads=[tpb, fac_b], writes=[kcs_b[i2]])
                pg, pgb = mm_ring.next()
                P.op("pe", lambda e, pg=pg, kt=kt, qt=qt, csl=csl: e.matmul(pg[:, 0:128], lhsT=kt[:, csl], rhs=qt[:, csl], start=True, stop=True),
                     reads=[lb[0], lb[1]], writes=[pgb])
                P.op("dve", lambda e, pg=pg, i2=i2, c=c, hh=hh: e.scalar_tensor_tensor(
                    out=pTm[i2][:, :], in0=pg[:, 0:128], scalar=colfac[:, c, hh:hh + 1], in1=tri_f[:, :], op0=ALU.mult, op1=ALU.mult),
                    reads=[pgb, fac_b, cb], writes=[pTm_b[i2]])
                pn, pnb = mm_ring.next()
                P.op("pe", lambda e, pn=pn, i2=i2, vx=vx, c=c: e.matmul(pn[:, 0:DV + 1], lhsT=pTm[i2][:, :], rhs=vx[:, c, :],
                                                                    start=True, stop=(c == 0)), reads=[pTm_b[i2], lb[2]], writes=[pnb])
                if c > 0:
                    P.op("pe", lambda e, pn=pn, qt=qt, csl=csl: e.matmul(pn[:, 0:DV + 1], lhsT=qt[:, csl], rhs=cst_bf[:, :], start=False, stop=True),
                         reads=[lb[0], cstbf_b], writes=[pnb])
                pu, pub = mm_ring.next()
                P.op("pe", lambda e, pu=pu, i2=i2, vx=vx, c=c: e.matmul(pu[:, 0:DV + 1], lhsT=kcs[i2][:, :], rhs=vx[:, c, :], start=True, stop=True),
                     reads=[kcs_b[i2], lb[2]], writes=[pub])
                s_t, s_b = sm[i2], sm_b[i2]
                P.op("act", lambda e, s_t=s_t, pn=pn, c=c, hh=hh: e.activation(
                    out=s_t[:, 0:1], in_=pn[:, DV:DV + 1], func=AF.Abs, scale=rowfac[:, c, hh:hh + 1]),
                    reads=[pnb, fac_b], writes=[s_b])
                P.op("dve", lambda e, s_t=s_t: e.tensor_scalar(out=s_t[:, 0:1], in0=s_t[:, 0:1], scalar1=1.0, scalar2=None, op0=ALU.max),
                     reads=[s_b], writes=[s_b])
                P.op("dve", lambda e, s_t=s_t: e.reciprocal(out=s_t[:, 1:2], in_=s_t[:, 0:1]), reads=[s_b], writes=[s_b])
                P.op("dve", lambda e, s_t=s_t, c=c, hh=hh: e.tensor_tensor(out=s_t[:, 2:3], in0=s_t[:, 1:2], in1=rowfac[:, c, hh:hh + 1], op=ALU.mult),
                     reads=[s_b, fac_b], writes=[s_b])
                hb2, hbb = hbuf[i2], hbuf_b[i2]
                P.op("act", lambda e, hb2=hb2, pn=pn, s_t=s_t: e.activation(out=hb2[:, :], in_=pn[:, 0:DV], func=AF.Copy, scale=s_t[:, 2:3]),
                     reads=[pnb, s_b], writes=[hbb])
                P.op("dve", lambda e, s_t=s_t: e.memset(s_t[:, 3:4], 0.0), reads=[s_b], writes=[s_b])
                P.op("act", lambda e, hb2=hb2, s_t=s_t: e.activation(out=hjunk[:, :], in_=hb2[:, :], func=AF.Square, accum_out=s_t[:, 3:4]),
                     reads=[hbb, s_b], writes=[s_b, hjunk_b])
                P.op("dve", lambda e, s_t=s_t: e.tensor_scalar(out=s_t[:, 4:5], in0=s_t[:, 3:4], scalar1=1.0 / DV, scalar2=1e-6, op0=ALU.mult, op1=ALU.add),
                     reads=[s_b], writes=[s_b])
                P.op("act", lambda e, s_t=s_t: e.activation(out=s_t[:, 5:6], in_=s_t[:, 4:5], func=AF.Ln), reads=[s_b], writes=[s_b])
                P.op("act", lambda e, s_t=s_t: e.activation(out=s_t[:, 6:7], in_=s_t[:, 5:6], func=AF.Exp, scale=-0.5), reads=[s_b], writes=[s_b])
                ho, hob = hout[i2], hout_b[i2]
                P.op("dve", lambda e, ho=ho, hb2=hb2, s_t=s_t, c=c: e.scalar_tensor_tensor(
                    out=ho[:, :], in0=hb2[:, :], scalar=s_t[:, 6:7], in1=sgall[:, c, :], op0=ALU.mult, op1=ALU.mult),
                    reads=[hbb, s_b, sgall_b], writes=[hob])
                tp2, tp2b = tp_ring.next()
                for j in range(2):
                    P.op("pe", lambda e, tp2=tp2, ho=ho, j=j: e.transpose(tp2[:, j * 128:(j + 1) * 128], ho[:, j * 128:(j + 1) * 128], C.ident_b[:, :]),
                         reads=[hob, C.ident_buf], writes=[tp2b])
                P.op("act", lambda e, tp2=tp2, os_t=os_t, csl=csl: e.activation(out=os_t[:, :, csl], in_=tp2[:, 0:256].rearrange("p (a b) -> p a b", a=2),
                                                                         func=AF.Copy), reads=[tp2b], writes=[os_b])
                if c < NCH - 1:
                    if c == 0:
                        P.op("dve", lambda e, pu=pu: e.tensor_copy(out=cst[1][:, :], in_=pu[:, 0:DV + 1]), reads=[pub], writes=[cst_b[1]])
                    else:
                        P.op("dve", lambda e, pu=pu: e.tensor_tensor(out=cst[1][:, :], in0=pu[:, 0:DV + 1], in1=cst[0][:, :], op=ALU.add),
                             reads=[pub, cst_b[0]], writes=[cst_b[1]])
                    P.op("act", lambda e, c=c, hh=hh: e.activation(out=cst[0][:, :], in_=cst[1][:, :], func=AF.Copy, scale=eaL[:, c, hh:hh + 1]),
                         reads=[cst_b[1], fac_b], writes=[cst_b[0]])
                    P.op("act", lambda e, c=c, hh=hh: e.activation(out=cst_bf[:, :], in_=cst[1][:, :], func=AF.Copy, scale=eaL[:, c, hh:hh + 1]),
                         reads=[cst_b[1], fac_b], writes=[cstbf_b])
            P.dma("sp", mixT_m[hh * DV:(hh + 1) * DV, :].rearrange("(a p) t -> p a t", p=128), os_t[:, :, :], reads=[os_b], is_output=True)

    if do_ssd:
        XC = SG * SR * SP_
        NCC = (XC + 2 * SG * SN) // 128
        sxT_in = C.dram_in("s_xT", [XC, S], BF16)
        sBT_in = C.dram_in("s_BT", [SG * SN, S], BF16)
        sCT_in = C.dram_in("s_CT", [SG * SN, S], BF16)

        def xbc_rows(cc):
            if cc < 8:
                return sxT_in[cc * 128:(cc + 1) * 128, :]
            if cc < 12:
                return sBT_in[(cc - 8) * 128:(cc - 7) * 128, :]
            return sCT_in[(cc - 12) * 128:(cc - 11) * 128, :]

        cw_in = C.dram_in("s_convw", [4, NCC * 128], F32)
        cbias_in = C.dram_in("s_convb", [NCC * 128], F32)
        z_in = C.dram_in("s_z", [S, XC], BF16)
        dt_in = C.dram_in("s_dt", [S, SHL], F32)
        dtb_in = C.dram_in("s_dtb", [SHL], F32)
        alog_in = C.dram_in("s_alog", [SHL], F32)
        dd_in = C.dram_in("s_d", [SHL], F32)
        snorm_in = C.dram_in("s_norm", [XC], F32)
        dtb, dtb_b = bcast_load(dtb_in, SHL, "dtb")
        aneg, aneg_b = bcast_load(alog_in, SHL, "aneg")
        dcol, dcol_b = bcast_load(dd_in, SHL, "dcol")
        snorm, snorm_b = bcast_load(snorm_in, XC, "snorm")
        P.op("act", lambda e: e.activation(out=aneg[:, :], in_=aneg[:, :], func=AF.Exp), reads=[aneg_b], writes=[aneg_b])
        P.op("dve", lambda e: e.tensor_scalar(out=aneg[:, :], in0=aneg[:, :], scalar1=-1.0, scalar2=None, op0=ALU.mult),
             reads=[aneg_b], writes=[aneg_b])
        cw = []
        for k in range(4):
            cw.append(load_colvec(C, cw_in[k, :], NCC, mm_ring))
        cbias, cbias_b = load_colvec(C, cbias_in, NCC, mm_ring)
        dtv = C.sb([128, NCH, SHL], F32, "dtv")
        dt_b = Buf()
        P.dma("sp", dtv[:, :, :], dt_in.rearrange("(c p) h -> p c h", p=128), writes=[dt_b])
        P.op("dve", lambda e: e.tensor_tensor(out=dtv[:, :, :], in0=dtv[:, :, :], in1=dtb[:, :].unsqueeze(1).to_broadcast([128, NCH, SHL]), op=ALU.add),
             reads=[dt_b, dtb_b], writes=[dt_b])
        softplus_inplace(dtv[:, :, :], dt_b, 1.0)
        da = C.sb([128, NCH, SHL], F32, "da")
        da_b = Buf()
        P.op("dve", lambda e: e.tensor_tensor(out=da[:, :, :], in0=dtv[:, :, :], in1=aneg[:, :].unsqueeze(1).to_broadcast([128, NCH, SHL]), op=ALU.mult),
             reads=[dt_b, aneg_b], writes=[da_b])
        da_hi = C.sb([128, NCH, SHL], BF16, "dahi")
        da_mid = C.sb([128, NCH, SHL], BF16, "damid")
        da_lo = C.sb([128, NCH, SHL], BF16, "dalo")
        da_r = C.sb([128, NCH, SHL], F32, "dar")
        negm_b = C.sb([128, 128], BF16, "negmb")
        P.op("dve", lambda e: e.tensor_copy(out=negm_b[:, :], in_=negm_f[:, :]), reads=[cb], writes=[cb])
        P.op("dve", lambda e: e.tensor_copy(out=da_hi[:, :, :], in_=da[:, :, :]), reads=[da_b], writes=[da_b])
        P.op("dve", lambda e: e.tensor_tensor(out=da_r[:, :, :], in0=da[:, :, :], in1=da_hi[:, :, :], op=ALU.subtract), reads=[da_b], writes=[da_b])
        P.op("dve", lambda e: e.tensor_copy(out=da_mid[:, :, :], in_=da_r[:, :, :]), reads=[da_b], writes=[da_b])
        P.op("dve", lambda e: e.tensor_tensor(out=da_r[:, :, :], in0=da_r[:, :, :], in1=da_mid[:, :, :], op=ALU.subtract), reads=[da_b], writes=[da_b])
        P.op("dve", lambda e: e.tensor_copy(out=da_lo[:, :, :], in_=da_r[:, :, :]), reads=[da_b], writes=[da_b])
        pcs, pcsb = big_ring.next()
        P.op("pe", lambda e: e.matmul(pcs[:, 0:NCH * SHL], lhsT=tri_f[:, :], rhs=da[:, :, :].rearrange("p c h -> p (c h)"), start=True, stop=True),
             reads=[cb, da_b], writes=[pcsb])
        ptot, ptotb = big_ring.next()
        P.op("pe", lambda e: e.matmul(ptot[:, 0:NCH * SHL], lhsT=ones_f[:, :], rhs=da[:, :, :].rearrange("p c h -> p (c h)"), start=True, stop=True),
             reads=[cb, da_b], writes=[ptotb])
        ncs = C.sb([128, NCH, SHL], F32, "ncs")
        ecs = C.sb([128, NCH, SHL], F32, "ecs")
        etot = C.sb([128, NCH, SHL], F32, "etot")
        dend = C.sb([128, NCH, SHL], F32, "dend")
        sc_b = Buf()
        fl = lambda t: t[:, :, :].rearrange("p c h -> p (c h)")
        P.op("dve", lambda e: e.tensor_scalar(out=fl(ncs), in0=pcs[:, 0:NCH * SHL], scalar1=-1.0, scalar2=None, op0=ALU.mult), reads=[pcsb], writes=[sc_b])
        P.op("act", lambda e: e.activation(out=fl(ecs), in_=pcs[:, 0:NCH * SHL], func=AF.Exp), reads=[pcsb, sc_b], writes=[sc_b])
        P.op("act", lambda e: e.activation(out=fl(etot), in_=ptot[:, 0:NCH * SHL], func=AF.Exp), reads=[ptotb, sc_b], writes=[sc_b])
        P.op("dve", lambda e: e.tensor_tensor(out=fl(dend), in0=ptot[:, 0:NCH * SHL], in1=fl(ncs), op=ALU.add), reads=[ptotb, sc_b], writes=[sc_b])
        P.op("act", lambda e: e.activation(out=fl(dend), in_=fl(dend), func=AF.Exp), reads=[sc_b], writes=[sc_b])

        raw = [C.sb([128, S + 4], BF16, "craw") for _ in range(2)]
        raw_b = [Buf(), Buf()]
        for i in range(2):
            P.op("pool", lambda e, i=i: e.memset(raw[i][:, 0:4], 0.0), writes=[raw_b[i]])
        cacc = [C.sb([128, S], F32, "cacc") for _ in range(2)]
        cacc_b = [Buf(), Buf()]
        cout = C.sb([128, S], BF16, "cout")
        cout_b = Buf()
        BT = C.sb([128, S], BF16, "BT")
        CT = C.sb([128, S], BF16, "CT")
        BT_b, CT_b = Buf(), Buf()
        x_tok = C.sb([128, NCH, SR * SP_], BF16, "xtok")
        xtok_b = Buf()
        B_tok = C.sb([128, NCH, SN], BF16, "Btok")
        Btok_b = Buf()
        z_t = C.sb([128, NCH, SR * SP_], BF16, "zt")
        z_b = Buf()
        state = [C.sb([128, SR * SP_], F32, "sst") for _ in range(2)]
        state_b = [Buf(), Buf()]
        state_bf = C.sb([128, SR * SP_], BF16, "sstbf")
        statebf_b = Buf()
        rbc = [C.sb([128, 128], F32, "rbc") for _ in range(2)]
        rbc_b = [Buf(), Buf()]
        Et = [C.sb([128, 128], F32, "Et") for _ in range(2)]
        Et_b = [Buf(), Buf()]
        MT = [C.sb([128, 128], BF16, "MT") for _ in range(2)]
        MT_b = [Buf(), Buf()]
        xd = [C.sb([128, SR * SP_], BF16, "xd") for _ in range(2)]
        xd_b = [Buf(), Buf()]
        xdd = [C.sb([128, SR * SP_], BF16, "xdd") for _ in range(2)]
        xdd_b = [Buf(), Buf()]
        yt = [C.sb([128, SR * SP_], F32, "yt") for _ in range(2)]
        yt_b = [Buf(), Buf()]
        y2 = [C.sb([128, SR * SP_], F32, "y2") for _ in range(2)]
        y2_b = [Buf(), Buf()]
        zs = [C.sb([128, SR * SP_], F32, "zs") for _ in range(2)]
        zs_b = [Buf(), Buf()]
        ssm = [C.sb([128, 8], F32, "ssm") for _ in range(2)]
        ssm_b = [Buf(), Buf()]
        yjunk = C.sb([128, SR * SP_], BF16, "yjunk")
        yjunk_b = Buf()
        yout = [C.sb([128, SR * SP_], BF16, "yout") for _ in range(2)]
        yout_b = [Buf(), Buf()]
        ci = 0
        ri = 0
        gi = 0
        zsall = C.sb([128, NCH, SR * SP_], F32, "zsall")
        zsall_b = Buf()

        def conv_chunk(cc, dst_ap, dst_b):
            nonlocal ci
            i2 = ci % 2
            ci += 1
            r_t, r_b = raw[i2], raw_b[i2]
            a_t, a_b = cacc[i2], cacc_b[i2]
            P.dma("sp", r_t[:, 4:4 + S], xbc_rows(cc), writes=[r_b])
            eng = "dve"
            P.op(eng, lambda e: e.tensor_scalar(out=a_t[:, :], in0=r_t[:, 1:1 + S], scalar1=cw[0][0][:, cc:cc + 1], scalar2=None, op0=ALU.mult),
                 reads=[r_b, cw[0][1]], writes=[a_b])
            for k in range(1, 4):
                P.op(eng, lambda e, k=k: e.scalar_tensor_tensor(out=a_t[:, :], in0=r_t[:, 1 + k:1 + k + S], scalar=cw[k][0][:, cc:cc + 1],
                                                                in1=a_t[:, :], op0=ALU.mult, op1=ALU.add),
                     reads=[r_b, cw[k][1], a_b], writes=[a_b])
            P.op("act", lambda e: e.activation(out=dst_ap, in_=a_t[:, :], func=AF.Silu, bias=cbias[:, cc:cc + 1]),
                 reads=[a_b, cbias_b], writes=[dst_b])

        for g in range(SG):
            h0 = g * SR
            conv_chunk(8 + g, BT[:, :], BT_b)
            conv_chunk(12 + g, CT[:, :], CT_b)
            for half in range(2):
                conv_chunk(2 * g + half, cout[:, :], cout_b)
                for c4 in range(0, NCH, 4):
                    tp, tpb = tp_ring.next()
                    for j in range(4):
                        c = c4 + j
                        P.op("pe", lambda e, tp=tp, j=j, c=c: e.transpose(tp[:, j * 128:(j + 1) * 128], cout[:, c * 128:(c + 1) * 128], C.ident_b[:, :]),
                             reads=[cout_b, C.ident_buf], writes=[tpb])
                    P.op("act", lambda e, tp=tp, c4=c4, half=half: e.activation(
                        out=x_tok[:, c4:c4 + 4, half * 128:(half + 1) * 128], in_=tp[:, :].rearrange("p (a b) -> p a b", a=4), func=AF.Copy),
                        reads=[tpb], writes=[xtok_b])
            for c4 in range(0, NCH, 4):
                tp, tpb = tp_ring.next()
                for j in range(4):
                    c = c4 + j
                    P.op("pe", lambda e, tp=tp, j=j, c=c: e.transpose(tp[:, j * 128:(j + 1) * 128], BT[:, c * 128:(c + 1) * 128], C.ident_b[:, :]),
                         reads=[BT_b, C.ident_buf], writes=[tpb])
                P.op("act", lambda e, tp=tp, c4=c4: e.activation(out=B_tok[:, c4:c4 + 4, :], in_=tp[:, :].rearrange("p (a b) -> p a b", a=4), func=AF.Copy),
                     reads=[tpb], writes=[Btok_b])
            P.dma("sp", z_t[:, :, :], z_in[:, g * SR * SP_:(g + 1) * SR * SP_].rearrange("(c p) d -> p c d", p=128), writes=[z_b])
            P.op("act", lambda e: e.activation(out=zsall[:, :, :].rearrange("p c d -> p (c d)"), in_=z_t[:, :, :].rearrange("p c d -> p (c d)"), func=AF.Silu),
                 reads=[z_b], writes=[zsall_b])
            os_t, os_b = ost[osti % 2], ost_b[osti % 2]
            osti += 1
            for c in range(NCH):
                csl = slice(c * 128, (c + 1) * 128)
                i2 = gi % 2
                gi += 1
                P.op("pool", lambda e, i2=i2, c=c, h0=h0: e.tensor_tensor(
                    out=xd[i2][:, :].rearrange("p (r d) -> p r d", r=SR), in0=x_tok[:, c, :].rearrange("p (r d) -> p r d", r=SR),
                    in1=dtv[:, c, h0:h0 + SR].unsqueeze(2).to_broadcast([128, SR, SP_]), op=ALU.mult),
                    reads=[xtok_b, dt_b], writes=[xd_b[i2]])
                P.op("pool", lambda e, i2=i2, c=c, h0=h0: e.tensor_tensor(
                    out=xdd[i2][:, :].rearrange("p (r d) -> p r d", r=SR), in0=xd[i2][:, :].rearrange("p (r d) -> p r d", r=SR),
                    in1=dend[:, c, h0:h0 + SR].unsqueeze(2).to_broadcast([128, SR, SP_]), op=ALU.mult),
                    reads=[xd_b[i2], sc_b], writes=[xdd_b[i2]])
                bkA, bkAb = mm_ring.tiles[2 * i2], mm_ring.bufs[2 * i2]
                bkB, bkBb = mm_ring.tiles[2 * i2 + 1], mm_ring.bufs[2 * i2 + 1]
                pcb, pcbb = bkA[:, 0:128], bkAb
                P.op("pe", lambda e, pcb=pcb, csl=csl: e.matmul(pcb, lhsT=BT[:, csl], rhs=CT[:, csl], start=True, stop=True),
                     reads=[BT_b, CT_b], writes=[pcbb])
                pyo, pyob = bkA[:, 128:128 + SR * SP_], bkAb
                if c > 0:
                    P.op("pe", lambda e, pyo=pyo, csl=csl: e.matmul(pyo, lhsT=CT[:, csl], rhs=state_bf[:, :], start=True, stop=True),
                         reads=[CT_b, statebf_b], writes=[pyob])
                pyd, pydb = bkB[:, 0:SR * SP_], bkBb
                for r in range(SR):
                    hh = h0 + r
                    j2 = ri % 2
                    ri += 1
                    pe_, peb = mm_ring.tiles[4 + j2], mm_ring.bufs[4 + j2]
                    for di, dpart in enumerate((da_hi, da_mid, da_lo)):
                        P.op("pe", lambda e, pe_=pe_, c=c, hh=hh, dpart=dpart, di=di: e.matmul(
                            pe_[:, 0:128], lhsT=dpart[:, c, hh:hh + 1].to_broadcast([128, 128]), rhs=tri_b[:, :],
                            start=(di == 0), stop=False), reads=[cb, da_b], writes=[peb])
                    P.op("pe", lambda e, pe_=pe_: e.matmul(pe_[:, 0:128], lhsT=C.ident_b[:, :], rhs=negm_b[:, :], start=False, stop=True),
                         reads=[cb, C.ident_buf], writes=[peb])
                    P.op("act", lambda e, pe_=pe_, j2=j2, c=c, hh=hh: e.activation(out=Et[j2][:, :], in_=pe_[:, 0:128], func=AF.Exp, bias=ncs[:, c, hh:hh + 1]),
                         reads=[peb, sc_b], writes=[Et_b[j2]])
                    P.op("dve", lambda e, j2=j2, pcb=pcb: e.tensor_tensor(out=MT[j2][:, :], in0=Et[j2][:, :], in1=pcb, op=ALU.mult),
                         reads=[Et_b[j2], pcbb], writes=[MT_b[j2]])
                    P.op("pe", lambda e, pyd=pyd, j2=j2, i2=i2, r=r: e.matmul(pyd[:, r * SP_:(r + 1) * SP_], lhsT=MT[j2][:, :], rhs=xd[i2][:, r * SP_:(r + 1) * SP_],
                                                                          start=True, stop=True), reads=[MT_b[j2], xd_b[i2]], writes=[pydb])
                pu, pub = bkB[:, SR * SP_:2 * SR * SP_], bkBb
                P.op("pe", lambda e, pu=pu, c=c, i2=i2: e.matmul(pu, lhsT=B_tok[:, c, :], rhs=xdd[i2][:, :], start=True, stop=True),
                     reads=[Btok_b, xdd_b[i2]], writes=[pub])
                y_t, y_b = yt[i2], yt_b[i2]
                P.op("dve", lambda e, y_t=y_t, c=c, h0=h0: e.tensor_tensor(
                    out=y_t[:, :].rearrange("p (r d) -> p r d", r=SR), in0=x_tok[:, c, :].rearrange("p (r d) -> p r d", r=SR),
                    in1=dcol[:, h0:h0 + SR].unsqueeze(2).to_broadcast([128, SR, SP_]), op=ALU.mult), reads=[xtok_b, dcol_b], writes=[y_b])
                if c > 0:
                    y2t, y2b = y2[i2], y2_b[i2]
                    P.op("dve", lambda e, y2t=y2t, pyo=pyo, c=c, h0=h0: e.tensor_tensor(
                        out=y2t[:, :].rearrange("p (r d) -> p r d", r=SR), in0=pyo.rearrange("p (r d) -> p r d", r=SR),
                        in1=ecs[:, c, h0:h0 + SR].unsqueeze(2).to_broadcast([128, SR, SP_]), op=ALU.mult), reads=[pyob, sc_b], writes=[y2b])
                    P.op("pool", lambda e, y_t=y_t, y2t=y2t: e.tensor_tensor(out=y_t[:, :], in0=y_t[:, :], in1=y2t[:, :], op=ALU.add),
                         reads=[y_b, y2b], writes=[y_b])
                P.op("dve", lambda e, y_t=y_t, pyd=pyd: e.tensor_tensor(out=y_t[:, :], in0=pyd, in1=y_t[:, :], op=ALU.add),
                     reads=[pydb, y_b], writes=[y_b])
                P.op("pool", lambda e, y_t=y_t, c=c: e.tensor_tensor(out=y_t[:, :], in0=y_t[:, :], in1=zsall[:, c, :], op=ALU.mult),
                     reads=[y_b, zsall_b], writes=[y_b])
                s_t, s_b = ssm[i2], ssm_b[i2]
                P.op("dve", lambda e, s_t=s_t: e.memset(s_t[:, 0:1], 0.0), writes=[s_b])
                P.op("act", lambda e, s_t=s_t, y_t=y_t: e.activation(out=yjunk[:, :], in_=y_t[:, :], func=AF.Square, accum_out=s_t[:, 0:1]),
                     reads=[y_b, s_b], writes=[s_b, yjunk_b])
                P.op("dve", lambda e, s_t=s_t: e.tensor_scalar(out=s_t[:, 1:2], in0=s_t[:, 0:1], scalar1=1.0 / (SR * SP_), scalar2=1e-6, op0=ALU.mult, op1=ALU.add),
                     reads=[s_b], writes=[s_b])
                P.op("act", lambda e, s_t=s_t: e.activation(out=s_t[:, 2:3], in_=s_t[:, 1:2], func=AF.Ln), reads=[s_b], writes=[s_b])
                P.op("act", lambda e, s_t=s_t: e.activation(out=s_t[:, 3:4], in_=s_t[:, 2:3], func=AF.Exp, scale=-0.5), reads=[s_b], writes=[s_b])
                yo, yob = yout[i2], yout_b[i2]
                P.op("dve", lambda e, yo=yo, y_t=y_t, s_t=s_t, g=g: e.scalar_tensor_tensor(
                    out=yo[:, :], in0=y_t[:, :], scalar=s_t[:, 3:4], in1=snorm[:, g * SR * SP_:(g + 1) * SR * SP_], op0=ALU.mult, op1=ALU.mult),
                    reads=[y_b, s_b, snorm_b], writes=[yob])
                tp2, tp2b = tp_ring.next()
                for j in range(2):
                    P.op("pe", lambda e, tp2=tp2, yo=yo, j=j: e.transpose(tp2[:, j * 128:(j + 1) * 128], yo[:, j * 128:(j + 1) * 128], C.ident_b[:, :]),
                         reads=[yob, C.ident_buf], writes=[tp2b])
                P.op("act", lambda e, tp2=tp2, os_t=os_t, csl=csl: e.activation(out=os_t[:, :, csl], in_=tp2[:, 0:256].rearrange("p (a b) -> p a b", a=2),
                                                                         func=AF.Copy), reads=[tp2b], writes=[os_b])
                if c < NCH - 1:
                    if c == 0:
                        P.op("dve", lambda e, pu=pu: e.tensor_copy(out=state[0][:, :], in_=pu), reads=[pub], writes=[state_b[0]])
                    else:
                        P.op("pool", lambda e, c=c, h0=h0: e.tensor_tensor(
                            out=state[1][:, :].rearrange("p (r d) -> p r d", r=SR), in0=state[0][:, :].rearrange("p (r d) -> p r d", r=SR),
                            in1=etot[:, c, h0:h0 + SR].unsqueeze(2).to_broadcast([128, SR, SP_]), op=ALU.mult), reads=[state_b[0], sc_b], writes=[state_b[1]])
                        P.op("dve", lambda e, pu=pu: e.tensor_tensor(out=state[0][:, :], in0=pu, in1=state[1][:, :], op=ALU.add),
                             reads=[pub, state_b[1]], writes=[state_b[0]])
                    P.op("act", lambda e: e.activation(out=state_bf[:, :], in_=state[0][:, :], func=AF.Copy), reads=[state_b[0]], writes=[statebf_b])
            P.dma("sp", mixT_s[g * SR * SP_:(g + 1) * SR * SP_, :].rearrange("(a p) t -> p a t", p=128), os_t[:, :, :],
                  reads=[os_b], is_output=True)
    return


C_Q, C_K, C_V, C_O, C_IG, C_FG, C_Z, C_XBC, C_DT = 0, 1024, 2048, 4096, 6144, 6152, 6160, 8208, 12304


def _xbc_cols(g):
    return np.concatenate([np.arange(g * 1024, (g + 1) * 1024), 2048 + np.arange(g * 512, (g + 1) * 512),
                           3072 + np.arange(g * 512, (g + 1) * 512)])


def mixer_params(inp, g):
    gb = np.asarray(inp["l0_mlstm_gate_bias"], np.float32)
    xc = _xbc_cols(g)
    return {
        "m_gbias": np.ascontiguousarray(np.concatenate([gb[4 * g:4 * g + 4], gb[8 + 4 * g:8 + 4 * g + 4]])),
        "m_norm": np.ascontiguousarray(np.asarray(inp["l0_mlstm_norm"], np.float32)[g * 1024:(g + 1) * 1024]),
        "s_convw": np.ascontiguousarray(np.asarray(inp["l0_ssd_conv_w"], np.float32)[:, xc]),
        "s_convb": np.ascontiguousarray(np.asarray(inp["l0_ssd_conv_b"], np.float32)[xc]),
        "s_dtb": np.ascontiguousarray(np.asarray(inp["l0_ssd_dt_bias"], np.float32)[16 * g:16 * g + 16]),
        "s_alog": np.ascontiguousarray(np.asarray(inp["l0_ssd_a_log"], np.float32)[16 * g:16 * g + 16]),
        "s_d": np.ascontiguousarray(np.asarray(inp["l0_ssd_d"], np.float32)[16 * g:16 * g + 16]),
        "s_norm": np.ascontiguousarray(np.asarray(inp["l0_ssd_norm"], np.float32)[g * 1024:(g + 1) * 1024]),
    }


def mixer_inputs(proj, inp, g):
    pb = proj.astype(NPBF)
    d = mixer_params(inp, g)
    d.update({
        "m_qT": np.ascontiguousarray(pb[:, C_Q + g * 512:C_Q + (g + 1) * 512].T),
        "m_kT": np.ascontiguousarray(pb[:, C_K + g * 512:C_K + (g + 1) * 512].T),
        "m_v": np.ascontiguousarray(pb[:, C_V + g * 1024:C_V + (g + 1) * 1024]),
        "m_o": np.ascontiguousarray(pb[:, C_O + g * 1024:C_O + (g + 1) * 1024]),
        "m_ig": np.ascontiguousarray(proj[:, C_IG + 4 * g:C_IG + 4 * g + 4]),
        "m_fg": np.ascontiguousarray(proj[:, C_FG + 4 * g:C_FG + 4 * g + 4]),
        "s_xT": np.ascontiguousarray(pb[:, C_XBC + g * 1024:C_XBC + (g + 1) * 1024].T),
        "s_BT": np.ascontiguousarray(pb[:, C_XBC + 2048 + g * 512:C_XBC + 2048 + (g + 1) * 512].T),
        "s_CT": np.ascontiguousarray(pb[:, C_XBC + 3072 + g * 512:C_XBC + 3072 + (g + 1) * 512].T),
        "s_z": np.ascontiguousarray(pb[:, C_Z + g * 1024:C_Z + (g + 1) * 1024]),
        "s_dt": np.ascontiguousarray(proj[:, C_DT + 16 * g:C_DT + 16 * g + 16]),
    })
    return d


def fused_program():
    C = Ctx()
    S = S_LEN
    x_in = C.nc.dram_tensor("x", [S, DM], F32, kind="ExternalInput").ap()
    y_out = C.nc.dram_tensor("y", [S, DM], F32, kind="ExternalOutput").ap()
    sc = C.scratch
    qT0, kT0 = sc("qT0", [1024, S], BF16), sc("kT0", [1024, S], BF16)
    v0, o0, z0 = sc("v0", [S, 2048], BF16), sc("o0", [S, 2048], BF16), sc("z0", [S, 2048], BF16)
    gates0, dt0 = sc("gates0", [S, 16], F32), sc("dt0", [S, 32], F32)
    xbcT0 = sc("xbcT0", [4096, S], BF16)
    mixT = sc("mixT", [4096, S], BF16)
    x1 = sc("x1", [S, DM], F32)
    qT1, kT1, oT1 = sc("qT1", [2048, S], BF16), sc("kT1", [2048, S], BF16), sc("oT1", [2048, S], BF16)
    v1 = sc("v1", [S, 2048], BF16)
    make_consts(C)
    hs = lambda h: slice(h * NT, (h + 1) * NT)
    for h in range(2):
        C.phase_begin(bind={"x": x_in[hs(h), :], "qT": qT0[:, hs(h)], "kT": kT0[:, hs(h)], "v": v0[hs(h), :], "o": o0[hs(h), :],
                            "gates": gates0[hs(h), :], "z": z0[hs(h), :], "xbcT": xbcT0[:, hs(h)], "dt": dt0[hs(h), :]}, prefix="l0_")
        dense_phase(C, dict(proj=[("qT", C_Q, 1024, "T", BF16), ("kT", C_K, 1024, "T", BF16), ("v", C_V, 2048, "N", BF16),
                                  ("o", C_O, 2048, "N", BF16), ("gates", C_IG, 16, "N", F32), ("z", C_Z, 2048, "N", BF16),
                                  ("xbcT", C_XBC, 4096, "T", BF16), ("dt", C_DT, 32, "N", F32)], wcols=12336))
        C.phase_end()
    for g in range(2):
        C.phase_begin(bind={"m_qT": qT0[g * 512:(g + 1) * 512, :], "m_kT": kT0[g * 512:(g + 1) * 512, :],
                            "m_v": v0[:, g * 1024:(g + 1) * 1024], "m_o": o0[:, g * 1024:(g + 1) * 1024],
                            "m_ig": gates0[:, 4 * g:4 * g + 4], "m_fg": gates0[:, 8 + 4 * g:8 + 4 * g + 4],
                            "s_xT": xbcT0[g * 1024:(g + 1) * 1024, :], "s_BT": xbcT0[2048 + g * 512:2048 + (g + 1) * 512, :],
                            "s_CT": xbcT0[3072 + g * 512:3072 + (g + 1) * 512, :], "s_z": z0[:, g * 1024:(g + 1) * 1024],
                            "s_dt": dt0[:, 16 * g:16 * g + 16],
                            "mixT_m": mixT[g * 1024:(g + 1) * 1024, :], "mixT_s": mixT[2048 + g * 1024:2048 + (g + 1) * 1024, :]},
                      prefix="g%d_" % g)
        mixer_phase(C)
        C.phase_end()
    for h in range(2):
        C.phase_begin(bind={"x": x_in[hs(h), :], "mixT": mixT[:, hs(h)], "x_out": x1[hs(h), :],
                            "qT": qT1[:, hs(h)], "kT": kT1[:, hs(h)], "v": v1[hs(h), :]}, prefix="l0_")
        dense_phase(C, dict(mix=4096, ffn=True, x_out=True, proj_names=("g_qkv", "w_qkv"),
                            proj=[("qT", 0, 2048, "T", BF16), ("kT", 2048, 2048, "T", BF16), ("v", 4096, 2048, "N", BF16)], wcols=6144))
        C.phase_end()
    C.phase_begin(bind={"qT": qT1, "kT": kT1, "v": v1, "oT": oT1})
    attn_phase(C, 16)
    C.phase_end()
    for h in range(2):
        C.phase_begin(bind={"x": x1[hs(h), :], "mixT": oT1[:, hs(h)], "y": y_out[hs(h), :]}, prefix="l1_")
        dense_phase(C, dict(mix=2048, ffn=True, final=True))
        C.phase_end()
    return C.finish()


_NC = None


def kernel(**inp):
    global _NC
    f32 = lambda a: np.ascontiguousarray(np.asarray(a, dtype=np.float32))
    if _NC is None:
        _NC = fused_program()
    x = f32(inp["x"])
    shared = {
        "l0_g_proj": f32(inp["l0_norm_mix"]), "l0_w_proj": f32(inp["l0_w_in"]),
        "l0_w_mix": f32(inp["l0_w_out"]), "l0_g_ffn": f32(inp["l0_norm_ffn"]), "l0_w_gate": f32(inp["l0_ffn_gate"]),
        "l0_w_up": f32(inp["l0_ffn_up"]), "l0_w_down": f32(inp["l0_ffn_down"]),
        "l0_g_qkv": f32(inp["l1_norm_mix"]), "l0_w_qkv": f32(inp["l1_w_qkv"]),
        "l1_w_mix": f32(inp["l1_w_o"]), "l1_g_ffn": f32(inp["l1_norm_ffn"]), "l1_w_gate": f32(inp["l1_ffn_gate"]),
        "l1_w_up": f32(inp["l1_ffn_up"]), "l1_w_down": f32(inp["l1_ffn_down"]), "l1_g_fin": f32(inp["final_norm"]),
    }
    for g in range(2):
        for k, v in mixer_params(inp, g).items():
            shared["g%d_%s" % (g, k)] = v
    owner = [0, 1, 4, 5]
    zero_x = np.zeros_like(x[0])
    in_maps = []
    for c in range(8):
        d = dict(shared)
        d["x"] = x[owner.index(c)] if c in owner else zero_x
        in_maps.append(d)
    res = run_bass_kernel_spmd(_NC, in_maps, core_ids=list(range(8)))
    out = np.stack([res.results[c]["y"] for c in owner], axis=0)
    return np.ascontiguousarray(out.astype(np.float32))
```
